# Optimizing a Trainium2 kernel written in Bass

```python
import jax, jax.numpy as jnp
from jax import lax
import numpy as np

D_MODEL = 1024
BATCH = 16
SEQ = 4096
DEPTH = 2
DEC_BATCH = 32
DEC_SEQ = 2048
PAST_LEN = 128

D_MIX = D_MODEL
D_RG = D_MIX // 2
D_ML = D_MIX - D_RG
RG_BLOCKS = 8
RG_BW = D_RG // RG_BLOCKS
RG_C = 8.0
RG_CONV = 4
ML_HEADS = 4
ML_HD = D_ML // ML_HEADS
ML_CHUNK = 128
D_FF = 3 * D_MODEL
FFN_CONV = 3
N_GATE = 4 * ML_HEADS
N_IN = 2 * D_RG + 4 * D_ML + N_GATE
SPLITS = [D_RG, 2 * D_RG, 2 * D_RG + D_ML, 2 * D_RG + 2 * D_ML, 2 * D_RG + 3 * D_ML, 2 * D_RG + 4 * D_ML]
EPS = 1e-6

kernel_name = "hybrid_rglru_mlstm_bidir_encoder"


def rmsnorm(x, g):
    x32 = x.astype(jnp.float32)
    y = x32 * lax.rsqrt(jnp.mean(x32 * x32, axis=-1, keepdims=True) + EPS)
    return (y * g.astype(jnp.float32)).astype(x.dtype)


def dwconv(x, w, b, pad_l, pad_r):
    S = x.shape[1]
    xp = jnp.pad(x, ((0, 0), (pad_l, pad_r), (0, 0)))
    y = b + xp[:, 0:S] * w[0]
    for j in range(1, w.shape[0]):
        y = y + xp[:, j:j + S] * w[j]
    return y


def _lin_combine(e1, e2):
    a1, u1 = e1
    a2, u2 = e2
    return a1 * a2, a2 * u1 + u2


def rglru(xc, wa, ba, wx, bx, lam, reverse):
    B, S, _ = xc.shape
    xb = xc.reshape(B, S, RG_BLOCKS, RG_BW)
    r = jax.nn.sigmoid(jnp.einsum('bsnc,ncd->bsnd', xb, wa).reshape(B, S, D_RG) + ba)
    i = jax.nn.sigmoid(jnp.einsum('bsnc,ncd->bsnd', xb, wx).reshape(B, S, D_RG) + bx)
    log_a = -RG_C * jax.nn.softplus(-lam) * r
    a = jnp.exp(log_a)
    u = jnp.sqrt(-jnp.expm1(2.0 * log_a)) * (i * xc)
    _, h = lax.associative_scan(_lin_combine, (a, u), axis=1, reverse=reverse)
    return h


def mlstm_chunkwise(q, k, v, li, lf):
    B, S, H, dh = q.shape
    L = ML_CHUNK
    NC = S // L

    def vec_chunks(t):
        return t.reshape(B, NC, L, H, dh).transpose(1, 0, 3, 2, 4)

    def gate_chunks(t):
        return t.reshape(B, NC, L, H).transpose(1, 0, 3, 2)

    lower = jnp.tril(jnp.ones((L, L), dtype=bool))

    def step(carry, xs):
        C, n, m = carry
        qc, kc, vc, ic, fc = xs
        b = jnp.cumsum(fc, axis=-1)
        D = jnp.where(lower, b[..., :, None] - b[..., None, :] + ic[..., None, :], -jnp.inf)
        inter = b + m[..., None]
        m_t = jnp.maximum(inter, jnp.max(D, axis=-1))
        w_inter = jnp.exp(inter - m_t)
        s = jnp.einsum('bhtd,bhsd->bhts', qc, kc) * jnp.exp(D - m_t[..., None])
        num = jnp.einsum('bhts,bhsd->bhtd', s, vc) + w_inter[..., None] * jnp.einsum('bhtk,bhkv->bhtv', qc, C)
        den = jnp.sum(s, axis=-1) + w_inter * jnp.einsum('bhtk,bhk->bht', qc, n)
        h = num / jnp.maximum(jnp.abs(den), jnp.exp(-m_t))[..., None]
        b_last = b[..., -1]
        g = b_last[..., None] - b + ic
        m_new = jnp.maximum(b_last + m, jnp.max(g, axis=-1))
        decay = jnp.exp(b_last + m - m_new)
        wg = jnp.exp(g - m_new[..., None])
        C_new = decay[..., None, None] * C + jnp.einsum('bhs,bhsk,bhsv->bhkv', wg, kc, vc)
        n_new = decay[..., None] * n + jnp.einsum('bhs,bhsk->bhk', wg, kc)
        return (C_new, n_new, m_new), h

    init = (jnp.zeros((B, H, dh, dh), jnp.float32), jnp.zeros((B, H, dh), jnp.float32),
            jnp.zeros((B, H), jnp.float32))
    _, h = lax.scan(step, init, (vec_chunks(q), vec_chunks(k), vec_chunks(v), gate_chunks(li), gate_chunks(lf)))
    return h.transpose(1, 0, 3, 2, 4).reshape(B, S, H, dh)


def encoder_layer(x, norm1_g, w_in, b_gates, rg_conv_w, rg_conv_b, rg_wa, rg_ba, rg_wx, rg_bx,
                  rg_lambda, ml_norm_g, w_out, norm2_g, w_up, ffn_conv_w, ffn_conv_b, w_down):
    f32 = jnp.float32
    B, S, _ = x.shape
    h = rmsnorm(x, norm1_g)
    p = jnp.matmul(h, w_in)
    rg_x, rg_gate, q, k, v, o, gates = jnp.split(p, SPLITS, axis=-1)

    xc = dwconv(rg_x.astype(f32), rg_conv_w.astype(f32), rg_conv_b.astype(f32),
                RG_CONV // 2, RG_CONV - 1 - RG_CONV // 2)
    wa, ba, wx, bx, lam = (t.astype(f32) for t in (rg_wa, rg_ba, rg_wx, rg_bx, rg_lambda))
    h_rg = (rglru(xc, wa[0], ba[0], wx[0], bx[0], lam[0], False)
            + rglru(xc, wa[1], ba[1], wx[1], bx[1], lam[1], True))
    y_rg = h_rg * jax.nn.gelu(rg_gate.astype(f32))

    gates = gates.astype(f32).reshape(B, S, 4, ML_HEADS) + b_gates.astype(f32)
    qh = q.astype(f32).reshape(B, S, ML_HEADS, ML_HD) * (ML_HD ** -0.5)
    kh = k.astype(f32).reshape(B, S, ML_HEADS, ML_HD)
    vh = v.astype(f32).reshape(B, S, ML_HEADS, ML_HD)
    h_f = mlstm_chunkwise(qh, kh, vh, gates[:, :, 0], jax.nn.log_sigmoid(gates[:, :, 1]))
    flip = lambda t: jnp.flip(t, axis=1)
    h_b = flip(mlstm_chunkwise(flip(qh), flip(kh), flip(vh), flip(gates[:, :, 2]),
                               flip(jax.nn.log_sigmoid(gates[:, :, 3]))))
    h_ml = h_f + h_b
    h_ml = h_ml * lax.rsqrt(jnp.mean(h_ml * h_ml, axis=-1, keepdims=True) + EPS)
    y_ml = jax.nn.sigmoid(o.astype(f32)) * (h_ml.reshape(B, S, D_ML) * ml_norm_g.astype(f32))

    mix = jnp.concatenate([y_rg, y_ml], axis=-1).astype(x.dtype)
    x = x + jnp.matmul(mix, w_out)

    h = rmsnorm(x, norm2_g)
    uv = dwconv(jnp.matmul(h, w_up), ffn_conv_w, ffn_conv_b, FFN_CONV // 2, FFN_CONV // 2)
    gate, val = jnp.split(uv, 2, axis=-1)
    x = x + jnp.matmul(jax.nn.gelu(gate) * val, w_down)
    return x


def encoder(x, norm1_g, w_in, b_gates, rg_conv_w, rg_conv_b, rg_wa, rg_ba, rg_wx, rg_bx, rg_lambda,
            ml_norm_g, w_out, norm2_g, w_up, ffn_conv_w, ffn_conv_b, w_down, final_g):
    for l in range(DEPTH):
        x = encoder_layer(x, norm1_g[l], w_in[l], b_gates[l], rg_conv_w[l], rg_conv_b[l], rg_wa[l],
                          rg_ba[l], rg_wx[l], rg_bx[l], rg_lambda[l], ml_norm_g[l], w_out[l],
                          norm2_g[l], w_up[l], ffn_conv_w[l], ffn_conv_b[l], w_down[l])
    return rmsnorm(x, final_g)


def setup_inputs(seed: int = 0) -> dict:
    key = jax.random.key(seed)
    ks = jax.random.split(key, 24)
    nrm = jax.random.normal
    f_base = jnp.linspace(3.0, 6.0, ML_HEADS)
    u = jax.random.uniform(ks[11], (DEPTH, 2, D_RG), minval=0.9, maxval=0.999)
    pa = u ** (1.0 / RG_C)
    return {
        "x_prompt": nrm(ks[0], (BATCH, SEQ, D_MODEL), jnp.float32),
        "x_sample": nrm(ks[1], (DEC_BATCH, DEC_SEQ, D_MODEL), jnp.float32),
        "norm1_g": 1.0 + 0.02 * nrm(ks[2], (DEPTH, D_MODEL)),
        "w_in": nrm(ks[3], (DEPTH, D_MODEL, N_IN)) * D_MODEL ** -0.5,
        "b_gates": 0.1 * nrm(ks[4], (DEPTH, 4, ML_HEADS)) + jnp.array([0.0, 1.0, 0.0, 1.0])[:, None] * f_base[None, :],
        "rg_conv_w": nrm(ks[5], (DEPTH, RG_CONV, D_RG)) * RG_CONV ** -0.5,
        "rg_conv_b": 0.01 * nrm(ks[6], (DEPTH, D_RG)),
        "rg_wa": nrm(ks[7], (DEPTH, 2, RG_BLOCKS, RG_BW, RG_BW)) * RG_BW ** -0.5,
        "rg_ba": 0.01 * nrm(ks[8], (DEPTH, 2, D_RG)),
        "rg_wx": nrm(ks[9], (DEPTH, 2, RG_BLOCKS, RG_BW, RG_BW)) * RG_BW ** -0.5,
        "rg_bx": 0.01 * nrm(ks[10], (DEPTH, 2, D_RG)),
        "rg_lambda": jnp.log(pa) - jnp.log1p(-pa),
        "ml_norm_g": 1.0 + 0.02 * nrm(ks[12], (DEPTH, D_ML)),
        "w_out": nrm(ks[13], (DEPTH, D_MIX, D_MODEL)) * D_MIX ** -0.5,
        "norm2_g": 1.0 + 0.02 * nrm(ks[14], (DEPTH, D_MODEL)),
        "w_up": nrm(ks[15], (DEPTH, D_MODEL, 2 * D_FF)) * D_MODEL ** -0.5,
        "ffn_conv_w": nrm(ks[16], (DEPTH, FFN_CONV, 2 * D_FF)) * FFN_CONV ** -0.5,
        "ffn_conv_b": 0.01 * nrm(ks[17], (DEPTH, 2 * D_FF)),
        "w_down": nrm(ks[18], (DEPTH, D_FF, D_MODEL)) * D_FF ** -0.5,
        "final_g": 1.0 + 0.02 * nrm(ks[19], (D_MODEL,)),
    }


def reference(x_prompt, x_sample, norm1_g, w_in, b_gates, rg_conv_w, rg_conv_b, rg_wa, rg_ba, rg_wx,
              rg_bx, rg_lambda, ml_norm_g, w_out, norm2_g, w_up, ffn_conv_w, ffn_conv_b, w_down, final_g):
    y_prompt = encoder(x_prompt, norm1_g, w_in, b_gates, rg_conv_w, rg_conv_b, rg_wa, rg_ba, rg_wx, rg_bx,
                       rg_lambda, ml_norm_g, w_out, norm2_g, w_up, ffn_conv_w, ffn_conv_b, w_down, final_g)
    y_sample = encoder(x_sample, norm1_g, w_in, b_gates, rg_conv_w, rg_conv_b, rg_wa, rg_ba, rg_wx, rg_bx,
                       rg_lambda, ml_norm_g, w_out, norm2_g, w_up, ffn_conv_w, ffn_conv_b, w_down, final_g)
    return (y_prompt, y_sample)
```

```python
import numpy as np
from contextlib import ExitStack
import concourse.bass as bass
import concourse.mybir as mybir
from concourse.bass_utils import run_bass_kernel_spmd

F32 = mybir.dt.float32
BF16 = mybir.dt.bfloat16
AF = mybir.ActivationFunctionType
ALU = mybir.AluOpType
AX = mybir.AxisListType

D = 1024
KT = 8
NIN = 3088
DFF = 3072
EPS = 1e-6
NLAYER = 2
QSCALE = 128 ** -0.5


class Buf:
    __slots__ = ("w", "r")

    def __init__(self):
        self.w = None
        self.r = {}


def toks(*shape):
    a = np.empty(shape, dtype=object)
    for idx in np.ndindex(*shape):
        a[idx] = Buf()
    return a


def flat(*xs):
    out = []
    for x in xs:
        if isinstance(x, Buf):
            out.append(x)
        elif isinstance(x, np.ndarray):
            out.extend(x.ravel().tolist())
        else:
            for y in x:
                out.extend(flat(y))
    return out


class Key:
    def __init__(self, sem, eng=None, name=""):
        self.sem = sem
        self.eng = eng
        self.val = 0
        self.waited = {}
        self.name = name


class Sched:
    def __init__(self, nc, es, nslots=24, ncslots=8, needed=None):
        self.nc = nc
        self.needed = needed
        self.rec = {}
        self.E = {}
        for nm in ("sync", "scalar", "vector", "gpsimd", "tensor"):
            self.E[nm] = Key(es.enter_context(nc.semaphore("s_" + nm)), getattr(nc, nm), nm)
            self.E[nm].ordn = 0
            self.E[nm].omap = {}
            self.rec[nm] = set()
        self.slots = [Key(es.enter_context(nc.semaphore("d%d" % i)), name="d%d" % i) for i in range(nslots)]
        self.cslots = [Key(es.enter_context(nc.semaphore("c%d" % i)), name="c%d" % i) for i in range(ncslots)]
        self.rr = 0
        self.crr = 0
        self.n_ins = 0

    def _wait(self, E, key, val, raw):
        if val <= 0:
            return
        if key is E and E.name == "tensor":
            return
        if E.waited.get(key, 0) >= val:
            return
        E.eng.wait_ge(key.sem, val)
        E.waited[key] = val
        if key.eng is not None:
            self.rec[key.name].add(key.omap.get(val, -1))

    def _deps(self, E, reads, writes):
        for b in reads:
            if b.w is not None:
                self._wait(E, b.w[0], b.w[1], True)
        for b in writes:
            if b.w is not None:
                self._wait(E, b.w[0], b.w[1], False)
            for k, v in b.r.items():
                self._wait(E, k, v, False)

    def op(self, en, fn, reads=(), writes=(), inc=True):
        E = self.E[en]
        reads = flat(reads)
        writes = flat(writes)
        self._deps(E, reads, writes)
        ins = fn(E.eng)
        self.n_ins += 1
        stamp = E.val + 1
        if inc:
            E.ordn += 1
            E.omap.setdefault(stamp, E.ordn)
            if self.needed is None or E.ordn in self.needed[en]:
                ins.then_inc(E.sem, 1)
                E.val += 1
                E.omap[stamp] = E.ordn
                self.n_inc = getattr(self, "n_inc", 0) + 1
        for b in reads:
            if b.r.get(E, 0) < stamp:
                b.r[E] = stamp
        for b in writes:
            b.w = (E, stamp)
            b.r = {}
        return ins

    def dma(self, out, in_, reads=(), writes=(), q="sync", cast=False, slow=False):
        Q = self.E[q]
        if cast:
            slot = self.cslots[self.crr % len(self.cslots)]
            self.crr += 1
        else:
            slot = self.slots[self.rr % len(self.slots)]
            self.rr += 1
        reads = flat(reads)
        writes = flat(writes)
        self._wait(Q, slot, slot.val, True)
        self._deps(Q, reads, writes)
        ins = Q.eng.dma_start(out=out, in_=in_, allow_slow_non_contiguous=True) if slow else Q.eng.dma_start(out=out, in_=in_)
        self.n_ins += 1
        slot.val += 16
        ins.then_inc(slot.sem, 16)
        for b in reads:
            b.r[slot] = slot.val
        for b in writes:
            b.w = (slot, slot.val)
            b.r = {}
        return ins

    def barrier(self, include_cast=False):
        keys = list(self.E.values()) + self.slots + (self.cslots if include_cast else [])
        for E in self.E.values():
            for k in keys:
                if k is E:
                    continue
                self._wait(E, k, k.val, True)


def seq_tiles(S, maxlen):
    n = -(-S // maxlen)
    base = -(-S // n)
    base = -(-base // 8) * 8
    out = []
    t = 0
    while t < S:
        ln = min(base, S - t)
        out.append((t, ln))
        t += ln
    return out


def _build(groups, debug=False, needed=None):
    nc = bass.Bass("TRN2", target_bir_lowering=False)
    SM = max(S for _, _, S in groups)
    Ssizes = sorted(set(S for _, _, S in groups))
    es = ExitStack()
    K = Sched(nc, es, needed=needed)
    uid = [0]

    def din(name, shape, dt=F32):
        return nc.dram_tensor(name, list(shape), dt, kind="ExternalInput").ap()

    def dscr(name, shape, dt):
        kind = "ExternalOutput" if debug else "Internal"
        return nc.dram_tensor(name, list(shape), dt, kind=kind).ap()

    def sb(stack, name, shape, dt):
        uid[0] += 1
        return stack.enter_context(nc.sbuf_tensor("%s_%d" % (name, uid[0]), list(shape), dt))

    xin = {}
    yout = {}
    for name, n, S in groups:
        xin[name] = din("x_" + name, (n, D, S))
        yout[name] = nc.dram_tensor("y_" + name, [n, D, S], F32, kind="ExternalOutput").ap()
    norm1_g = din("norm1_g", (NLAYER, 128, 8))
    w_in = din("w_in", (NLAYER, D, NIN))
    b_gates = din("b_gates", (NLAYER, 16, 1))
    rg_conv_w = din("rg_conv_w", (NLAYER, 128, 4, 4))
    rg_conv_b = din("rg_conv_b", (NLAYER, 128, 4))
    rg_wa = din("rg_wa", (NLAYER, 2, 8, 64, 64))
    rg_ba = din("rg_ba", (128, 16))
    rg_wx = din("rg_wx", (NLAYER, 2, 8, 64, 64))
    rg_bx = din("rg_bx", (128, 16))
    rg_lambda = din("rg_lambda", (128, 16))
    ml_norm_g = din("ml_norm_g", (NLAYER, 512))
    w_out = din("w_out", (NLAYER, D, D))
    norm2_g = din("norm2_g", (NLAYER, 128, 8))
    w_up = din("w_up", (NLAYER, D, 2 * DFF))
    ffn_conv_w = din("ffn_conv_w", (NLAYER, 128, 3, 48))
    ffn_conv_b = din("ffn_conv_b", (NLAYER, 128, 48))
    w_down = din("w_down", (NLAYER, DFF, D))
    final_g = din("final_g", (128, 8))
    c_ident = din("c_ident", (128, 128))
    c_maskf = din("c_maskf", (128, 128))
    c_maskb = din("c_maskb", (128, 128))

    w_in_bf = dscr("w_in_bf", (NLAYER, D, NIN), BF16)
    w_out_bf = dscr("w_out_bf", (NLAYER, D, D), BF16)
    w_up_bf = dscr("w_up_bf", (NLAYER, D, 2 * DFF), BF16)
    w_down_bf = dscr("w_down_bf", (NLAYER, 8, 128, 24, 128), BF16)
    wrg_bf = dscr("wrg_bf", (NLAYER, 2, 2, 4, 128, 128), BF16)
    tok_wbf = {(w, l): Buf() for w in ("in", "out", "up", "down", "rg") for l in range(NLAYER)}
    seqs = []
    tb = 0
    for name, n, S in groups:
        for si in range(n):
            seqs.append(dict(name=name, si=si, S=S, base=tb, pbase=tb + 3 * len(seqs), idx=len(seqs)))
            tb += S
    TOT = tb
    rgx_all = dscr("rgx_pad", (512, TOT + 3 * len(seqs)), BF16)
    gg_all = dscr("gg_d", (512, TOT), BF16)
    qT_all = dscr("qT_d", (512, TOT), BF16)
    kT_all = dscr("kT_d", (512, TOT), BF16)
    va_all = dscr("va_d", (TOT, 516), BF16)
    og_all = dscr("og_d", (TOT, 512), BF16)
    ktm_all = dscr("ktm_d", (TOT, 512), BF16)
    mixT_all = dscr("mixT_d", (D, TOT), BF16)
    xmid_all = dscr("xmid_d", (D, TOT), F32)
    xc_d = dscr("xc_d", (512, SM), F32)
    hbrg_d = dscr("hbrg_d", (512, SM), F32)
    hbml_d = dscr("hbml_d", (SM, 512), F32)
    for q in seqs:
        b0, S = q["base"], q["S"]
        q["rgx_pad"] = rgx_all[:, q["pbase"]:q["pbase"] + S + 3]
        q["gg_d"] = gg_all[:, b0:b0 + S]
        q["qT_d"] = qT_all[:, b0:b0 + S]
        q["kT_d"] = kT_all[:, b0:b0 + S]
        q["va_d"] = va_all[b0:b0 + S, :]
        q["og_d"] = og_all[b0:b0 + S, :]
        q["ktm_d"] = ktm_all[b0:b0 + S, :]
        q["mixT_d"] = mixT_all[:, b0:b0 + S]
        q["xmid_d"] = xmid_all[:, b0:b0 + S]
        q["grow"] = dscr("grow_%d" % q["idx"], (16, S), F32)
        q["xin"] = xin[q["name"]][q["si"]]
        q["yout"] = yout[q["name"]][q["si"]]

    ps = es.enter_context(nc.psum_tensor("ps", [128, 8, 512], F32))
    bank_tok = toks(8)
    bank_rr = [0]

    def bank():
        b = bank_rr[0] % 8
        bank_rr[0] += 1
        return b

    def bank2():
        if bank_rr[0] % 2:
            bank_rr[0] += 1
        b = bank_rr[0] % 8
        bank_rr[0] += 2
        return [b, b + 1]

    def A(fn, r=(), w=()):
        return K.op("scalar", fn, r, w)

    def V(fn, r=(), w=()):
        return K.op("vector", fn, r, w)

    def G(fn, r=(), w=()):
        return K.op("gpsimd", fn, r, w)

    def PE(fn, r=(), w=(), inc=True):
        return K.op("tensor", fn, r, w, inc)

    def mmgroup(out_ap, pairs, rtoks, wtok):
        n = len(pairs)
        for i, (l_, r_) in enumerate(pairs):
            PE(lambda e, l_=l_, r_=r_, i=i: e.matmul(out_ap, lhsT=l_, rhs=r_, start=(i == 0), stop=(i == n - 1)),
               r=rtoks[i] if isinstance(rtoks, list) else rtoks, w=[wtok], inc=(i == n - 1))

    cst = ExitStack()
    es.enter_context(cst)
    ident_bf = sb(cst, "ident_bf", [128, 128], BF16)
    ident_f = sb(cst, "ident_f", [128, 128], F32)
    ones_bf = sb(cst, "ones_bf", [128, 128], BF16)
    ones_f = sb(cst, "ones_f", [128, 128], F32)
    maskf = sb(cst, "maskf", [128, 128], BF16)
    maskb = sb(cst, "maskb", [128, 128], BF16)
    neghalf = sb(cst, "neghalf", [128, 512], F32)
    zer = sb(cst, "zer", [128, 2048], BF16)
    g1c = sb(cst, "g1c", [128, NLAYER, 8], F32)
    g2c = sb(cst, "g2c", [128, NLAYER, 8], F32)
    gfc = sb(cst, "gfc", [128, 8], F32)
    rgcw = sb(cst, "rgcw", [128, NLAYER, 4, 4], F32)
    rgcb = sb(cst, "rgcb", [128, NLAYER, 4], F32)
    hba = sb(cst, "hba", [128, 16], F32)
    hbx = sb(cst, "hbx", [128, 16], F32)
    lam = sb(cst, "lam", [128, 16], F32)
    Kh = sb(cst, "Kh", [128, 16], F32)
    K2 = sb(cst, "K2", [128, 16], F32)
    ltmp = sb(cst, "ltmp", [128, 16], F32)
    ffcw = sb(cst, "ffcw", [128, NLAYER, 3, 48], F32)
    ffcb = sb(cst, "ffcb", [128, NLAYER, 48], F32)
    bg = sb(cst, "bg", [16, NLAYER], F32)
    ghalf = sb(cst, "ghalf", [128, NLAYER, 512], F32)
    epsc = sb(cst, "epsc", [128, 1], F32)
    tok_c = Buf()

    cw = [tok_c]
    K.dma(ident_f[:], c_ident[:, :], writes=cw)
    K.dma(ident_bf[:], c_ident[:, :], writes=cw, q="gpsimd", cast=True)
    K.dma(maskf[:], c_maskf[:, :], writes=cw, q="gpsimd", cast=True)
    K.dma(maskb[:], c_maskb[:, :], writes=cw, q="gpsimd", cast=True)
    for l in range(NLAYER):
        K.dma(g1c[:, l, :], norm1_g[l], writes=cw)
        K.dma(g2c[:, l, :], norm2_g[l], writes=cw)
        K.dma(rgcw[:, l, :, :], rg_conv_w[l], writes=cw)
        K.dma(rgcb[:, l, :], rg_conv_b[l], writes=cw)
        K.dma(ffcw[:, l, :, :], ffn_conv_w[l], writes=cw)
        K.dma(ffcb[:, l, :], ffn_conv_b[l], writes=cw)
        K.dma(bg[:, l:l + 1], b_gates[l], writes=cw)
        K.dma(ghalf[:, l, :], ml_norm_g[l:l + 1, :].to_broadcast([128, 512]), writes=cw)
    K.dma(hba[:], rg_ba[:, :], writes=cw)
    K.dma(hbx[:], rg_bx[:, :], writes=cw)
    K.dma(lam[:], rg_lambda[:, :], writes=cw)
    K.dma(gfc[:], final_g[:, :], writes=cw)
    V(lambda e: e.memset(ones_bf[:], 1.0), w=cw)
    V(lambda e: e.memset(ones_f[:], 1.0), w=cw)
    V(lambda e: e.memset(neghalf[:], -0.5), w=cw)
    V(lambda e: e.memset(zer[:], 0.0), w=cw)
    V(lambda e: e.memset(epsc[:], EPS), w=cw)
    V(lambda e: e.tensor_scalar_mul(out=hba[:], in0=hba[:], scalar1=0.5), r=cw, w=cw)
    V(lambda e: e.tensor_scalar_mul(out=hbx[:], in0=hbx[:], scalar1=0.5), r=cw, w=cw)
    for l in range(NLAYER):
        V(lambda e, l=l: e.tensor_scalar_mul(out=ghalf[:, l, :], in0=ghalf[:, l, :], scalar1=0.5), r=cw, w=cw)
    A(lambda e: e.activation(out=ltmp[:], in_=lam[:], func=AF.Abs), r=cw, w=cw)
    A(lambda e: e.activation(out=ltmp[:], in_=ltmp[:], func=AF.Exp, scale=-1.0), r=cw, w=cw)
    A(lambda e: e.activation(out=ltmp[:], in_=ltmp[:], func=AF.Ln, bias=1.0), r=cw, w=cw)
    V(lambda e: e.tensor_scalar(out=lam[:], in0=lam[:], scalar1=-1.0, scalar2=0.0, op0=ALU.mult, op1=ALU.max), r=cw, w=cw)
    V(lambda e: e.tensor_add(out=lam[:], in0=lam[:], in1=ltmp[:]), r=cw, w=cw)
    V(lambda e: e.tensor_scalar_mul(out=K2[:], in0=lam[:], scalar1=-8.0), r=cw, w=cw)
    V(lambda e: e.tensor_scalar_mul(out=Kh[:], in0=lam[:], scalar1=-4.0), r=cw, w=cw)

    def cast_weights(l):
        t = tok_wbf[("in", l)]
        for r0 in range(0, D, 256):
            K.dma(w_in_bf[l, r0:r0 + 256, :], w_in[l, r0:r0 + 256, :], writes=[t], q="gpsimd", cast=True)
        t = tok_wbf[("rg", l)]
        K.dma(wrg_bf[l].rearrange("d t c i o -> (d t c i) o")[:, :].rearrange("(a p) o -> p a o", p=128),
              zer[:, 0:2048].rearrange("p (a o) -> p a o", o=128), reads=cw, writes=[t])
        for d in range(2):
            for ty, wsrc in enumerate((rg_wa, rg_wx)):
                for par in range(2):
                    src = wsrc[l, d].rearrange("(c two) i o -> two c i o", two=2)[par]
                    dst = wrg_bf[l, d, ty, :, par * 64:(par + 1) * 64, par * 64:(par + 1) * 64]
                    K.dma(dst, src, writes=[t], q="gpsimd", cast=True)
        t = tok_wbf[("out", l)]
        for r0 in range(0, D, 512):
            K.dma(w_out_bf[l, r0:r0 + 512, :], w_out[l, r0:r0 + 512, :], writes=[t], q="gpsimd", cast=True)
        t = tok_wbf[("up", l)]
        for r0 in range(0, D, 128):
            K.dma(w_up_bf[l, r0:r0 + 128, :], w_up[l, r0:r0 + 128, :], writes=[t], q="gpsimd", cast=True)
        t = tok_wbf[("down", l)]
        for j in range(8):
            for c0 in range(0, 24, 8):
                src = w_down[l, c0 * 128:(c0 + 8) * 128, j * 128:(j + 1) * 128].rearrange("(c p) n -> p c n", p=128)
                K.dma(w_down_bf[l, j, :, c0:c0 + 8, :], src, writes=[t], q="gpsimd", cast=True)

    for l in range(NLAYER):
        cast_weights(l)

    def rmsnorm(xt, xtok, W, gcol, out_t, out_tok, sq, sqtok, rstd, rstd_tok, tmp, tmp_tok):
        for kt in range(KT):
            A(lambda e, kt=kt: e.activation(out=sq[:, kt, 0:W], in_=xt[:, kt, 0:W], func=AF.Square),
              r=[xtok[kt]], w=[sqtok[kt]])
        b = bank()
        mmgroup(ps[:, b, 0:W], [(ones_bf[:, :], sq[:, kt, 0:W]) for kt in range(KT)],
                [[sqtok[kt], tok_c] for kt in range(KT)], bank_tok[b])
        A(lambda e: e.activation(out=tmp[:, 0:W], in_=ps[:, b, 0:W], func=AF.Sqrt, scale=1.0 / D, bias=epsc[:, 0:1]),
          r=[bank_tok[b], tok_c], w=[tmp_tok])
        V(lambda e: e.reciprocal(out=rstd[:, 0:W], in_=tmp[:, 0:W]), r=[tmp_tok], w=[rstd_tok])
        for kt in range(KT):
            V(lambda e, kt=kt: e.scalar_tensor_tensor(out=out_t[:, kt, 0:W], in0=xt[:, kt, 0:W],
                                                      scalar=gcol[:, kt:kt + 1], in1=rstd[:, 0:W],
                                                      op0=ALU.mult, op1=ALU.mult),
              r=[xtok[kt], rstd_tok, tok_c], w=[out_tok[kt]])

    def phase_A(sq_list, l):
        flat_tiles = [(q, i) for q in sq_list for i in range(q["S"] // 512)]
        NT = len(flat_tiles)
        with ExitStack() as st:
            w_sb = sb(st, "w_in_sb", [128, KT, NIN], BF16)
            wtok = toks(KT)
            xT = [sb(st, "xT", [128, KT, 512], F32) for _ in range(2)]
            xtok = [toks(KT) for _ in range(2)]
            sq = sb(st, "sq", [128, KT, 512], BF16)
            sqtok = toks(KT)
            hTs = [sb(st, "hT", [128, KT, 512], BF16) for _ in range(2)]
            htoks = [toks(KT) for _ in range(2)]
            rstd = sb(st, "rstd", [128, 512], F32)
            tmp = sb(st, "tmp", [128, 512], F32)
            rstd_tok, tmp_tok = Buf(), Buf()
            rgx_st = [sb(st, "rgx_st", [128, 4, 512], BF16) for _ in range(2)]
            gg_st = [sb(st, "gg_st", [128, 4, 512], BF16) for _ in range(2)]
            q_st = [sb(st, "q_st", [128, 4, 512], BF16) for _ in range(2)]
            k_st = [sb(st, "k_st", [128, 4, 512], BF16) for _ in range(2)]
            va_st = [sb(st, "va_st", [128, 4, 516], BF16) for _ in range(2)]
            og_st = [sb(st, "og_st", [128, 4, 512], BF16) for _ in range(2)]
            ktm_st = [sb(st, "ktm_st", [128, 4, 512], BF16) for _ in range(2)]
            gr_st = [sb(st, "gr_st", [16, 512], F32) for _ in range(2)]
            th = sb(st, "th", [128, 512], F32)
            th_tok = Buf()
            stok = [{n: toks(4) for n in ("rgx", "gg", "q", "k", "va", "og", "ktm")} for _ in range(2)]
            grtok = [Buf(), Buf()]
            for p in range(2):
                G(lambda e, p=p: e.memset(va_st[p][:], 1.0), w=[stok[p]["va"]])
            for kt in range(KT):
                K.dma(w_sb[:, kt, :], w_in_bf[l, kt * 128:(kt + 1) * 128, :], reads=[tok_wbf[("in", l)]], writes=[wtok[kt]])
            def load_x(i):
                p = i % 2
                q_, ti = flat_tiles[i]
                xsrc = q_["xin"] if l == 0 else q_["xmid_d"]
                for h in range(2):
                    K.dma(xT[p][:, h * 4:(h + 1) * 4, :],
                          xsrc[h * 512:(h + 1) * 512, ti * 512:(ti + 1) * 512].rearrange("(k p) t -> p k t", p=128),
                          writes=[xtok[p][h * 4:(h + 1) * 4]])

            load_x(0)
            if NT > 1:
                load_x(1)
            rmsnorm(xT[0], xtok[0], 512, g1c[:, l, :], hTs[0], htoks[0], sq, sqtok, rstd, rstd_tok, tmp, tmp_tok)
            for i in range(NT):
                p = i % 2
                sv, ti_ = flat_tiles[i]
                t0 = ti_ * 512
                hT, htok = hTs[p], htoks[p]
                if i + 1 < NT:
                    rmsnorm(xT[1 - p], xtok[1 - p], 512, g1c[:, l, :], hTs[1 - p], htoks[1 - p], sq, sqtok, rstd, rstd_tok, tmp, tmp_tok)
                if i + 2 < NT:
                    load_x(i + 2)
                for cg in range(16):
                    b = bank()
                    mmgroup(ps[:, b, :], [(w_sb[:, kt, cg * 128:(cg + 1) * 128], hT[:, kt, :]) for kt in range(KT)],
                            [[wtok[kt], htok[kt]] for kt in range(KT)], bank_tok[b])
                    c = cg % 4
                    if cg < 4:
                        A(lambda e, b=b, c=c: e.copy(out=rgx_st[p][:, c, :], in_=ps[:, b, :]),
                          r=[bank_tok[b]], w=[stok[p]["rgx"][c]])
                    elif cg < 8:
                        A(lambda e, b=b, c=c: e.activation(out=gg_st[p][:, c, :], in_=ps[:, b, :], func=AF.Gelu_apprx_tanh),
                          r=[bank_tok[b]], w=[stok[p]["gg"][c]])
                    elif cg < 12:
                        A(lambda e, b=b, c=c: e.activation(out=q_st[p][:, c, :], in_=ps[:, b, :], func=AF.Copy, scale=QSCALE),
                          r=[bank_tok[b]], w=[stok[p]["q"][c]])
                    else:
                        V(lambda e, b=b, c=c: e.tensor_copy(out=k_st[p][:, c, :], in_=ps[:, b, :]),
                          r=[bank_tok[b]], w=[stok[p]["k"][c]])
                b = bank()
                mmgroup(ps[0:16, b, :], [(w_sb[:, kt, 3072:3088], hT[:, kt, :]) for kt in range(KT)],
                        [[wtok[kt], htok[kt]] for kt in range(KT)], bank_tok[b])
                A(lambda e, b=b: e.activation(out=gr_st[p][:], in_=ps[0:16, b, :], func=AF.Identity, bias=bg[:, l:l + 1]),
                  r=[bank_tok[b], tok_c], w=[grtok[p]])
                for sub in range(4):
                    ts = slice(sub * 128, (sub + 1) * 128)
                    for gi, c0 in enumerate((2048, 2560, 1536)):
                        b = bank()
                        mmgroup(ps[:, b, :], [(hT[:, kt, ts], w_sb[:, kt, c0:c0 + 512]) for kt in range(KT)],
                                [[wtok[kt], htok[kt]] for kt in range(KT)], bank_tok[b])
                        if gi == 0:
                            V(lambda e, b=b, sub=sub: e.tensor_copy(
                                out=va_st[p][:, sub, :].rearrange("p (h c) -> p h c", c=129)[:, :, 0:128],
                                in_=ps[:, b, :].rearrange("p (h c) -> p h c", c=128)),
                              r=[bank_tok[b]], w=[stok[p]["va"][sub]])
                        elif gi == 1:
                            A(lambda e, b=b: e.activation(out=th[:], in_=ps[:, b, :], func=AF.Tanh, scale=0.5),
                              r=[bank_tok[b]], w=[th_tok])
                            V(lambda e, sub=sub: e.scalar_tensor_tensor(out=og_st[p][:, sub, :], in0=th[:], scalar=1.0,
                                                                        in1=ghalf[:, l, :], op0=ALU.add, op1=ALU.mult),
                              r=[th_tok, tok_c], w=[stok[p]["og"][sub]])
                        else:
                            V(lambda e, b=b, sub=sub: e.tensor_copy(out=ktm_st[p][:, sub, :], in_=ps[:, b, :]),
                              r=[bank_tok[b]], w=[stok[p]["ktm"][sub]])
                tsl = slice(t0, t0 + 512)
                K.dma(sv["rgx_pad"][:, 2 + t0:2 + t0 + 512].rearrange("(c p) t -> p c t", p=128), rgx_st[p][:], reads=[stok[p]["rgx"]])
                K.dma(sv["gg_d"][:, tsl].rearrange("(c p) t -> p c t", p=128), gg_st[p][:], reads=[stok[p]["gg"]])
                K.dma(sv["qT_d"][:, tsl].rearrange("(c p) t -> p c t", p=128), q_st[p][:], reads=[stok[p]["q"]])
                K.dma(sv["kT_d"][:, tsl].rearrange("(c p) t -> p c t", p=128), k_st[p][:], reads=[stok[p]["k"]])
                K.dma(sv["va_d"][tsl, :].rearrange("(s p) c -> p s c", p=128), va_st[p][:], reads=[stok[p]["va"]])
                K.dma(sv["og_d"][tsl, :].rearrange("(s p) c -> p s c", p=128), og_st[p][:], reads=[stok[p]["og"]])
                K.dma(sv["ktm_d"][tsl, :].rearrange("(s p) c -> p s c", p=128), ktm_st[p][:], reads=[stok[p]["ktm"]])
                K.dma(sv["grow"][:, tsl], gr_st[p][:], reads=[grtok[p]])
            K.barrier()

    def phase_G(sv, gt_tiles):
        S = sv["S"]
        NC = S // 128
        P = 4 * NC
        WT, STT_, SCB = gt_tiles
        with ExitStack() as st:
            T = [sb(st, "gT", [P, 128], F32) for _ in range(4)]
            tk = Buf()
            for g4 in range(4):
                K.dma(T[g4][:], sv["grow"][g4 * 4:(g4 + 1) * 4, :].rearrange("g (j r) -> (g j) r", r=128), writes=[tk])
            ax = sb(st, "ax", [P, 128], F32)
            lf = sb(st, "lf", [P, 128], F32)
            bb = sb(st, "bb", [P, 128], F32)
            cc = sb(st, "cc", [P, 128], F32)
            ww = sb(st, "ww", [P, 128], F32)
            sst = sb(st, "sst", [P, 128], F32)
            cmax = sb(st, "cmax", [P, 1], F32)
            nmc_col = sb(st, "nmc_col", [P, 1], F32)
            rows = sb(st, "rows", [1, 8, 128], F32)
            for d in range(2):
                I_, F_ = T[2 * d], T[2 * d + 1]
                rev = (d == 1)

                def R(ap2):
                    return ap2[:, ::-1] if rev else ap2
                A(lambda e: e.activation(out=ax[:], in_=F_[:], func=AF.Abs), r=[tk], w=[tk])
                A(lambda e: e.activation(out=ax[:], in_=ax[:], func=AF.Exp, scale=-1.0), r=[tk], w=[tk])
                A(lambda e: e.activation(out=ax[:], in_=ax[:], func=AF.Ln, bias=1.0), r=[tk], w=[tk])
                V(lambda e: e.tensor_scalar_min(out=lf[:], in0=F_[:], scalar1=0.0), r=[tk], w=[tk])
                V(lambda e: e.tensor_sub(out=lf[:], in0=lf[:], in1=ax[:]), r=[tk], w=[tk])
                V(lambda e: e.tensor_tensor_scan(out=R(bb[:]), data0=R(ones_f[0:P, :]), data1=R(lf[:]), initial=0.0,
                                                 op0=ALU.mult, op1=ALU.add), r=[tk, tok_c], w=[tk])
                V(lambda e: e.tensor_sub(out=cc[:], in0=I_[:], in1=bb[:]), r=[tk], w=[tk])
                V(lambda e: e.reduce_max(out=cmax[:], in_=cc[:], axis=AX.X), r=[tk], w=[tk])
                bl = bb[:, 0:1] if rev else bb[:, 127:128]
                b1 = bank()
                PE(lambda e: e.transpose(out=ps[0:1, b1, 0:P], in_=cmax[:], identity=ident_f[0:P, 0:P]), r=[tk, tok_c], w=[bank_tok[b1]])
                V(lambda e: e.tensor_copy(out=rows[:, 0, 0:P], in_=ps[0:1, b1, 0:P]), r=[bank_tok[b1]], w=[tk])
                b2 = bank()
                PE(lambda e: e.transpose(out=ps[0:1, b2, 0:P], in_=bl, identity=ident_f[0:P, 0:P]), r=[tk, tok_c], w=[bank_tok[b2]])
                V(lambda e: e.tensor_copy(out=rows[:, 1, 0:P], in_=ps[0:1, b2, 0:P]), r=[bank_tok[b2]], w=[tk])
                V(lambda e: e.memset(rows[:, 3, :], 0.0), r=[tk], w=[tk])
                for g in range(4):
                    seg = slice(g * NC, (g + 1) * NC)
                    V(lambda e, seg=seg: e.tensor_tensor_scan(out=R(rows[:, 2, seg]), data0=R(rows[:, 0, seg]), data1=R(rows[:, 1, seg]),
                                                              initial=0.0, op0=ALU.max, op1=ALU.add), r=[tk], w=[tk])
                    if NC > 1:
                        if rev:
                            V(lambda e, g=g: e.tensor_copy(out=rows[:, 3, g * NC:(g + 1) * NC - 1], in_=rows[:, 2, g * NC + 1:(g + 1) * NC]), r=[tk], w=[tk])
                        else:
                            V(lambda e, g=g: e.tensor_copy(out=rows[:, 3, g * NC + 1:(g + 1) * NC], in_=rows[:, 2, g * NC:(g + 1) * NC - 1]), r=[tk], w=[tk])
                V(lambda e: e.tensor_max(out=rows[:, 4, 0:P], in0=rows[:, 3, 0:P], in1=rows[:, 0, 0:P]), r=[tk], w=[tk])
                V(lambda e: e.tensor_sub(out=rows[:, 5, 0:P], in0=rows[:, 3, 0:P], in1=rows[:, 4, 0:P]), r=[tk], w=[tk])
                A(lambda e: e.activation(out=rows[:, 5, 0:P], in_=rows[:, 5, 0:P], func=AF.Exp), r=[tk], w=[tk])
                V(lambda e: e.tensor_scalar_mul(out=rows[:, 6, 0:P], in0=rows[:, 4, 0:P], scalar1=-1.0), r=[tk], w=[tk])
                b3 = bank()
                PE(lambda e: e.matmul(ps[0:P, b3, 0:1], lhsT=rows[:, 6, 0:P], rhs=ones_f[0:1, 0:1], start=True, stop=True),
                   r=[tk, tok_c], w=[bank_tok[b3]])
                V(lambda e: e.tensor_copy(out=nmc_col[:], in_=ps[0:P, b3, 0:1]), r=[bank_tok[b3]], w=[tk])
                A(lambda e: e.activation(out=ww[:], in_=cc[:], func=AF.Exp, bias=nmc_col[:]), r=[tk], w=[tk])
                A(lambda e: e.activation(out=sst[:], in_=bb[:], func=AF.Exp, bias=nmc_col[:], scale=-1.0), r=[tk], w=[tk])
                b4 = bank()
                PE(lambda e: e.transpose(out=ps[:, b4, 0:P], in_=ww[:], identity=ident_f[0:P, 0:P]), r=[tk, tok_c], w=[bank_tok[b4]])
                V(lambda e: e.tensor_copy(out=WT[d][:, 0:P], in_=ps[:, b4, 0:P]), r=[bank_tok[b4]], w=[tk])
                b5 = bank()
                PE(lambda e: e.transpose(out=ps[:, b5, 0:P], in_=sst[:], identity=ident_f[0:P, 0:P]), r=[tk, tok_c], w=[bank_tok[b5]])
                V(lambda e: e.tensor_copy(out=STT_[d][:, 0:P], in_=ps[:, b5, 0:P]), r=[bank_tok[b5]], w=[tk])
                b6 = bank()
                PE(lambda e: e.matmul(ps[:, b6, 0:P], lhsT=ones_f[0:1, :], rhs=rows[:, 5, 0:P], start=True, stop=True),
                   r=[tk, tok_c], w=[bank_tok[b6]])
                V(lambda e: e.tensor_copy(out=SCB[d][:, 0:P], in_=ps[:, b6, 0:P]), r=[bank_tok[b6]], w=[tk])
            K.barrier()

    def phase_B(sv, l, gt_tiles):
        S = sv["S"]
        NT = S // 512
        NC = S // 128
        WT, STT_, SCB = gt_tiles
        with ExitStack() as st0:
            wrg = sb(st0, "wrg", [128, 16, 128], BF16)
            wrg_tok = Buf()
            K.dma(wrg[:], wrg_bf[l].rearrange("d t c i o -> i (d t c) o"), reads=[tok_wbf[("rg", l)]], writes=[wrg_tok])
            dgw = sb(st0, "dgw", [128, 16, 128], BF16)
            dgw_tok = Buf()
            for j in range(4):
                for c in range(4):
                    V(lambda e, j=j, c=c: e.tensor_scalar(out=dgw[:, j * 4 + c, :], in0=ident_f[:], scalar1=rgcw[:, l, j, c:c + 1], scalar2=None, op0=ALU.mult),
                      r=[tok_c], w=[dgw_tok])
            for d in (1, 0):
                rev = (d == 1)
                with ExitStack() as st:
                    if rev:
                        rgxw = [sb(st, "rgxw", [128, 4, 516], BF16) for _ in range(2)]
                        xc = [sb(st, "xc", [128, 4, 512], F32)] * 2
                    else:
                        xc = [sb(st, "xc", [128, 4, 512], F32) for _ in range(2)]
                        hbl = sb(st, "hbl", [128, 4, 512], F32)
                        ggl = sb(st, "ggl", [128, 4, 512], BF16)
                        ogl = sb(st, "ogl", [128, 4, 512], BF16)
                        hbm = sb(st, "hbm", [128, 4, 512], F32)
                        mix = sb(st, "mix", [128, 8, 512], BF16)
                        sqv = sb(st, "sqv", [128, 4, 512], F32)
                        yb = sb(st, "yb", [128, 4, 512], BF16)
                        ssum = sb(st, "ssum", [128, 16], F32)
                    qTl = [sb(st, "qTl", [128, 4, 512], BF16) for _ in range(2)]
                    kTl = [sb(st, "kTl", [128, 4, 512], BF16) for _ in range(2)]
                    val = [sb(st, "val", [128, 4, 516], BF16) for _ in range(2)]
                    ktl = [sb(st, "ktl", [128, 4, 512], BF16) for _ in range(2)]
                    xcb = sb(st, "xcb", [128, 4, 512], BF16)
                    ta = sb(st, "ta", [128, 4, 512], F32)
                    tx = sb(st, "tx", [128, 4, 512], F32)
                    aa = sb(st, "aa", [128, 4, 512], F32)
                    a2 = sb(st, "a2", [128, 4, 512], F32)
                    hh = sb(st, "hh", [128, 4, 512], F32)
                    carry = sb(st, "carry", [128, 4], F32)
                    PT = sb(st, "PT", [128, 4, 512], BF16)
                    STb = sb(st, "STb", [128, 4, 512], BF16)
                    vw = sb(st, "vw", [128, 4, 516], BF16)
                    Cst = sb(st, "Cst", [128, 516], F32)
                    Cb = sb(st, "Cb", [128, 4, 516], BF16)
                    hd = sb(st, "hd", [128, 4, 512], F32)
                    dd = sb(st, "dd", [128, 4, 4], F32)
                    names = ("rgxw", "xc", "qT", "kT", "val", "ktl")
                    tkp = [{n: toks(4) for n in names} for _ in range(2)]
                    if rev:
                        tkp[1]["xc"] = tkp[0]["xc"]
                    tk = {n: toks(4) for n in ("xcb", "ta", "tx", "aa", "a2", "hh", "hbl", "ggl", "ogl", "hbm", "hd", "PT", "vw", "Cb", "dd", "STb")}
                    mixtok = toks(8)
                    t_carry, t_C, t_sqv, t_yb, t_ss = (Buf() for _ in range(5))
                    V(lambda e: e.memset(carry[:], 0.0), w=[t_carry])
                    V(lambda e: e.memset(Cst[:], 0.0), w=[t_C])
                    order = list(range(NT - 1, -1, -1) if rev else range(NT))

                    def loads_early(i, p):
                        t0 = i * 512
                        tsl = slice(t0, t0 + 512)
                        if rev:
                            K.dma(rgxw[p][:, :, 0:515], sv["rgx_pad"][:, t0:t0 + 515].rearrange("(c p) t -> p c t", p=128), writes=[tkp[p]["rgxw"]])
                        else:
                            K.dma(xc[p][:], xc_d[:, tsl].rearrange("(c p) t -> p c t", p=128), writes=[tkp[p]["xc"]])
                        K.dma(kTl[p][:], sv["kT_d"][:, tsl].rearrange("(c p) t -> p c t", p=128), writes=[tkp[p]["kT"]])
                        K.dma(qTl[p][:], sv["qT_d"][:, tsl].rearrange("(c p) t -> p c t", p=128), writes=[tkp[p]["qT"]])
                        K.dma(val[p][:], sv["va_d"][tsl, :].rearrange("(s p) c -> p s c", p=128), writes=[tkp[p]["val"]])
                        K.dma(ktl[p][:], sv["ktm_d"][tsl, :].rearrange("(s p) c -> p s c", p=128), writes=[tkp[p]["ktl"]])

                    loads_early(order[0], 0)
                    for it, i in enumerate(order):
                        p = it % 2
                        T = tkp[p]
                        t0 = i * 512
                        tsl = slice(t0, t0 + 512)
                        if it + 1 < NT:
                            loads_early(order[it + 1], 1 - p)
                        if not rev:
                            K.dma(hbl[:], hbrg_d[:, tsl].rearrange("(c p) t -> p c t", p=128), writes=[tk["hbl"]])
                            K.dma(hbm[:], hbml_d[tsl, :].rearrange("(s p) c -> p s c", p=128), writes=[tk["hbm"]])
                            K.dma(ggl[:], sv["gg_d"][:, tsl].rearrange("(c p) t -> p c t", p=128), writes=[tk["ggl"]])
                            K.dma(ogl[:], sv["og_d"][tsl, :].rearrange("(s p) c -> p s c", p=128), writes=[tk["ogl"]])
                        xcp = xc[p]
                        if rev:
                            for c in range(4):
                                bcv = bank()
                                for j in range(4):
                                    PE(lambda e, c=c, j=j, bcv=bcv: e.matmul(ps[:, bcv, :], lhsT=dgw[:, j * 4 + c, :], rhs=rgxw[p][:, c, j:j + 512], start=(j == 0), stop=(j == 3)),
                                       r=[dgw_tok, T["rgxw"][c]], w=[bank_tok[bcv]], inc=(j == 3))
                                A(lambda e, c=c, bcv=bcv: e.activation(out=xcp[:, c, :], in_=ps[:, bcv, :], func=AF.Identity, bias=rgcb[:, l, c:c + 1]),
                                  r=[bank_tok[bcv], tok_c], w=[T["xc"][c]])
                        for c in range(4):
                            V(lambda e, c=c: e.tensor_copy(out=xcb[:, c, :], in_=xcp[:, c, :]), r=[T["xc"][c]], w=[tk["xcb"][c]])
                        mk = maskb if rev else maskf
                        def emit_s12(sub):
                            j = i * 4 + sub
                            cs = slice(sub * 128, (sub + 1) * 128)
                            bst = bank()
                            for h in range(4):
                                PE(lambda e, h=h, bst=bst, cs=cs: e.matmul(ps[:, bst, h * 128:(h + 1) * 128], lhsT=kTl[p][:, h, cs], rhs=qTl[p][:, h, cs], start=True, stop=True),
                                   r=[T["kT"][h], T["qT"][h]], w=[bank_tok[bst]], inc=(h == 3))
                            V(lambda e, bst=bst, sub=sub: e.tensor_tensor(out=PT[:, sub, :].rearrange("p (h t) -> p h t", h=4),
                                                                          in0=ps[:, bst, :].rearrange("p (h t) -> p h t", h=4),
                                                                          in1=mk[:].unsqueeze(1).to_broadcast([128, 4, 128]), op=ALU.mult),
                              r=[bank_tok[bst], tok_c], w=[tk["PT"][sub]])
                            wcols = WT[d][:, j:j + 3 * NC + 1:NC]
                            G(lambda e, sub=sub, wcols=wcols: e.tensor_tensor(out=vw[:, sub, :].rearrange("p (h c) -> p h c", h=4),
                                                                             in0=val[p][:, sub, :].rearrange("p (h c) -> p h c", h=4),
                                                                             in1=wcols.unsqueeze(2).to_broadcast([128, 4, 129]), op=ALU.mult),
                              r=[T["val"][sub]], w=[tk["vw"][sub]])
                            cb_ = bank2()
                            for h in range(4):
                                o_ap = ps[:, cb_[h // 2], (h % 2) * 129:(h % 2) * 129 + 129]
                                PE(lambda e, h=h, o_ap=o_ap, sub=sub: e.matmul(o_ap, lhsT=ktl[p][:, sub, h * 128:(h + 1) * 128], rhs=vw[:, sub, h * 129:(h + 1) * 129], start=True, stop=True),
                                   r=[T["ktl"][sub], tk["vw"][sub]], w=[bank_tok[cb_[h // 2]]], inc=(h % 2 == 1))
                            sccols = SCB[d][:, j:j + 3 * NC + 1:NC]
                            V(lambda e, sccols=sccols: e.tensor_tensor(out=Cst[:].rearrange("p (h c) -> p h c", h=4),
                                                                      in0=Cst[:].rearrange("p (h c) -> p h c", h=4),
                                                                      in1=sccols.unsqueeze(2).to_broadcast([128, 4, 129]), op=ALU.mult),
                              r=[t_C], w=[t_C])
                            V(lambda e, sub=sub: e.tensor_copy(out=Cb[:, sub, :], in_=Cst[:]), r=[t_C], w=[tk["Cb"][sub]])
                            V(lambda e, cb_=cb_: e.tensor_add(out=Cst[:].rearrange("p (b c) -> p b c", b=2), in0=Cst[:].rearrange("p (b c) -> p b c", b=2),
                                                              in1=ps[:, cb_[0]:cb_[0] + 2, 0:258]),
                              r=[t_C, bank_tok[cb_[0]], bank_tok[cb_[1]]], w=[t_C])
                        def emit_rg34(half):
                            bks = {}
                            for c in (2 * half, 2 * half + 1):
                                for ty in range(2):
                                    bb_ = bank()
                                    bks[(c, ty)] = bb_
                                    PE(lambda e, c=c, ty=ty, bb_=bb_: e.matmul(ps[:, bb_, :], lhsT=wrg[:, (d * 2 + ty) * 4 + c, :], rhs=xcb[:, c, :], start=True, stop=True),
                                       r=[wrg_tok, tk["xcb"][c]], w=[bank_tok[bb_]])
                            for c in (2 * half, 2 * half + 1):
                                col = (l * 2 + d) * 4 + c
                                A(lambda e, c=c, col=col, bb_=bks[(c, 0)]: e.activation(out=ta[:, c, :], in_=ps[:, bb_, :], func=AF.Tanh, scale=0.5, bias=hba[:, col:col + 1]),
                                  r=[bank_tok[bks[(c, 0)]], tok_c], w=[tk["ta"][c]])
                                A(lambda e, c=c, col=col, bb_=bks[(c, 1)]: e.activation(out=tx[:, c, :], in_=ps[:, bb_, :], func=AF.Tanh, scale=0.5, bias=hbx[:, col:col + 1]),
                                  r=[bank_tok[bks[(c, 1)]], tok_c], w=[tk["tx"][c]])
                            for c in (2 * half, 2 * half + 1):
                                col = (l * 2 + d) * 4 + c
                                A(lambda e, c=c, col=col: e.activation(out=aa[:, c, :], in_=ta[:, c, :], func=AF.Exp, scale=Kh[:, col:col + 1], bias=Kh[:, col:col + 1]),
                                  r=[tk["ta"][c], tok_c], w=[tk["aa"][c]])
                                A(lambda e, c=c, col=col: e.activation(out=a2[:, c, :], in_=ta[:, c, :], func=AF.Exp, scale=K2[:, col:col + 1], bias=K2[:, col:col + 1]),
                                  r=[tk["ta"][c], tok_c], w=[tk["a2"][c]])
                        subs = list(range(3, -1, -1) if rev else range(4))
                        emit_s12(subs[0])
                        emit_rg34(0)
                        emit_s12(subs[1])
                        emit_rg34(1)
                        emit_s12(subs[2])
                        emit_s12(subs[3])
                        for sub in subs:
                            j = i * 4 + sub
                            cs = slice(sub * 128, (sub + 1) * 128)
                            nb = bank2()
                            for h in range(4):
                                o_ap = ps[:, nb[h // 2], (h % 2) * 129:(h % 2) * 129 + 129]
                                PE(lambda e, h=h, o_ap=o_ap, sub=sub: e.matmul(o_ap, lhsT=PT[:, sub, h * 128:(h + 1) * 128], rhs=vw[:, sub, h * 129:(h + 1) * 129], start=True, stop=False),
                                   r=[tk["PT"][sub], tk["vw"][sub]], w=[bank_tok[nb[h // 2]]], inc=False)
                                PE(lambda e, h=h, o_ap=o_ap, sub=sub, cs=cs: e.matmul(o_ap, lhsT=qTl[p][:, h, cs], rhs=Cb[:, sub, h * 129:(h + 1) * 129], start=False, stop=True),
                                   r=[T["qT"][h], tk["Cb"][sub]], w=[bank_tok[nb[h // 2]]], inc=(h % 2 == 1))
                            stc = STT_[d][:, j:j + 3 * NC + 1:NC]
                            den_ap = ps[:, nb[0]:nb[0] + 2, 128:258:129]
                            nbt = [bank_tok[nb[0]], bank_tok[nb[1]]]
                            V(lambda e, sub=sub, den_ap=den_ap, stc=stc: e.tensor_tensor(out=dd[:, sub, :].rearrange("p (b h) -> p b h", b=2), in0=den_ap,
                                                                                      in1=stc.rearrange("p (b h) -> p b h", b=2), op=ALU.max),
                              r=nbt, w=[tk["dd"][sub]])
                            V(lambda e, sub=sub, den_ap=den_ap: e.scalar_tensor_tensor(out=dd[:, sub, :].rearrange("p (b h) -> p b h", b=2), in0=den_ap, scalar=-1.0,
                                                                                    in1=dd[:, sub, :].rearrange("p (b h) -> p b h", b=2), op0=ALU.mult, op1=ALU.max),
                              r=nbt + [tk["dd"][sub]], w=[tk["dd"][sub]])
                            V(lambda e, sub=sub: e.reciprocal(out=dd[:, sub, :], in_=dd[:, sub, :]), r=[tk["dd"][sub]], w=[tk["dd"][sub]])
                            for bi in range(2):
                                V(lambda e, bi=bi, sub=sub, nb=nb: e.tensor_tensor(
                                    out=hd[:, sub, bi * 256:(bi + 1) * 256].rearrange("p (h c) -> p h c", h=2),
                                    in0=ps[:, nb[bi], 0:258].rearrange("p (h c) -> p h c", h=2)[:, :, 0:128],
                                    in1=dd[:, sub, 2 * bi:2 * bi + 2].unsqueeze(2).to_broadcast([128, 2, 128]), op=ALU.mult),
                                  r=[bank_tok[nb[bi]], tk["dd"][sub]], w=[tk["hd"][sub]])
                        for c in range(4):
                            A(lambda e, c=c: e.activation(out=a2[:, c, :], in_=a2[:, c, :], func=AF.Sqrt, scale=-1.0, bias=1.0),
                              r=[tk["a2"][c]], w=[tk["a2"][c]])
                        for c in range(4):
                            if rev:
                                G(lambda e, c=c: e.tensor_scalar(out=tx[:, c, :], in0=tx[:, c, :], scalar1=1.0, scalar2=1.0, op0=ALU.add, op1=ALU.mult), r=[tk["tx"][c]], w=[tk["tx"][c]])
                                G(lambda e, c=c: e.tensor_mul(out=tx[:, c, :], in0=tx[:, c, :], in1=xcp[:, c, :]), r=[tk["tx"][c], T["xc"][c]], w=[tk["tx"][c]])
                            else:
                                V(lambda e, c=c: e.scalar_tensor_tensor(out=tx[:, c, :], in0=tx[:, c, :], scalar=1.0, in1=xcp[:, c, :],
                                                                        op0=ALU.add, op1=ALU.mult),
                                  r=[tk["tx"][c], T["xc"][c]], w=[tk["tx"][c]])
                        for c in range(4):
                            V(lambda e, c=c: e.scalar_tensor_tensor(out=tx[:, c, :], in0=a2[:, c, :], scalar=0.5, in1=tx[:, c, :],
                                                                    op0=ALU.mult, op1=ALU.mult),
                              r=[tk["a2"][c], tk["tx"][c]], w=[tk["tx"][c]])
                        for c in range(4):
                            if rev:
                                V(lambda e, c=c: e.tensor_tensor_scan(out=hh[:, c, ::-1], data0=aa[:, c, ::-1], data1=tx[:, c, ::-1],
                                                                      initial=carry[:, c:c + 1], op0=ALU.mult, op1=ALU.add),
                                  r=[tk["aa"][c], tk["tx"][c], t_carry], w=[tk["hh"][c]])
                            else:
                                V(lambda e, c=c: e.tensor_tensor_scan(out=hh[:, c, :], data0=aa[:, c, :], data1=tx[:, c, :],
                                                                      initial=carry[:, c:c + 1], op0=ALU.mult, op1=ALU.add),
                                  r=[tk["aa"][c], tk["tx"][c], t_carry], w=[tk["hh"][c]])
                        last = 0 if rev else 511
                        V(lambda e: e.tensor_copy(out=carry[:, :], in_=hh[:, :, last]), r=[tk["hh"]], w=[t_carry])
                        if rev:
                            K.dma(xc_d[:, tsl].rearrange("(c p) t -> p c t", p=128), xcp[:], reads=[T["xc"]])
                            K.dma(hbrg_d[:, tsl].rearrange("(c p) t -> p c t", p=128), hh[:], reads=[tk["hh"]])
                            K.dma(hbml_d[tsl, :].rearrange("(s p) c -> p s c", p=128), hd[:], reads=[tk["hd"]])
                        else:
                            for c in range(4):
                                G(lambda e, c=c: e.tensor_add(out=hh[:, c, :], in0=hh[:, c, :], in1=hbl[:, c, :]),
                                  r=[tk["hh"][c], tk["hbl"][c]], w=[tk["hh"][c]])
                            for c in range(4):
                                G(lambda e, c=c: e.tensor_mul(out=mix[:, c, :], in0=hh[:, c, :], in1=ggl[:, c, :]),
                                  r=[tk["hh"][c], tk["ggl"][c]], w=[mixtok[c]])
                            G(lambda e: e.tensor_add(out=hd[:], in0=hd[:], in1=hbm[:]), r=[tk["hd"], tk["hbm"]], w=[tk["hd"]])
                            G(lambda e: e.tensor_mul(out=sqv[:], in0=hd[:], in1=hd[:]), r=[tk["hd"]], w=[t_sqv])
                            V(lambda e: e.reduce_sum(out=ssum[:], in_=sqv[:].rearrange("p s (h c) -> p (s h) c", h=4), axis=AX.X),
                              r=[t_sqv], w=[t_ss])
                            A(lambda e: e.activation(out=ssum[:], in_=ssum[:], func=AF.Sqrt, scale=1.0 / 128, bias=epsc[:, 0:1]), r=[t_ss, tok_c], w=[t_ss])
                            V(lambda e: e.reciprocal(out=ssum[:], in_=ssum[:]), r=[t_ss], w=[t_ss])
                            V(lambda e: e.tensor_tensor(out=sqv[:].rearrange("p s (h c) -> p (s h) c", h=4),
                                                        in0=hd[:].rearrange("p s (h c) -> p (s h) c", h=4),
                                                        in1=ssum[:].unsqueeze(2).to_broadcast([128, 16, 128]), op=ALU.mult),
                              r=[tk["hd"], t_ss, t_sqv], w=[t_sqv])
                            G(lambda e: e.tensor_mul(out=yb[:], in0=sqv[:], in1=ogl[:]), r=[t_sqv, tk["ogl"]], w=[t_yb])
                            for half in range(2):
                                bt = bank()
                                pbf = ps[:, bt, :].bitcast(BF16)
                                for q in range(8):
                                    sub, h = half * 2 + q // 4, q % 4
                                    PE(lambda e, q=q, sub=sub, h=h, pbf=pbf: e.transpose(out=pbf[:, q * 128:(q + 1) * 128], in_=yb[:, sub, h * 128:(h + 1) * 128], identity=ident_bf[:]),
                                       r=[t_yb, tok_c], w=[bank_tok[bt]], inc=(q == 7))
                                for s2 in range(2):
                                    sub = half * 2 + s2
                                    A(lambda e, pbf=pbf, sub=sub, s2=s2: e.copy(out=mix[:, 4:8, sub * 128:(sub + 1) * 128],
                                                                            in_=pbf[:, s2 * 512:(s2 + 1) * 512].rearrange("p (h t) -> p h t", h=4)),
                                      r=[bank_tok[bt]], w=[mixtok[4:8]])
                            K.dma(sv["mixT_d"][:, tsl].rearrange("(k p) t -> p k t", p=128), mix[:], reads=[mixtok])
                    K.barrier()

    def phase_C(sq_list, l, final):
        with ExitStack() as st:
            wo = sb(st, "wo", [128, KT, D], BF16)
            wu = sb(st, "wu", [128, KT, 2 * DFF], BF16)
            wd = [sb(st, "wd", [128, 24, 128], BF16) for _ in range(2)]
            wo_tok, wu_tok = toks(KT), toks(KT)
            wd_tok = [Buf() for _ in range(2)]
            for kt in range(KT):
                K.dma(wo[:, kt, :], w_out_bf[l, kt * 128:(kt + 1) * 128, :], reads=[tok_wbf[("out", l)]], writes=[wo_tok[kt]])
            for kt in range(KT):
                K.dma(wu[:, kt, :], w_up_bf[l, kt * 128:(kt + 1) * 128, :], reads=[tok_wbf[("up", l)]], writes=[wu_tok[kt]])
            xt = sb(st, "xt", [128, KT, 512], F32)
            xtok = toks(KT)
            gbuf = sb(st, "gbuf", [128, 24, 512], BF16)
            gtok = toks(24)
            hT = sb(st, "hTc", [128, KT, 512], BF16)
            htok = toks(KT)
            rstd = sb(st, "rstdc", [128, 512], F32)
            tmp = sb(st, "tmpc", [128, 512], F32)
            rstd_tok, tmp_tok = Buf(), Buf()
            accg = [sb(st, "accg", [128, 512], F32) for _ in range(2)]
            accv = [sb(st, "accv", [128, 512], F32) for _ in range(2)]
            acc_tok = [[Buf(), Buf(), Buf()] for _ in range(2)]
            wdc = [0]
            all_tiles = [(q, t0, ln) for q in sq_list for (t0, ln) in seq_tiles(q["S"], 510)]
            for (sv, t0, ln) in all_tiles:
                S = sv["S"]
                xsrc = sv["xin"] if l == 0 else sv["xmid_d"]
                xdst = sv["yout"] if final else sv["xmid_d"]
                mixT_d = sv["mixT_d"]
                W = ln + 2
                lo = t0 - 1
                c0 = 1 if lo < 0 else 0
                c1 = W - 1 if lo + W > S else W
                if c0 == 1:
                    V(lambda e: e.memset(xt[:, :, 0:1], 0.0), w=[xtok])
                    V(lambda e: e.memset(gbuf[:, 0:8, 0:1], 0.0), w=[gtok[0:8]])
                if c1 == W - 1:
                    V(lambda e: e.memset(xt[:, :, W - 1:W], 0.0), w=[xtok])
                    V(lambda e: e.memset(gbuf[:, 0:8, W - 1:W], 0.0), w=[gtok[0:8]])
                for h in range(2):
                    K.dma(gbuf[:, h * 4:(h + 1) * 4, c0:c1],
                          mixT_d[h * 512:(h + 1) * 512, lo + c0:lo + c1].rearrange("(k p) t -> p k t", p=128),
                          writes=[gtok[h * 4:(h + 1) * 4]])
                for kt in range(KT):
                    K.dma(xt[:, kt, c0:c1], xsrc[kt * 128:(kt + 1) * 128, lo + c0:lo + c1], writes=[xtok[kt]])
                for jd in range(KT):
                    b = bank()
                    mmgroup(ps[:, b, 0:W], [(wo[:, kt, jd * 128:(jd + 1) * 128], gbuf[:, kt, 0:W]) for kt in range(KT)],
                            [[wo_tok[kt], gtok[kt]] for kt in range(KT)], bank_tok[b])
                    V(lambda e, b=b, jd=jd: e.tensor_add(out=xt[:, jd, 0:W], in0=xt[:, jd, 0:W], in1=ps[:, b, 0:W]),
                      r=[bank_tok[b], xtok[jd]], w=[xtok[jd]])
                rmsnorm(xt, xtok, W, g2c[:, l, :], hT, htok, gbuf[:, 8:16, :], gtok[8:16], rstd, rstd_tok, tmp, tmp_tok)
                for c in range(24):
                    pp = c % 2
                    bg_ = bank()
                    mmgroup(ps[:, bg_, 0:W], [(wu[:, kt, c * 128:(c + 1) * 128], hT[:, kt, 0:W]) for kt in range(KT)],
                            [[wu_tok[kt], htok[kt]] for kt in range(KT)], bank_tok[bg_])
                    bv_ = bank()
                    mmgroup(ps[:, bv_, 0:W], [(wu[:, kt, DFF + c * 128:DFF + (c + 1) * 128], hT[:, kt, 0:W]) for kt in range(KT)],
                            [[wu_tok[kt], htok[kt]] for kt in range(KT)], bank_tok[bv_])
                    for (bb_, acc, ti, cc_) in ((bg_, accg[pp], 0, c), (bv_, accv[pp], 1, 24 + c)):
                        A(lambda e, bb_=bb_, acc=acc, cc_=cc_: e.activation(out=acc[:, 1:W - 1], in_=ps[:, bb_, 1:W - 1], func=AF.Identity,
                                                                           scale=ffcw[:, l, 1, cc_:cc_ + 1], bias=ffcb[:, l, cc_:cc_ + 1]),
                          r=[bank_tok[bb_], tok_c], w=[acc_tok[pp][ti]])
                        V(lambda e, bb_=bb_, acc=acc, cc_=cc_: e.scalar_tensor_tensor(out=acc[:, 1:W - 1], in0=ps[:, bb_, 0:W - 2],
                                                                                     scalar=ffcw[:, l, 0, cc_:cc_ + 1], in1=acc[:, 1:W - 1],
                                                                                     op0=ALU.mult, op1=ALU.add),
                          r=[bank_tok[bb_], acc_tok[pp][ti], tok_c], w=[acc_tok[pp][ti]])
                        V(lambda e, bb_=bb_, acc=acc, cc_=cc_: e.scalar_tensor_tensor(out=acc[:, 1:W - 1], in0=ps[:, bb_, 2:W],
                                                                                     scalar=ffcw[:, l, 2, cc_:cc_ + 1], in1=acc[:, 1:W - 1],
                                                                                     op0=ALU.mult, op1=ALU.add),
                          r=[bank_tok[bb_], acc_tok[pp][ti], tok_c], w=[acc_tok[pp][ti]])
                    A(lambda e, pp=pp: e.activation(out=accg[pp][:, 1:W - 1], in_=accg[pp][:, 1:W - 1], func=AF.Gelu_apprx_tanh),
                      r=[acc_tok[pp][0]], w=[acc_tok[pp][0]])
                    G(lambda e, pp=pp, c=c: e.tensor_mul(out=gbuf[:, c, 1:W - 1], in0=accg[pp][:, 1:W - 1], in1=accv[pp][:, 1:W - 1]),
                      r=[acc_tok[pp][0], acc_tok[pp][1]], w=[gtok[c]])
                for jd in range(KT):
                    wi = wdc[0] % 2
                    wdc[0] += 1
                    K.dma(wd[wi][:], w_down_bf[l, jd], reads=[tok_wbf[("down", l)]], writes=[wd_tok[wi]])
                    b = bank()
                    mmgroup(ps[:, b, 0:ln], [(wd[wi][:, c, :], gbuf[:, c, 1:W - 1]) for c in range(24)],
                            [[wd_tok[wi], gtok[c]] for c in range(24)], bank_tok[b])
                    V(lambda e, b=b, jd=jd: e.tensor_add(out=xt[:, jd, 1:W - 1], in0=xt[:, jd, 1:W - 1], in1=ps[:, b, 0:ln]),
                      r=[bank_tok[b], xtok[jd]], w=[xtok[jd]])
                    if not final:
                        K.dma(xdst[jd * 128:(jd + 1) * 128, t0:t0 + ln], xt[:, jd, 1:W - 1], reads=[xtok[jd]])
                if final:
                    class _Sh:
                        def __init__(s, t): s.t = t
                        def __getitem__(s, idx):
                            a, b_, c_ = idx
                            return s.t[a, b_, slice(c_.start + 1, c_.stop + 1)]
                    rmsnorm(_Sh(xt), xtok, ln, gfc, _Sh(xt), xtok, gbuf[:, 8:16, :], gtok[8:16], rstd, rstd_tok, tmp, tmp_tok)
                    K.dma(xdst[:, t0:t0 + ln].rearrange("(k p) t -> p k t", p=128), xt[:, :, 1:W - 1], reads=[xtok])
            K.barrier()

    zf = sb(cst, "zf", [128, 4], F32)
    V(lambda e: e.memset(zf[:], 0.0), w=cw)
    for q in seqs:
        for c in range(4):
            K.dma(q["rgx_pad"][c * 128:(c + 1) * 128, 0:2], zer[:, 0:2], reads=cw)
            K.dma(q["rgx_pad"][c * 128:(c + 1) * 128, q["S"] + 2:q["S"] + 3], zer[:, 0:1], reads=cw, slow=True)
    K.barrier()

    for l in range(NLAYER):
        phase_A(seqs, l)
        for q in seqs:
            NC = q["S"] // 128
            with ExitStack() as gs:
                WT = [sb(gs, "WT", [128, 4 * NC], F32) for _ in range(2)]
                STT_ = [sb(gs, "STT", [128, 4 * NC], F32) for _ in range(2)]
                SCB = [sb(gs, "SCB", [128, 4 * NC], F32) for _ in range(2)]
                phase_G(q, (WT, STT_, SCB))
                phase_B(q, l, (WT, STT_, SCB))
        phase_C(seqs, l, final=(l == NLAYER - 1))
    K.barrier(include_cast=True)
    es.close()
    return nc, K


def build(groups, debug=False):
    _, K1 = _build(groups, debug, None)
    nc, K2 = _build(groups, debug, K1.rec)
    build.n_ins = K2.n_ins
    build.n_inc = (getattr(K1, "n_inc", 0), getattr(K2, "n_inc", 0))
    return nc


_GROUPS = [("p", 2, 4096), ("s", 4, 2048)]
_WNAMES = ["norm1_g", "w_in", "b_gates", "rg_conv_w", "rg_conv_b", "rg_wa", "rg_ba", "rg_wx", "rg_bx", "rg_lambda",
           "ml_norm_g", "w_out", "norm2_g", "w_up", "ffn_conv_w", "ffn_conv_b", "w_down", "final_g"]


def host_consts():
    s = np.arange(128)[:, None]
    t = np.arange(128)[None, :]
    return {"c_ident": np.eye(128, dtype=np.float32),
            "c_maskf": (s <= t).astype(np.float32),
            "c_maskb": (s >= t).astype(np.float32)}


def weight_map(inputs):
    m = {k: np.asarray(inputs[k], dtype=np.float32) for k in _WNAMES}
    L = NLAYER
    m["b_gates"] = m["b_gates"].reshape(L, 16, 1)
    m["norm1_g"] = m["norm1_g"].reshape(L, 8, 128).transpose(0, 2, 1)
    m["norm2_g"] = m["norm2_g"].reshape(L, 8, 128).transpose(0, 2, 1)
    m["final_g"] = m["final_g"].reshape(8, 128).transpose(1, 0)
    m["rg_conv_w"] = m["rg_conv_w"].reshape(L, 4, 4, 128).transpose(0, 3, 1, 2)
    m["rg_conv_b"] = m["rg_conv_b"].reshape(L, 4, 128).transpose(0, 2, 1)
    for k in ("rg_ba", "rg_bx", "rg_lambda"):
        m[k] = m[k].reshape(L, 2, 4, 128).transpose(3, 0, 1, 2).reshape(128, 16)
    m["ffn_conv_w"] = m["ffn_conv_w"].reshape(L, 3, 48, 128).transpose(0, 3, 1, 2)
    m["ffn_conv_b"] = m["ffn_conv_b"].reshape(L, 48, 128).transpose(0, 2, 1)
    m = {k: np.ascontiguousarray(v, dtype=np.float32) for k, v in m.items()}
    m.update(host_consts())
    return m


def kernel(**inputs):
    xp = np.asarray(inputs["x_prompt"], dtype=np.float32)
    xs = np.asarray(inputs["x_sample"], dtype=np.float32)
    n = 8
    nc = build(_GROUPS)
    wm = weight_map(inputs)
    in_maps = []
    for c in range(n):
        m = dict(wm)
        m["x_p"] = np.ascontiguousarray(xp[2 * c:2 * c + 2].transpose(0, 2, 1))
        m["x_s"] = np.ascontiguousarray(xs[4 * c:4 * c + 4].transpose(0, 2, 1))
        in_maps.append(m)
    res = run_bass_kernel_spmd(nc, in_maps, core_ids=list(range(n)))
    yp = np.concatenate([np.asarray(r["y_p"]).transpose(0, 2, 1) for r in res.results], axis=0)
    ys = np.concatenate([np.asarray(r["y_s"]).transpose(0, 2, 1) for r in res.results], axis=0)
    return (np.ascontiguousarray(yp, dtype=np.float32), np.ascontiguousarray(ys, dtype=np.float32))
```

```python
import numpy as np
from contextlib import ExitStack
import concourse.bass as bass
import concourse.mybir as mybir
from concourse.bass_utils import run_bass_kernel_spmd

F32 = mybir.dt.float32
BF16 = mybir.dt.bfloat16
AF = mybir.ActivationFunctionType
ALU = mybir.AluOpType
AX = mybir.AxisListType

D = 1024
KT = 8
NIN = 3088
DFF = 3072
EPS = 1e-6
NLAYER = 2
QSCALE = 128 ** -0.5


class Buf:
    __slots__ = ("w", "r")

    def __init__(self):
        self.w = None
        self.r = {}


def toks(*shape):
    a = np.empty(shape, dtype=object)
    for idx in np.ndindex(*shape):
        a[idx] = Buf()
    return a


def flat(*xs):
    out = []
    for x in xs:
        if isinstance(x, Buf):
            out.append(x)
        elif isinstance(x, np.ndarray):
            out.extend(x.ravel().tolist())
        else:
            for y in x:
                out.extend(flat(y))
    return out


class Key:
    def __init__(self, sem, eng=None, name=""):
        self.sem = sem
        self.eng = eng
        self.val = 0
        self.waited = {}
        self.name = name


class Sched:
    def __init__(self, nc, es, nslots=24, ncslots=8, needed=None):
        self.nc = nc
        self.needed = needed
        self.rec = {}
        self.E = {}
        for nm in ("sync", "scalar", "vector", "gpsimd", "tensor"):
            self.E[nm] = Key(es.enter_context(nc.semaphore("s_" + nm)), getattr(nc, nm), nm)
            self.E[nm].ordn = 0
            self.E[nm].omap = {}
            self.rec[nm] = set()
        self.slots = [Key(es.enter_context(nc.semaphore("d%d" % i)), name="d%d" % i) for i in range(nslots)]
        self.cslots = [Key(es.enter_context(nc.semaphore("c%d" % i)), name="c%d" % i) for i in range(ncslots)]
        self.rr = 0
        self.crr = 0
        self.n_ins = 0

    def _wait(self, E, key, val, raw):
        if val <= 0:
            return
        if key is E and E.name == "tensor":
            return
        if E.waited.get(key, 0) >= val:
            return
        E.eng.wait_ge(key.sem, val)
        E.waited[key] = val
        if key.eng is not None:
            self.rec[key.name].add(key.omap.get(val, -1))

    def _deps(self, E, reads, writes):
        for b in reads:
            if b.w is not None:
                self._wait(E, b.w[0], b.w[1], True)
        for b in writes:
            if b.w is not None:
                self._wait(E, b.w[0], b.w[1], False)
            for k, v in b.r.items():
                self._wait(E, k, v, False)

    def op(self, en, fn, reads=(), writes=(), inc=True):
        E = self.E[en]
        reads = flat(reads)
        writes = flat(writes)
        self._deps(E, reads, writes)
        ins = fn(E.eng)
        self.n_ins += 1
        stamp = E.val + 1
        if inc:
            E.ordn += 1
            E.omap.setdefault(stamp, E.ordn)
            if self.needed is None or E.ordn in self.needed[en]:
                ins.then_inc(E.sem, 1)
                E.val += 1
                E.omap[stamp] = E.ordn
                self.n_inc = getattr(self, "n_inc", 0) + 1
        for b in reads:
            if b.r.get(E, 0) < stamp:
                b.r[E] = stamp
        for b in writes:
            b.w = (E, stamp)
            b.r = {}
        return ins

    def dma(self, out, in_, reads=(), writes=(), q="sync", cast=False, slow=False):
        Q = self.E[q]
        if cast:
            slot = self.cslots[self.crr % len(self.cslots)]
            self.crr += 1
        else:
            slot = self.slots[self.rr % len(self.slots)]
            self.rr += 1
        reads = flat(reads)
        writes = flat(writes)
        self._wait(Q, slot, slot.val, True)
        self._deps(Q, reads, writes)
        ins = Q.eng.dma_start(out=out, in_=in_, allow_slow_non_contiguous=True) if slow else Q.eng.dma_start(out=out, in_=in_)
        self.n_ins += 1
        slot.val += 16
        ins.then_inc(slot.sem, 16)
        for b in reads:
            b.r[slot] = slot.val
        for b in writes:
            b.w = (slot, slot.val)
            b.r = {}
        return ins

    def barrier(self, include_cast=False):
        keys = list(self.E.values()) + self.slots + (self.cslots if include_cast else [])
        for E in self.E.values():
            for k in keys:
                if k is E:
                    continue
                self._wait(E, k, k.val, True)


def seq_tiles(S, maxlen):
    n = -(-S // maxlen)
    base = -(-S // n)
    base = -(-base // 8) * 8
    out = []
    t = 0
    while t < S:
        ln = min(base, S - t)
        out.append((t, ln))
        t += ln
    return out


def _build(groups, debug=False, needed=None):
    nc = bass.Bass("TRN2", target_bir_lowering=False)
    SM = max(S for _, _, S in groups)
    Ssizes = sorted(set(S for _, _, S in groups))
    es = ExitStack()
    K = Sched(nc, es, needed=needed)
    uid = [0]

    def din(name, shape, dt=F32):
        return nc.dram_tensor(name, list(shape), dt, kind="ExternalInput").ap()

    def dscr(name, shape, dt):
        kind = "ExternalOutput" if debug else "Internal"
        return nc.dram_tensor(name, list(shape), dt, kind=kind).ap()

    def sb(stack, name, shape, dt):
        uid[0] += 1
        return stack.enter_context(nc.sbuf_tensor("%s_%d" % (name, uid[0]), list(shape), dt))

    xin = {}
    yout = {}
    for name, n, S in groups:
        xin[name] = din("x_" + name, (n, D, S))
        yout[name] = nc.dram_tensor("y_" + name, [n, D, S], F32, kind="ExternalOutput").ap()
    norm1_g = din("norm1_g", (NLAYER, 128, 8))
    w_in = din("w_in", (NLAYER, D, NIN))
    b_gates = din("b_gates", (NLAYER, 16, 1))
    rg_conv_w = din("rg_conv_w", (NLAYER, 128, 4, 4))
    rg_conv_b = din("rg_conv_b", (NLAYER, 128, 4))
    rg_wa = din("rg_wa", (NLAYER, 2, 8, 64, 64))
    rg_ba = din("rg_ba", (128, 16))
    rg_wx = din("rg_wx", (NLAYER, 2, 8, 64, 64))
    rg_bx = din("rg_bx", (128, 16))
    rg_lambda = din("rg_lambda", (128, 16))
    ml_norm_g = din("ml_norm_g", (NLAYER, 512))
    w_out = din("w_out", (NLAYER, D, D))
    norm2_g = din("norm2_g", (NLAYER, 128, 8))
    w_up = din("w_up", (NLAYER, D, 2 * DFF))
    ffn_conv_w = din("ffn_conv_w", (NLAYER, 128, 3, 48))
    ffn_conv_b = din("ffn_conv_b", (NLAYER, 128, 48))
    w_down = din("w_down", (NLAYER, DFF, D))
    final_g = din("final_g", (128, 8))
    c_ident = din("c_ident", (128, 128))
    c_maskf = din("c_maskf", (128, 128))
    c_maskb = din("c_maskb", (128, 128))

    w_in_bf = dscr("w_in_bf", (NLAYER, D, NIN), BF16)
    w_out_bf = dscr("w_out_bf", (NLAYER, D, D), BF16)
    w_up_bf = dscr("w_up_bf", (NLAYER, D, 2 * DFF), BF16)
    w_down_bf = dscr("w_down_bf", (NLAYER, 8, 128, 24, 128), BF16)
    wrg_bf = dscr("wrg_bf", (NLAYER, 2, 2, 4, 128, 128), BF16)
    tok_wbf = {(w, l): Buf() for w in ("in", "out", "up", "down", "rg") for l in range(NLAYER)}
    seqs = []
    tb = 0
    for name, n, S in groups:
        for si in range(n):
            seqs.append(dict(name=name, si=si, S=S, base=tb, pbase=tb + 3 * len(seqs), idx=len(seqs)))
            tb += S
    TOT = tb
    rgx_all = dscr("rgx_pad", (512, TOT + 3 * len(seqs)), BF16)
    gg_all = dscr("gg_d", (512, TOT), BF16)
    qT_all = dscr("qT_d", (512, TOT), BF16)
    kT_all = dscr("kT_d", (512, TOT), BF16)
    va_all = dscr("va_d", (TOT, 516), BF16)
    og_all = dscr("og_d", (TOT, 512), BF16)
    ktm_all = dscr("ktm_d", (TOT, 512), BF16)
    mixT_all = dscr("mixT_d", (D, TOT), BF16)
    xmid_all = dscr("xmid_d", (D, TOT), F32)
    xc_d = dscr("xc_d", (512, SM), F32)
    hbrg_d = dscr("hbrg_d", (512, SM), F32)
    hbml_d = dscr("hbml_d", (SM, 512), F32)
    for q in seqs:
        b0, S = q["base"], q["S"]
        q["rgx_pad"] = rgx_all[:, q["pbase"]:q["pbase"] + S + 3]
        q["gg_d"] = gg_all[:, b0:b0 + S]
        q["qT_d"] = qT_all[:, b0:b0 + S]
        q["kT_d"] = kT_all[:, b0:b0 + S]
        q["va_d"] = va_all[b0:b0 + S, :]
        q["og_d"] = og_all[b0:b0 + S, :]
        q["ktm_d"] = ktm_all[b0:b0 + S, :]
        q["mixT_d"] = mixT_all[:, b0:b0 + S]
        q["xmid_d"] = xmid_all[:, b0:b0 + S]
        q["grow"] = dscr("grow_%d" % q["idx"], (16, S), F32)
        q["xin"] = xin[q["name"]][q["si"]]
        q["yout"] = yout[q["name"]][q["si"]]

    ps = es.enter_context(nc.psum_tensor("ps", [128, 8, 512], F32))
    bank_tok = toks(8)
    bank_rr = [0]

    def bank():
        b = bank_rr[0] % 8
        bank_rr[0] += 1
        return b

    def bank2():
        if bank_rr[0] % 2:
            bank_rr[0] += 1
        b = bank_rr[0] % 8
        bank_rr[0] += 2
        return [b, b + 1]

    def A(fn, r=(), w=()):
        return K.op("scalar", fn, r, w)

    def V(fn, r=(), w=()):
        return K.op("vector", fn, r, w)

    def G(fn, r=(), w=()):
        return K.op("gpsimd", fn, r, w)

    def PE(fn, r=(), w=(), inc=True):
        return K.op("tensor", fn, r, w, inc)

    def mmgroup(out_ap, pairs, rtoks, wtok):
        n = len(pairs)
        for i, (l_, r_) in enumerate(pairs):
            PE(lambda e, l_=l_, r_=r_, i=i: e.matmul(out_ap, lhsT=l_, rhs=r_, start=(i == 0), stop=(i == n - 1)),
               r=rtoks[i] if isinstance(rtoks, list) else rtoks, w=[wtok], inc=(i == n - 1))

    cst = ExitStack()
    es.enter_context(cst)
    ident_bf = sb(cst, "ident_bf", [128, 128], BF16)
    ident_f = sb(cst, "ident_f", [128, 128], F32)
    ones_bf = sb(cst, "ones_bf", [128, 128], BF16)
    ones_f = sb(cst, "ones_f", [128, 128], F32)
    maskf = sb(cst, "maskf", [128, 128], BF16)
    maskb = sb(cst, "maskb", [128, 128], BF16)
    neghalf = sb(cst, "neghalf", [128, 512], F32)
    zer = sb(cst, "zer", [128, 2048], BF16)
    g1c = sb(cst, "g1c", [128, NLAYER, 8], F32)
    g2c = sb(cst, "g2c", [128, NLAYER, 8], F32)
    gfc = sb(cst, "gfc", [128, 8], F32)
    rgcw = sb(cst, "rgcw", [128, NLAYER, 4, 4], F32)
    rgcb = sb(cst, "rgcb", [128, NLAYER, 4], F32)
    hba = sb(cst, "hba", [128, 16], F32)
    hbx = sb(cst, "hbx", [128, 16], F32)
    lam = sb(cst, "lam", [128, 16], F32)
    Kh = sb(cst, "Kh", [128, 16], F32)
    K2 = sb(cst, "K2", [128, 16], F32)
    ltmp = sb(cst, "ltmp", [128, 16], F32)
    ffcw = sb(cst, "ffcw", [128, NLAYER, 3, 48], F32)
    ffcb = sb(cst, "ffcb", [128, NLAYER, 48], F32)
    bg = sb(cst, "bg", [16, NLAYER], F32)
    ghalf = sb(cst, "ghalf", [128, NLAYER, 512], F32)
    epsc = sb(cst, "epsc", [128, 1], F32)
    tok_c = Buf()

    cw = [tok_c]
    K.dma(ident_f[:], c_ident[:, :], writes=cw)
    K.dma(ident_bf[:], c_ident[:, :], writes=cw, q="gpsimd", cast=True)
    K.dma(maskf[:], c_maskf[:, :], writes=cw, q="gpsimd", cast=True)
    K.dma(maskb[:], c_maskb[:, :], writes=cw, q="gpsimd", cast=True)
    for l in range(NLAYER):
        K.dma(g1c[:, l, :], norm1_g[l], writes=cw)
        K.dma(g2c[:, l, :], norm2_g[l], writes=cw)
        K.dma(rgcw[:, l, :, :], rg_conv_w[l], writes=cw)
        K.dma(rgcb[:, l, :], rg_conv_b[l], writes=cw)
        K.dma(ffcw[:, l, :, :], ffn_conv_w[l], writes=cw)
        K.dma(ffcb[:, l, :], ffn_conv_b[l], writes=cw)
        K.dma(bg[:, l:l + 1], b_gates[l], writes=cw)
        K.dma(ghalf[:, l, :], ml_norm_g[l:l + 1, :].to_broadcast([128, 512]), writes=cw)
    K.dma(hba[:], rg_ba[:, :], writes=cw)
    K.dma(hbx[:], rg_bx[:, :], writes=cw)
    K.dma(lam[:], rg_lambda[:, :], writes=cw)
    K.dma(gfc[:], final_g[:, :], writes=cw)
    V(lambda e: e.memset(ones_bf[:], 1.0), w=cw)
    V(lambda e: e.memset(ones_f[:], 1.0), w=cw)
    V(lambda e: e.memset(neghalf[:], -0.5), w=cw)
    V(lambda e: e.memset(zer[:], 0.0), w=cw)
    V(lambda e: e.memset(epsc[:], EPS), w=cw)
    V(lambda e: e.tensor_scalar_mul(out=hba[:], in0=hba[:], scalar1=0.5), r=cw, w=cw)
    V(lambda e: e.tensor_scalar_mul(out=hbx[:], in0=hbx[:], scalar1=0.5), r=cw, w=cw)
    for l in range(NLAYER):
        V(lambda e, l=l: e.tensor_scalar_mul(out=ghalf[:, l, :], in0=ghalf[:, l, :], scalar1=0.5), r=cw, w=cw)
    A(lambda e: e.activation(out=ltmp[:], in_=lam[:], func=AF.Abs), r=cw, w=cw)
    A(lambda e: e.activation(out=ltmp[:], in_=ltmp[:], func=AF.Exp, scale=-1.0), r=cw, w=cw)
    A(lambda e: e.activation(out=ltmp[:], in_=ltmp[:], func=AF.Ln, bias=1.0), r=cw, w=cw)
    V(lambda e: e.tensor_scalar(out=lam[:], in0=lam[:], scalar1=-1.0, scalar2=0.0, op0=ALU.mult, op1=ALU.max), r=cw, w=cw)
    V(lambda e: e.tensor_add(out=lam[:], in0=lam[:], in1=ltmp[:]), r=cw, w=cw)
    V(lambda e: e.tensor_scalar_mul(out=K2[:], in0=lam[:], scalar1=-8.0), r=cw, w=cw)
    V(lambda e: e.tensor_scalar_mul(out=Kh[:], in0=lam[:], scalar1=-4.0), r=cw, w=cw)

    def cast_weights(l):
        t = tok_wbf[("in", l)]
        for r0 in range(0, D, 256):
            K.dma(w_in_bf[l, r0:r0 + 256, :], w_in[l, r0:r0 + 256, :], writes=[t], q="gpsimd", cast=True)
        t = tok_wbf[("rg", l)]
        K.dma(wrg_bf[l].rearrange("d t c i o -> (d t c i) o")[:, :].rearrange("(a p) o -> p a o", p=128),
              zer[:, 0:2048].rearrange("p (a o) -> p a o", o=128), reads=cw, writes=[t])
        for d in range(2):
            for ty, wsrc in enumerate((rg_wa, rg_wx)):
                for par in range(2):
                    src = wsrc[l, d].rearrange("(c two) i o -> two c i o", two=2)[par]
                    dst = wrg_bf[l, d, ty, :, par * 64:(par + 1) * 64, par * 64:(par + 1) * 64]
                    K.dma(dst, src, writes=[t], q="gpsimd", cast=True)
        t = tok_wbf[("out", l)]
        for r0 in range(0, D, 512):
            K.dma(w_out_bf[l, r0:r0 + 512, :], w_out[l, r0:r0 + 512, :], writes=[t], q="gpsimd", cast=True)
        t = tok_wbf[("up", l)]
        for r0 in range(0, D, 128):
            K.dma(w_up_bf[l, r0:r0 + 128, :], w_up[l, r0:r0 + 128, :], writes=[t], q="gpsimd", cast=True)
        t = tok_wbf[("down", l)]
        for j in range(8):
            for c0 in range(0, 24, 8):
                src = w_down[l, c0 * 128:(c0 + 8) * 128, j * 128:(j + 1) * 128].rearrange("(c p) n -> p c n", p=128)
                K.dma(w_down_bf[l, j, :, c0:c0 + 8, :], src, writes=[t], q="gpsimd", cast=True)

    for l in range(NLAYER):
        cast_weights(l)

    def rmsnorm(xt, xtok, W, gcol, out_t, out_tok, sq, sqtok, rstd, rstd_tok, tmp, tmp_tok):
        for kt in range(KT):
            A(lambda e, kt=kt: e.activation(out=sq[:, kt, 0:W], in_=xt[:, kt, 0:W], func=AF.Square),
              r=[xtok[kt]], w=[sqtok[kt]])
        b = bank()
        mmgroup(ps[:, b, 0:W], [(ones_bf[:, :], sq[:, kt, 0:W]) for kt in range(KT)],
                [[sqtok[kt], tok_c] for kt in range(KT)], bank_tok[b])
        A(lambda e: e.activation(out=tmp[:, 0:W], in_=ps[:, b, 0:W], func=AF.Sqrt, scale=1.0 / D, bias=epsc[:, 0:1]),
          r=[bank_tok[b], tok_c], w=[tmp_tok])
        V(lambda e: e.reciprocal(out=rstd[:, 0:W], in_=tmp[:, 0:W]), r=[tmp_tok], w=[rstd_tok])
        for kt in range(KT):
            V(lambda e, kt=kt: e.scalar_tensor_tensor(out=out_t[:, kt, 0:W], in0=xt[:, kt, 0:W],
                                                      scalar=gcol[:, kt:kt + 1], in1=rstd[:, 0:W],
                                                      op0=ALU.mult, op1=ALU.mult),
              r=[xtok[kt], rstd_tok, tok_c], w=[out_tok[kt]])

    def phase_A(sq_list, l):
        flat_tiles = [(q, i) for q in sq_list for i in range(q["S"] // 512)]
        NT = len(flat_tiles)
        with ExitStack() as st:
            w_sb = sb(st, "w_in_sb", [128, KT, NIN], BF16)
            wtok = toks(KT)
            xT = [sb(st, "xT", [128, KT, 512], F32) for _ in range(2)]
            xtok = [toks(KT) for _ in range(2)]
            sq = sb(st, "sq", [128, KT, 512], BF16)
            sqtok = toks(KT)
            hTs = [sb(st, "hT", [128, KT, 512], BF16) for _ in range(2)]
            htoks = [toks(KT) for _ in range(2)]
            rstd = sb(st, "rstd", [128, 512], F32)
            tmp = sb(st, "tmp", [128, 512], F32)
            rstd_tok, tmp_tok = Buf(), Buf()
            rgx_st = [sb(st, "rgx_st", [128, 4, 512], BF16) for _ in range(2)]
            gg_st = [sb(st, "gg_st", [128, 4, 512], BF16) for _ in range(2)]
            q_st = [sb(st, "q_st", [128, 4, 512], BF16) for _ in range(2)]
            k_st = [sb(st, "k_st", [128, 4, 512], BF16) for _ in range(2)]
            va_st = [sb(st, "va_st", [128, 4, 516], BF16) for _ in range(2)]
            og_st = [sb(st, "og_st", [128, 4, 512], BF16) for _ in range(2)]
            ktm_st = [sb(st, "ktm_st", [128, 4, 512], BF16) for _ in range(2)]
            gr_st = [sb(st, "gr_st", [16, 512], F32) for _ in range(2)]
            th = sb(st, "th", [128, 512], F32)
            th_tok = Buf()
            stok = [{n: toks(4) for n in ("rgx", "gg", "q", "k", "va", "og", "ktm")} for _ in range(2)]
            grtok = [Buf(), Buf()]
            for p in range(2):
                G(lambda e, p=p: e.memset(va_st[p][:], 1.0), w=[stok[p]["va"]])
            for kt in range(KT):
                K.dma(w_sb[:, kt, :], w_in_bf[l, kt * 128:(kt + 1) * 128, :], reads=[tok_wbf[("in", l)]], writes=[wtok[kt]])
            def load_x(i):
                p = i % 2
                q_, ti = flat_tiles[i]
                xsrc = q_["xin"] if l == 0 else q_["xmid_d"]
                for h in range(2):
                    K.dma(xT[p][:, h * 4:(h + 1) * 4, :],
                          xsrc[h * 512:(h + 1) * 512, ti * 512:(ti + 1) * 512].rearrange("(k p) t -> p k t", p=128),
                          writes=[xtok[p][h * 4:(h + 1) * 4]])

            load_x(0)
            if NT > 1:
                load_x(1)
            rmsnorm(xT[0], xtok[0], 512, g1c[:, l, :], hTs[0], htoks[0], sq, sqtok, rstd, rstd_tok, tmp, tmp_tok)
            for i in range(NT):
                p = i % 2
                sv, ti_ = flat_tiles[i]
                t0 = ti_ * 512
                hT, htok = hTs[p], htoks[p]
                if i + 1 < NT:
                    rmsnorm(xT[1 - p], xtok[1 - p], 512, g1c[:, l, :], hTs[1 - p], htoks[1 - p], sq, sqtok, rstd, rstd_tok, tmp, tmp_tok)
                if i + 2 < NT:
                    load_x(i + 2)
                for cg in range(16):
                    b = bank()
                    mmgroup(ps[:, b, :], [(w_sb[:, kt, cg * 128:(cg + 1) * 128], hT[:, kt, :]) for kt in range(KT)],
                            [[wtok[kt], htok[kt]] for kt in range(KT)], bank_tok[b])
                    c = cg % 4
                    if cg < 4:
                        A(lambda e, b=b, c=c: e.copy(out=rgx_st[p][:, c, :], in_=ps[:, b, :]),
                          r=[bank_tok[b]], w=[stok[p]["rgx"][c]])
                    elif cg < 8:
                        A(lambda e, b=b, c=c: e.activation(out=gg_st[p][:, c, :], in_=ps[:, b, :], func=AF.Gelu_apprx_tanh),
                          r=[bank_tok[b]], w=[stok[p]["gg"][c]])
                    elif cg < 12:
                        A(lambda e, b=b, c=c: e.activation(out=q_st[p][:, c, :], in_=ps[:, b, :], func=AF.Copy, scale=QSCALE),
                          r=[bank_tok[b]], w=[stok[p]["q"][c]])
                    else:
                        V(lambda e, b=b, c=c: e.tensor_copy(out=k_st[p][:, c, :], in_=ps[:, b, :]),
                          r=[bank_tok[b]], w=[stok[p]["k"][c]])
                b = bank()
                mmgroup(ps[0:16, b, :], [(w_sb[:, kt, 3072:3088], hT[:, kt, :]) for kt in range(KT)],
                        [[wtok[kt], htok[kt]] for kt in range(KT)], bank_tok[b])
                A(lambda e, b=b: e.activation(out=gr_st[p][:], in_=ps[0:16, b, :], func=AF.Identity, bias=bg[:, l:l + 1]),
                  r=[bank_tok[b], tok_c], w=[grtok[p]])
                for sub in range(4):
                    ts = slice(sub * 128, (sub + 1) * 128)
                    for gi, c0 in enumerate((2048, 2560, 1536)):
                        b = bank()
                        mmgroup(ps[:, b, :], [(hT[:, kt, ts], w_sb[:, kt, c0:c0 + 512]) for kt in range(KT)],
                                [[wtok[kt], htok[kt]] for kt in range(KT)], bank_tok[b])
                        if gi == 0:
                            V(lambda e, b=b, sub=sub: e.tensor_copy(
                                out=va_st[p][:, sub, :].rearrange("p (h c) -> p h c", c=129)[:, :, 0:128],
                                in_=ps[:, b, :].rearrange("p (h c) -> p h c", c=128)),
                              r=[bank_tok[b]], w=[stok[p]["va"][sub]])
                        elif gi == 1:
                            A(lambda e, b=b: e.activation(out=th[:], in_=ps[:, b, :], func=AF.Tanh, scale=0.5),
                              r=[bank_tok[b]], w=[th_tok])
                            V(lambda e, sub=sub: e.scalar_tensor_tensor(out=og_st[p][:, sub, :], in0=th[:], scalar=1.0,
                                                                        in1=ghalf[:, l, :], op0=ALU.add, op1=ALU.mult),
                              r=[th_tok, tok_c], w=[stok[p]["og"][sub]])
                        else:
                            V(lambda e, b=b, sub=sub: e.tensor_copy(out=ktm_st[p][:, sub, :], in_=ps[:, b, :]),
                              r=[bank_tok[b]], w=[stok[p]["ktm"][sub]])
                tsl = slice(t0, t0 + 512)
                K.dma(sv["rgx_pad"][:, 2 + t0:2 + t0 + 512].rearrange("(c p) t -> p c t", p=128), rgx_st[p][:], reads=[stok[p]["rgx"]])
                K.dma(sv["gg_d"][:, tsl].rearrange("(c p) t -> p c t", p=128), gg_st[p][:], reads=[stok[p]["gg"]])
                K.dma(sv["qT_d"][:, tsl].rearrange("(c p) t -> p c t", p=128), q_st[p][:], reads=[stok[p]["q"]])
                K.dma(sv["kT_d"][:, tsl].rearrange("(c p) t -> p c t", p=128), k_st[p][:], reads=[stok[p]["k"]])
                K.dma(sv["va_d"][tsl, :].rearrange("(s p) c -> p s c", p=128), va_st[p][:], reads=[stok[p]["va"]])
                K.dma(sv["og_d"][tsl, :].rearrange("(s p) c -> p s c", p=128), og_st[p][:], reads=[stok[p]["og"]])
                K.dma(sv["ktm_d"][tsl, :].rearrange("(s p) c -> p s c", p=128), ktm_st[p][:], reads=[stok[p]["ktm"]])
                K.dma(sv["grow"][:, tsl], gr_st[p][:], reads=[grtok[p]])
            K.barrier()

    def phase_G(sv, gt_tiles):
        S = sv["S"]
        NC = S // 128
        P = 4 * NC
        WT, STT_, SCB = gt_tiles
        with ExitStack() as st:
            T = [sb(st, "gT", [P, 128], F32) for _ in range(4)]
            tk = Buf()
            for g4 in range(4):
                K.dma(T[g4][:], sv["grow"][g4 * 4:(g4 + 1) * 4, :].rearrange("g (j r) -> (g j) r", r=128), writes=[tk])
            ax = sb(st, "ax", [P, 128], F32)
            lf = sb(st, "lf", [P, 128], F32)
            bb = sb(st, "bb", [P, 128], F32)
            cc = sb(st, "cc", [P, 128], F32)
            ww = sb(st, "ww", [P, 128], F32)
            sst = sb(st, "sst", [P, 128], F32)
            cmax = sb(st, "cmax", [P, 1], F32)
            nmc_col = sb(st, "nmc_col", [P, 1], F32)
            rows = sb(st, "rows", [1, 8, 128], F32)
            for d in range(2):
                I_, F_ = T[2 * d], T[2 * d + 1]
                rev = (d == 1)

                def R(ap2):
                    return ap2[:, ::-1] if rev else ap2
                A(lambda e: e.activation(out=ax[:], in_=F_[:], func=AF.Abs), r=[tk], w=[tk])
                A(lambda e: e.activation(out=ax[:], in_=ax[:], func=AF.Exp, scale=-1.0), r=[tk], w=[tk])
                A(lambda e: e.activation(out=ax[:], in_=ax[:], func=AF.Ln, bias=1.0), r=[tk], w=[tk])
                V(lambda e: e.tensor_scalar_min(out=lf[:], in0=F_[:], scalar1=0.0), r=[tk], w=[tk])
                V(lambda e: e.tensor_sub(out=lf[:], in0=lf[:], in1=ax[:]), r=[tk], w=[tk])
                V(lambda e: e.tensor_tensor_scan(out=R(bb[:]), data0=R(ones_f[0:P, :]), data1=R(lf[:]), initial=0.0,
                                                 op0=ALU.mult, op1=ALU.add), r=[tk, tok_c], w=[tk])
                V(lambda e: e.tensor_sub(out=cc[:], in0=I_[:], in1=bb[:]), r=[tk], w=[tk])
                V(lambda e: e.reduce_max(out=cmax[:], in_=cc[:], axis=AX.X), r=[tk], w=[tk])
                bl = bb[:, 0:1] if rev else bb[:, 127:128]
                b1 = bank()
                PE(lambda e: e.transpose(out=ps[0:1, b1, 0:P], in_=cmax[:], identity=ident_f[0:P, 0:P]), r=[tk, tok_c], w=[bank_tok[b1]])
                V(lambda e: e.tensor_copy(out=rows[:, 0, 0:P], in_=ps[0:1, b1, 0:P]), r=[bank_tok[b1]], w=[tk])
                b2 = bank()
                PE(lambda e: e.transpose(out=ps[0:1, b2, 0:P], in_=bl, identity=ident_f[0:P, 0:P]), r=[tk, tok_c], w=[bank_tok[b2]])
                V(lambda e: e.tensor_copy(out=rows[:, 1, 0:P], in_=ps[0:1, b2, 0:P]), r=[bank_tok[b2]], w=[tk])
                V(lambda e: e.memset(rows[:, 3, :], 0.0), r=[tk], w=[tk])
                for g in range(4):
                    seg = slice(g * NC, (g + 1) * NC)
                    V(lambda e, seg=seg: e.tensor_tensor_scan(out=R(rows[:, 2, seg]), data0=R(rows[:, 0, seg]), data1=R(rows[:, 1, seg]),
                                                              initial=0.0, op0=ALU.max, op1=ALU.add), r=[tk], w=[tk])
                    if NC > 1:
                        if rev:
                            V(lambda e, g=g: e.tensor_copy(out=rows[:, 3, g * NC:(g + 1) * NC - 1], in_=rows[:, 2, g * NC + 1:(g + 1) * NC]), r=[tk], w=[tk])
                        else:
                            V(lambda e, g=g: e.tensor_copy(out=rows[:, 3, g * NC + 1:(g + 1) * NC], in_=rows[:, 2, g * NC:(g + 1) * NC - 1]), r=[tk], w=[tk])
                V(lambda e: e.tensor_max(out=rows[:, 4, 0:P], in0=rows[:, 3, 0:P], in1=rows[:, 0, 0:P]), r=[tk], w=[tk])
                V(lambda e: e.tensor_sub(out=rows[:, 5, 0:P], in0=rows[:, 3, 0:P], in1=rows[:, 4, 0:P]), r=[tk], w=[tk])
                A(lambda e: e.activation(out=rows[:, 5, 0:P], in_=rows[:, 5, 0:P], func=AF.Exp), r=[tk], w=[tk])
                V(lambda e: e.tensor_scalar_mul(out=rows[:, 6, 0:P], in0=rows[:, 4, 0:P], scalar1=-1.0), r=[tk], w=[tk])
                b3 = bank()
                PE(lambda e: e.matmul(ps[0:P, b3, 0:1], lhsT=rows[:, 6, 0:P], rhs=ones_f[0:1, 0:1], start=True, stop=True),
                   r=[tk, tok_c], w=[bank_tok[b3]])
                V(lambda e: e.tensor_copy(out=nmc_col[:], in_=ps[0:P, b3, 0:1]), r=[bank_tok[b3]], w=[tk])
                A(lambda e: e.activation(out=ww[:], in_=cc[:], func=AF.Exp, bias=nmc_col[:]), r=[tk], w=[tk])
                A(lambda e: e.activation(out=sst[:], in_=bb[:], func=AF.Exp, bias=nmc_col[:], scale=-1.0), r=[tk], w=[tk])
                b4 = bank()
                PE(lambda e: e.transpose(out=ps[:, b4, 0:P], in_=ww[:], identity=ident_f[0:P, 0:P]), r=[tk, tok_c], w=[bank_tok[b4]])
                V(lambda e: e.tensor_copy(out=WT[d][:, 0:P], in_=ps[:, b4, 0:P]), r=[bank_tok[b4]], w=[tk])
                b5 = bank()
                PE(lambda e: e.transpose(out=ps[:, b5, 0:P], in_=sst[:], identity=ident_f[0:P, 0:P]), r=[tk, tok_c], w=[bank_tok[b5]])
                V(lambda e: e.tensor_copy(out=STT_[d][:, 0:P], in_=ps[:, b5, 0:P]), r=[bank_tok[b5]], w=[tk])
                b6 = bank()
                PE(lambda e: e.matmul(ps[:, b6, 0:P], lhsT=ones_f[0:1, :], rhs=rows[:, 5, 0:P], start=True, stop=True),
                   r=[tk, tok_c], w=[bank_tok[b6]])
                V(lambda e: e.tensor_copy(out=SCB[d][:, 0:P], in_=ps[:, b6, 0:P]), r=[bank_tok[b6]], w=[tk])
            K.barrier()

    def phase_B(sv, l, gt_tiles):
        S = sv["S"]
        NT = S // 512
        NC = S // 128
        WT, STT_, SCB = gt_tiles
        with ExitStack() as st0:
            wrg = sb(st0, "wrg", [128, 16, 128], BF16)
            wrg_tok = Buf()
            K.dma(wrg[:], wrg_bf[l].rearrange("d t c i o -> i (d t c) o"), reads=[tok_wbf[("rg", l)]], writes=[wrg_tok])
            dgw = sb(st0, "dgw", [128, 16, 128], BF16)
            dgw_tok = Buf()
            for j in range(4):
                for c in range(4):
                    V(lambda e, j=j, c=c: e.tensor_scalar(out=dgw[:, j * 4 + c, :], in0=ident_f[:], scalar1=rgcw[:, l, j, c:c + 1], scalar2=None, op0=ALU.mult),
                      r=[tok_c], w=[dgw_tok])
            for d in (1, 0):
                rev = (d == 1)
                with ExitStack() as st:
                    if rev:
                        rgxw = [sb(st, "rgxw", [128, 4, 516], BF16) for _ in range(2)]
                        xc = [sb(st, "xc", [128, 4, 512], F32)] * 2
                    else:
                        xc = [sb(st, "xc", [128, 4, 512], F32) for _ in range(2)]
                        hbl = sb(st, "hbl", [128, 4, 512], F32)
                        ggl = sb(st, "ggl", [128, 4, 512], BF16)
                        ogl = sb(st, "ogl", [128, 4, 512], BF16)
                        hbm = sb(st, "hbm", [128, 4, 512], F32)
                        mix = sb(st, "mix", [128, 8, 512], BF16)
                        sqv = sb(st, "sqv", [128, 4, 512], F32)
                        yb = sb(st, "yb", [128, 4, 512], BF16)
                        ssum = sb(st, "ssum", [128, 16], F32)
                    qTl = [sb(st, "qTl", [128, 4, 512], BF16) for _ in range(2)]
                    kTl = [sb(st, "kTl", [128, 4, 512], BF16) for _ in range(2)]
                    val = [sb(st, "val", [128, 4, 516], BF16) for _ in range(2)]
                    ktl = [sb(st, "ktl", [128, 4, 512], BF16) for _ in range(2)]
                    xcb = sb(st, "xcb", [128, 4, 512], BF16)
                    ta = sb(st, "ta", [128, 4, 512], F32)
                    tx = sb(st, "tx", [128, 4, 512], F32)
                    aa = sb(st, "aa", [128, 4, 512], F32)
                    a2 = sb(st, "a2", [128, 4, 512], F32)
                    hh = sb(st, "hh", [128, 4, 512], F32)
                    carry = sb(st, "carry", [128, 4], F32)
                    PT = sb(st, "PT", [128, 4, 512], BF16)
                    STb = sb(st, "STb", [128, 4, 512], BF16)
                    vw = sb(st, "vw", [128, 4, 516], BF16)
                    Cst = sb(st, "Cst", [128, 516], F32)
                    Cb = sb(st, "Cb", [128, 4, 516], BF16)
                    hd = sb(st, "hd", [128, 4, 512], F32)
                    dd = sb(st, "dd", [128, 4, 4], F32)
                    names = ("rgxw", "xc", "qT", "kT", "val", "ktl")
                    tkp = [{n: toks(4) for n in names} for _ in range(2)]
                    if rev:
                        tkp[1]["xc"] = tkp[0]["xc"]
                    tk = {n: toks(4) for n in ("xcb", "ta", "tx", "aa", "a2", "hh", "hbl", "ggl", "ogl", "hbm", "hd", "PT", "vw", "Cb", "dd", "STb")}
                    mixtok = toks(8)
                    t_carry, t_C, t_sqv, t_yb, t_ss = (Buf() for _ in range(5))
                    V(lambda e: e.memset(carry[:], 0.0), w=[t_carry])
                    V(lambda e: e.memset(Cst[:], 0.0), w=[t_C])
                    order = list(range(NT - 1, -1, -1) if rev else range(NT))

                    def loads_early(i, p):
                        t0 = i * 512
                        tsl = slice(t0, t0 + 512)
                        if rev:
                            K.dma(rgxw[p][:, :, 0:515], sv["rgx_pad"][:, t0:t0 + 515].rearrange("(c p) t -> p c t", p=128), writes=[tkp[p]["rgxw"]])
                        else:
                            K.dma(xc[p][:], xc_d[:, tsl].rearrange("(c p) t -> p c t", p=128), writes=[tkp[p]["xc"]])
                        K.dma(kTl[p][:], sv["kT_d"][:, tsl].rearrange("(c p) t -> p c t", p=128), writes=[tkp[p]["kT"]])
                        K.dma(qTl[p][:], sv["qT_d"][:, tsl].rearrange("(c p) t -> p c t", p=128), writes=[tkp[p]["qT"]])
                        K.dma(val[p][:], sv["va_d"][tsl, :].rearrange("(s p) c -> p s c", p=128), writes=[tkp[p]["val"]])
                        K.dma(ktl[p][:], sv["ktm_d"][tsl, :].rearrange("(s p) c -> p s c", p=128), writes=[tkp[p]["ktl"]])

                    loads_early(order[0], 0)
                    for it, i in enumerate(order):
                        p = it % 2
                        T = tkp[p]
                        t0 = i * 512
                        tsl = slice(t0, t0 + 512)
                        if it + 1 < NT:
                            loads_early(order[it + 1], 1 - p)
                        if not rev:
                            K.dma(hbl[:], hbrg_d[:, tsl].rearrange("(c p) t -> p c t", p=128), writes=[tk["hbl"]])
                            K.dma(hbm[:], hbml_d[tsl, :].rearrange("(s p) c -> p s c", p=128), writes=[tk["hbm"]])
                            K.dma(ggl[:], sv["gg_d"][:, tsl].rearrange("(c p) t -> p c t", p=128), writes=[tk["ggl"]])
                            K.dma(ogl[:], sv["og_d"][tsl, :].rearrange("(s p) c -> p s c", p=128), writes=[tk["ogl"]])
                        xcp = xc[p]
                        if rev:
                            for c in range(4):
                                bcv = bank()
                                for j in range(4):
                                    PE(lambda e, c=c, j=j, bcv=bcv: e.matmul(ps[:, bcv, :], lhsT=dgw[:, j * 4 + c, :], rhs=rgxw[p][:, c, j:j + 512], start=(j == 0), stop=(j == 3)),
                                       r=[dgw_tok, T["rgxw"][c]], w=[bank_tok[bcv]], inc=(j == 3))
                                A(lambda e, c=c, bcv=bcv: e.activation(out=xcp[:, c, :], in_=ps[:, bcv, :], func=AF.Identity, bias=rgcb[:, l, c:c + 1]),
                                  r=[bank_tok[bcv], tok_c], w=[T["xc"][c]])
                        for c in range(4):
                            V(lambda e, c=c: e.tensor_copy(out=xcb[:, c, :], in_=xcp[:, c, :]), r=[T["xc"][c]], w=[tk["xcb"][c]])
                        mk = maskb if rev else maskf
                        def emit_s12(sub):
                            j = i * 4 + sub
                            cs = slice(sub * 128, (sub + 1) * 128)
                            bst = bank()
                            for h in range(4):
                                PE(lambda e, h=h, bst=bst, cs=cs: e.matmul(ps[:, bst, h * 128:(h + 1) * 128], lhsT=kTl[p][:, h, cs], rhs=qTl[p][:, h, cs], start=True, stop=True),
                                   r=[T["kT"][h], T["qT"][h]], w=[bank_tok[bst]], inc=(h == 3))
                            V(lambda e, bst=bst, sub=sub: e.tensor_tensor(out=PT[:, sub, :].rearrange("p (h t) -> p h t", h=4),
                                                                          in0=ps[:, bst, :].rearrange("p (h t) -> p h t", h=4),
                                                                          in1=mk[:].unsqueeze(1).to_broadcast([128, 4, 128]), op=ALU.mult),
                              r=[bank_tok[bst], tok_c], w=[tk["PT"][sub]])
                            wcols = WT[d][:, j:j + 3 * NC + 1:NC]
                            G(lambda e, sub=sub, wcols=wcols: e.tensor_tensor(out=vw[:, sub, :].rearrange("p (h c) -> p h c", h=4),
                                                                             in0=val[p][:, sub, :].rearrange("p (h c) -> p h c", h=4),
                                                                             in1=wcols.unsqueeze(2).to_broadcast([128, 4, 129]), op=ALU.mult),
                              r=[T["val"][sub]], w=[tk["vw"][sub]])
                            cb_ = bank2()
                            for h in range(4):
                                o_ap = ps[:, cb_[h // 2], (h % 2) * 129:(h % 2) * 129 + 129]
                                PE(lambda e, h=h, o_ap=o_ap, sub=sub: e.matmul(o_ap, lhsT=ktl[p][:, sub, h * 128:(h + 1) * 128], rhs=vw[:, sub, h * 129:(h + 1) * 129], start=True, stop=True),
                                   r=[T["ktl"][sub], tk["vw"][sub]], w=[bank_tok[cb_[h // 2]]], inc=(h % 2 == 1))
                            sccols = SCB[d][:, j:j + 3 * NC + 1:NC]
                            V(lambda e, sccols=sccols: e.tensor_tensor(out=Cst[:].rearrange("p (h c) -> p h c", h=4),
                                                                      in0=Cst[:].rearrange("p (h c) -> p h c", h=4),
                                                                      in1=sccols.unsqueeze(2).to_broadcast([128, 4, 129]), op=ALU.mult),
                              r=[t_C], w=[t_C])
                            V(lambda e, sub=sub: e.tensor_copy(out=Cb[:, sub, :], in_=Cst[:]), r=[t_C], w=[tk["Cb"][sub]])
                            V(lambda e, cb_=cb_: e.tensor_add(out=Cst[:].rearrange("p (b c) -> p b c", b=2), in0=Cst[:].rearrange("p (b c) -> p b c", b=2),
                                                              in1=ps[:, cb_[0]:cb_[0] + 2, 0:258]),
                              r=[t_C, bank_tok[cb_[0]], bank_tok[cb_[1]]], w=[t_C])
                        def emit_rg34(half):
                            bks = {}
                            for c in (2 * half, 2 * half + 1):
                                for ty in range(2):
                                    bb_ = bank()
                                    bks[(c, ty)] = bb_
                                    PE(lambda e, c=c, ty=ty, bb_=bb_: e.matmul(ps[:, bb_, :], lhsT=wrg[:, (d * 2 + ty) * 4 + c, :], rhs=xcb[:, c, :], start=True, stop=True),
                                       r=[wrg_tok, tk["xcb"][c]], w=[bank_tok[bb_]])
                            for c in (2 * half, 2 * half + 1):
                                col = (l * 2 + d) * 4 + c
                                A(lambda e, c=c, col=col, bb_=bks[(c, 0)]: e.activation(out=ta[:, c, :], in_=ps[:, bb_, :], func=AF.Tanh, scale=0.5, bias=hba[:, col:col + 1]),
                                  r=[bank_tok[bks[(c, 0)]], tok_c], w=[tk["ta"][c]])
                                A(lambda e, c=c, col=col, bb_=bks[(c, 1)]: e.activation(out=tx[:, c, :], in_=ps[:, bb_, :], func=AF.Tanh, scale=0.5, bias=hbx[:, col:col + 1]),
                                  r=[bank_tok[bks[(c, 1)]], tok_c], w=[tk["tx"][c]])
                            for c in (2 * half, 2 * half + 1):
                                col = (l * 2 + d) * 4 + c
                                A(lambda e, c=c, col=col: e.activation(out=aa[:, c, :], in_=ta[:, c, :], func=AF.Exp, scale=Kh[:, col:col + 1], bias=Kh[:, col:col + 1]),
                                  r=[tk["ta"][c], tok_c], w=[tk["aa"][c]])
                                A(lambda e, c=c, col=col: e.activation(out=a2[:, c, :], in_=ta[:, c, :], func=AF.Exp, scale=K2[:, col:col + 1], bias=K2[:, col:col + 1]),
                                  r=[tk["ta"][c], tok_c], w=[tk["a2"][c]])
                        subs = list(range(3, -1, -1) if rev else range(4))
                        emit_s12(subs[0])
                        emit_rg34(0)
                        emit_s12(subs[1])
                        emit_rg34(1)
                        emit_s12(subs[2])
                        emit_s12(subs[3])
                        for sub in subs:
                            j = i * 4 + sub
                            cs = slice(sub * 128, (sub + 1) * 128)
                            nb = bank2()
                            for h in range(4):
                                o_ap = ps[:, nb[h // 2], (h % 2) * 129:(h % 2) * 129 + 129]
                                PE(lambda e, h=h, o_ap=o_ap, sub=sub: e.matmul(o_ap, lhsT=PT[:, sub, h * 128:(h + 1) * 128], rhs=vw[:, sub, h * 129:(h + 1) * 129], start=True, stop=False),
                                   r=[tk["PT"][sub], tk["vw"][sub]], w=[bank_tok[nb[h // 2]]], inc=False)
                                PE(lambda e, h=h, o_ap=o_ap, sub=sub, cs=cs: e.matmul(o_ap, lhsT=qTl[p][:, h, cs], rhs=Cb[:, sub, h * 129:(h + 1) * 129], start=False, stop=True),
                                   r=[T["qT"][h], tk["Cb"][sub]], w=[bank_tok[nb[h // 2]]], inc=(h % 2 == 1))
                            stc = STT_[d][:, j:j + 3 * NC + 1:NC]
                            den_ap = ps[:, nb[0]:nb[0] + 2, 128:258:129]
                            nbt = [bank_tok[nb[0]], bank_tok[nb[1]]]
                            V(lambda e, sub=sub, den_ap=den_ap, stc=stc: e.tensor_tensor(out=dd[:, sub, :].rearrange("p (b h) -> p b h", b=2), in0=den_ap,
                                                                                      in1=stc.rearrange("p (b h) -> p b h", b=2), op=ALU.max),
                              r=nbt, w=[tk["dd"][sub]])
                            V(lambda e, sub=sub, den_ap=den_ap: e.scalar_tensor_tensor(out=dd[:, sub, :].rearrange("p (b h) -> p b h", b=2), in0=den_ap, scalar=-1.0,
                                                                                    in1=dd[:, sub, :].rearrange("p (b h) -> p b h", b=2), op0=ALU.mult, op1=ALU.max),
                              r=nbt + [tk["dd"][sub]], w=[tk["dd"][sub]])
                            V(lambda e, sub=sub: e.reciprocal(out=dd[:, sub, :], in_=dd[:, sub, :]), r=[tk["dd"][sub]], w=[tk["dd"][sub]])
                            for bi in range(2):
                                V(lambda e, bi=bi, sub=sub, nb=nb: e.tensor_tensor(
                                    out=hd[:, sub, bi * 256:(bi + 1) * 256].rearrange("p (h c) -> p h c", h=2),
                                    in0=ps[:, nb[bi], 0:258].rearrange("p (h c) -> p h c", h=2)[:, :, 0:128],
                                    in1=dd[:, sub, 2 * bi:2 * bi + 2].unsqueeze(2).to_broadcast([128, 2, 128]), op=ALU.mult),
                                  r=[bank_tok[nb[bi]], tk["dd"][sub]], w=[tk["hd"][sub]])
                        for c in range(4):
                            A(lambda e, c=c: e.activation(out=a2[:, c, :], in_=a2[:, c, :], func=AF.Sqrt, scale=-1.0, bias=1.0),
                              r=[tk["a2"][c]], w=[tk["a2"][c]])
                        for c in range(4):
                            if rev:
                                G(lambda e, c=c: e.tensor_scalar(out=tx[:, c, :], in0=tx[:, c, :], scalar1=1.0, scalar2=1.0, op0=ALU.add, op1=ALU.mult), r=[tk["tx"][c]], w=[tk["tx"][c]])
                                G(lambda e, c=c: e.tensor_mul(out=tx[:, c, :], in0=tx[:, c, :], in1=xcp[:, c, :]), r=[tk["tx"][c], T["xc"][c]], w=[tk["tx"][c]])
                            else:
                                V(lambda e, c=c: e.scalar_tensor_tensor(out=tx[:, c, :], in0=tx[:, c, :], scalar=1.0, in1=xcp[:, c, :],
                                                                        op0=ALU.add, op1=ALU.mult),
                                  r=[tk["tx"][c], T["xc"][c]], w=[tk["tx"][c]])
                        for c in range(4):
                            V(lambda e, c=c: e.scalar_tensor_tensor(out=tx[:, c, :], in0=a2[:, c, :], scalar=0.5, in1=tx[:, c, :],
                                                                    op0=ALU.mult, op1=ALU.mult),
                              r=[tk["a2"][c], tk["tx"][c]], w=[tk["tx"][c]])
                        for c in range(4):
                            if rev:
                                V(lambda e, c=c: e.tensor_tensor_scan(out=hh[:, c, ::-1], data0=aa[:, c, ::-1], data1=tx[:, c, ::-1],
                                                                      initial=carry[:, c:c + 1], op0=ALU.mult, op1=ALU.add),
                                  r=[tk["aa"][c], tk["tx"][c], t_carry], w=[tk["hh"][c]])
                            else:
                                V(lambda e, c=c: e.tensor_tensor_scan(out=hh[:, c, :], data0=aa[:, c, :], data1=tx[:, c, :],
                                                                      initial=carry[:, c:c + 1], op0=ALU.mult, op1=ALU.add),
                                  r=[tk["aa"][c], tk["tx"][c], t_carry], w=[tk["hh"][c]])
                        last = 0 if rev else 511
                        V(lambda e: e.tensor_copy(out=carry[:, :], in_=hh[:, :, last]), r=[tk["hh"]], w=[t_carry])
                        if rev:
                            K.dma(xc_d[:, tsl].rearrange("(c p) t -> p c t", p=128), xcp[:], reads=[T["xc"]])
                            K.dma(hbrg_d[:, tsl].rearrange("(c p) t -> p c t", p=128), hh[:], reads=[tk["hh"]])
                            K.dma(hbml_d[tsl, :].rearrange("(s p) c -> p s c", p=128), hd[:], reads=[tk["hd"]])
                        else:
                            for c in range(4):
                                G(lambda e, c=c: e.tensor_add(out=hh[:, c, :], in0=hh[:, c, :], in1=hbl[:, c, :]),
                                  r=[tk["hh"][c], tk["hbl"][c]], w=[tk["hh"][c]])
                            for c in range(4):
                                G(lambda e, c=c: e.tensor_mul(out=mix[:, c, :], in0=hh[:, c, :], in1=ggl[:, c, :]),
                                  r=[tk["hh"][c], tk["ggl"][c]], w=[mixtok[c]])
                            G(lambda e: e.tensor_add(out=hd[:], in0=hd[:], in1=hbm[:]), r=[tk["hd"], tk["hbm"]], w=[tk["hd"]])
                            G(lambda e: e.tensor_mul(out=sqv[:], in0=hd[:], in1=hd[:]), r=[tk["hd"]], w=[t_sqv])
                            V(lambda e: e.reduce_sum(out=ssum[:], in_=sqv[:].rearrange("p s (h c) -> p (s h) c", h=4), axis=AX.X),
                              r=[t_sqv], w=[t_ss])
                            A(lambda e: e.activation(out=ssum[:], in_=ssum[:], func=AF.Sqrt, scale=1.0 / 128, bias=epsc[:, 0:1]), r=[t_ss, tok_c], w=[t_ss])
                            V(lambda e: e.reciprocal(out=ssum[:], in_=ssum[:]), r=[t_ss], w=[t_ss])
                            V(lambda e: e.tensor_tensor(out=sqv[:].rearrange("p s (h c) -> p (s h) c", h=4),
                                                        in0=hd[:].rearrange("p s (h c) -> p (s h) c", h=4),
                                                        in1=ssum[:].unsqueeze(2).to_broadcast([128, 16, 128]), op=ALU.mult),
                              r=[tk["hd"], t_ss, t_sqv], w=[t_sqv])
                            G(lambda e: e.tensor_mul(out=yb[:], in0=sqv[:], in1=ogl[:]), r=[t_sqv, tk["ogl"]], w=[t_yb])
                            for half in range(2):
                                bt = bank()
                                pbf = ps[:, bt, :].bitcast(BF16)
                                for q in range(8):
                                    sub, h = half * 2 + q // 4, q % 4
                                    PE(lambda e, q=q, sub=sub, h=h, pbf=pbf: e.transpose(out=pbf[:, q * 128:(q + 1) * 128], in_=yb[:, sub, h * 128:(h + 1) * 128], identity=ident_bf[:]),
                                       r=[t_yb, tok_c], w=[bank_tok[bt]], inc=(q == 7))
                                for s2 in range(2):
                                    sub = half * 2 + s2
                                    A(lambda e, pbf=pbf, sub=sub, s2=s2: e.copy(out=mix[:, 4:8, sub * 128:(sub + 1) * 128],
                                                                            in_=pbf[:, s2 * 512:(s2 + 1) * 512].rearrange("p (h t) -> p h t", h=4)),
                                      r=[bank_tok[bt]], w=[mixtok[4:8]])
                            K.dma(sv["mixT_d"][:, tsl].rearrange("(k p) t -> p k t", p=128), mix[:], reads=[mixtok])
                    K.barrier()

    def phase_C(sq_list, l, final):
        with ExitStack() as st:
            wo = sb(st, "wo", [128, KT, D], BF16)
            wu = sb(st, "wu", [128, KT, 2 * DFF], BF16)
            wd = [sb(st, "wd", [128, 24, 128], BF16) for _ in range(2)]
            wo_tok, wu_tok = toks(KT), toks(KT)
            wd_tok = [Buf() for _ in range(2)]
            for kt in range(KT):
                K.dma(wo[:, kt, :], w_out_bf[l, kt * 128:(kt + 1) * 128, :], reads=[tok_wbf[("out", l)]], writes=[wo_tok[kt]])
            for kt in range(KT):
                K.dma(wu[:, kt, :], w_up_bf[l, kt * 128:(kt + 1) * 128, :], reads=[tok_wbf[("up", l)]], writes=[wu_tok[kt]])
            xt = sb(st, "xt", [128, KT, 512], F32)
            xtok = toks(KT)
            gbuf = sb(st, "gbuf", [128, 24, 512], BF16)
            gtok = toks(24)
            hT = sb(st, "hTc", [128, KT, 512], BF16)
            htok = toks(KT)
            rstd = sb(st, "rstdc", [128, 512], F32)
            tmp = sb(st, "tmpc", [128, 512], F32)
            rstd_tok, tmp_tok = Buf(), Buf()
            accg = [sb(st, "accg", [128, 512], F32) for _ in range(2)]
            accv = [sb(st, "accv", [128, 512], F32) for _ in range(2)]
            acc_tok = [[Buf(), Buf(), Buf()] for _ in range(2)]
            wdc = [0]
            all_tiles = [(q, t0, ln) for q in sq_list for (t0, ln) in seq_tiles(q["S"], 510)]
            for (sv, t0, ln) in all_tiles:
                S = sv["S"]
                xsrc = sv["xin"] if l == 0 else sv["xmid_d"]
                xdst = sv["yout"] if final else sv["xmid_d"]
                mixT_d = sv["mixT_d"]
                W = ln + 2
                lo = t0 - 1
                c0 = 1 if lo < 0 else 0
                c1 = W - 1 if lo + W > S else W
                if c0 == 1:
                    V(lambda e: e.memset(xt[:, :, 0:1], 0.0), w=[xtok])
                    V(lambda e: e.memset(gbuf[:, 0:8, 0:1], 0.0), w=[gtok[0:8]])
                if c1 == W - 1:
                    V(lambda e: e.memset(xt[:, :, W - 1:W], 0.0), w=[xtok])
                    V(lambda e: e.memset(gbuf[:, 0:8, W - 1:W], 0.0), w=[gtok[0:8]])
                for h in range(2):
                    K.dma(gbuf[:, h * 4:(h + 1) * 4, c0:c1],
                          mixT_d[h * 512:(h + 1) * 512, lo + c0:lo + c1].rearrange("(k p) t -> p k t", p=128),
                          writes=[gtok[h * 4:(h + 1) * 4]])
                for kt in range(KT):
                    K.dma(xt[:, kt, c0:c1], xsrc[kt * 128:(kt + 1) * 128, lo + c0:lo + c1], writes=[xtok[kt]])
                for jd in range(KT):
                    b = bank()
                    mmgroup(ps[:, b, 0:W], [(wo[:, kt, jd * 128:(jd + 1) * 128], gbuf[:, kt, 0:W]) for kt in range(KT)],
                            [[wo_tok[kt], gtok[kt]] for kt in range(KT)], bank_tok[b])
                    V(lambda e, b=b, jd=jd: e.tensor_add(out=xt[:, jd, 0:W], in0=xt[:, jd, 0:W], in1=ps[:, b, 0:W]),
                      r=[bank_tok[b], xtok[jd]], w=[xtok[jd]])
                rmsnorm(xt, xtok, W, g2c[:, l, :], hT, htok, gbuf[:, 8:16, :], gtok[8:16], rstd, rstd_tok, tmp, tmp_tok)
                for c in range(24):
                    pp = c % 2
                    bg_ = bank()
                    mmgroup(ps[:, bg_, 0:W], [(wu[:, kt, c * 128:(c + 1) * 128], hT[:, kt, 0:W]) for kt in range(KT)],
                            [[wu_tok[kt], htok[kt]] for kt in range(KT)], bank_tok[bg_])
                    bv_ = bank()
                    mmgroup(ps[:, bv_, 0:W], [(wu[:, kt, DFF + c * 128:DFF + (c + 1) * 128], hT[:, kt, 0:W]) for kt in range(KT)],
                            [[wu_tok[kt], htok[kt]] for kt in range(KT)], bank_tok[bv_])
                    for (bb_, acc, ti, cc_) in ((bg_, accg[pp], 0, c), (bv_, accv[pp], 1, 24 + c)):
                        A(lambda e, bb_=bb_, acc=acc, cc_=cc_: e.activation(out=acc[:, 1:W - 1], in_=ps[:, bb_, 1:W - 1], func=AF.Identity,
                                                                           scale=ffcw[:, l, 1, cc_:cc_ + 1], bias=ffcb[:, l, cc_:cc_ + 1]),
                          r=[bank_tok[bb_], tok_c], w=[acc_tok[pp][ti]])
                        V(lambda e, bb_=bb_, acc=acc, cc_=cc_: e.scalar_tensor_tensor(out=acc[:, 1:W - 1], in0=ps[:, bb_, 0:W - 2],
                                                                                     scalar=ffcw[:, l, 0, cc_:cc_ + 1], in1=acc[:, 1:W - 1],
                                                                                     op0=ALU.mult, op1=ALU.add),
                          r=[bank_tok[bb_], acc_tok[pp][ti], tok_c], w=[acc_tok[pp][ti]])
                        V(lambda e, bb_=bb_, acc=acc, cc_=cc_: e.scalar_tensor_tensor(out=acc[:, 1:W - 1], in0=ps[:, bb_, 2:W],
                                                                                     scalar=ffcw[:, l, 2, cc_:cc_ + 1], in1=acc[:, 1:W - 1],
                                                                                     op0=ALU.mult, op1=ALU.add),
                          r=[bank_tok[bb_], acc_tok[pp][ti], tok_c], w=[acc_tok[pp][ti]])
                    A(lambda e, pp=pp: e.activation(out=accg[pp][:, 1:W - 1], in_=accg[pp][:, 1:W - 1], func=AF.Gelu_apprx_tanh),
                      r=[acc_tok[pp][0]], w=[acc_tok[pp][0]])
                    G(lambda e, pp=pp, c=c: e.tensor_mul(out=gbuf[:, c, 1:W - 1], in0=accg[pp][:, 1:W - 1], in1=accv[pp][:, 1:W - 1]),
                      r=[acc_tok[pp][0], acc_tok[pp][1]], w=[gtok[c]])
                for jd in range(KT):
                    wi = wdc[0] % 2
                    wdc[0] += 1
                    K.dma(wd[wi][:], w_down_bf[l, jd], reads=[tok_wbf[("down", l)]], writes=[wd_tok[wi]])
                    b = bank()
                    mmgroup(ps[:, b, 0:ln], [(wd[wi][:, c, :], gbuf[:, c, 1:W - 1]) for c in range(24)],
                            [[wd_tok[wi], gtok[c]] for c in range(24)], bank_tok[b])
                    V(lambda e, b=b, jd=jd: e.tensor_add(out=xt[:, jd, 1:W - 1], in0=xt[:, jd, 1:W - 1], in1=ps[:, b, 0:ln]),
                      r=[bank_tok[b], xtok[jd]], w=[xtok[jd]])
                    if not final:
                        K.dma(xdst[jd * 128:(jd + 1) * 128, t0:t0 + ln], xt[:, jd, 1:W - 1], reads=[xtok[jd]], q="scalar")
                if final:
                    class _Sh:
                        def __init__(s, t): s.t = t
                        def __getitem__(s, idx):
                            a, b_, c_ = idx
                            return s.t[a, b_, slice(c_.start + 1, c_.stop + 1)]
                    rmsnorm(_Sh(xt), xtok, ln, gfc, _Sh(xt), xtok, gbuf[:, 8:16, :], gtok[8:16], rstd, rstd_tok, tmp, tmp_tok)
                    K.dma(xdst[:, t0:t0 + ln].rearrange("(k p) t -> p k t", p=128), xt[:, :, 1:W - 1], reads=[xtok])
            K.barrier()

    zf = sb(cst, "zf", [128, 4], F32)
    V(lambda e: e.memset(zf[:], 0.0), w=cw)
    for q in seqs:
        for c in range(4):
            K.dma(q["rgx_pad"][c * 128:(c + 1) * 128, 0:2], zer[:, 0:2], reads=cw)
            K.dma(q["rgx_pad"][c * 128:(c + 1) * 128, q["S"] + 2:q["S"] + 3], zer[:, 0:1], reads=cw, slow=True)
    K.barrier()

    for l in range(NLAYER):
        phase_A(seqs, l)
        for q in seqs:
            NC = q["S"] // 128
            with ExitStack() as gs:
                WT = [sb(gs, "WT", [128, 4 * NC], F32) for _ in range(2)]
                STT_ = [sb(gs, "STT", [128, 4 * NC], F32) for _ in range(2)]
                SCB = [sb(gs, "SCB", [128, 4 * NC], F32) for _ in range(2)]
                phase_G(q, (WT, STT_, SCB))
                phase_B(q, l, (WT, STT_, SCB))
        phase_C(seqs, l, final=(l == NLAYER - 1))
    K.barrier(include_cast=True)
    es.close()
    return nc, K


def build(groups, debug=False):
    _, K1 = _build(groups, debug, None)
    nc, K2 = _build(groups, debug, K1.rec)
    build.n_ins = K2.n_ins
    build.n_inc = (getattr(K1, "n_inc", 0), getattr(K2, "n_inc", 0))
    return nc


_GROUPS = [("p", 2, 4096), ("s", 4, 2048)]
_WNAMES = ["norm1_g", "w_in", "b_gates", "rg_conv_w", "rg_conv_b", "rg_wa", "rg_ba", "rg_wx", "rg_bx", "rg_lambda",
           "ml_norm_g", "w_out", "norm2_g", "w_up", "ffn_conv_w", "ffn_conv_b", "w_down", "final_g"]


def host_consts():
    s = np.arange(128)[:, None]
    t = np.arange(128)[None, :]
    return {"c_ident": np.eye(128, dtype=np.float32),
            "c_maskf": (s <= t).astype(np.float32),
            "c_maskb": (s >= t).astype(np.float32)}


def weight_map(inputs):
    m = {k: np.asarray(inputs[k], dtype=np.float32) for k in _WNAMES}
    L = NLAYER
    m["b_gates"] = m["b_gates"].reshape(L, 16, 1)
    m["norm1_g"] = m["norm1_g"].reshape(L, 8, 128).transpose(0, 2, 1)
    m["norm2_g"] = m["norm2_g"].reshape(L, 8, 128).transpose(0, 2, 1)
    m["final_g"] = m["final_g"].reshape(8, 128).transpose(1, 0)
    m["rg_conv_w"] = m["rg_conv_w"].reshape(L, 4, 4, 128).transpose(0, 3, 1, 2)
    m["rg_conv_b"] = m["rg_conv_b"].reshape(L, 4, 128).transpose(0, 2, 1)
    for k in ("rg_ba", "rg_bx", "rg_lambda"):
        m[k] = m[k].reshape(L, 2, 4, 128).transpose(3, 0, 1, 2).reshape(128, 16)
    m["ffn_conv_w"] = m["ffn_conv_w"].reshape(L, 3, 48, 128).transpose(0, 3, 1, 2)
    m["ffn_conv_b"] = m["ffn_conv_b"].reshape(L, 48, 128).transpose(0, 2, 1)
    m = {k: np.ascontiguousarray(v, dtype=np.float32) for k, v in m.items()}
    m.update(host_consts())
    return m


def kernel(**inputs):
    xp = np.asarray(inputs["x_prompt"], dtype=np.float32)
    xs = np.asarray(inputs["x_sample"], dtype=np.float32)
    n = 8
    nc = build(_GROUPS)
    wm = weight_map(inputs)
    in_maps = []
    for c in range(n):
        m = dict(wm)
        m["x_p"] = np.ascontiguousarray(xp[2 * c:2 * c + 2].transpose(0, 2, 1))
        m["x_s"] = np.ascontiguousarray(xs[4 * c:4 * c + 4].transpose(0, 2, 1))
        in_maps.append(m)
    res = run_bass_kernel_spmd(nc, in_maps, core_ids=list(range(n)))
    yp = np.concatenate([np.asarray(r["y_p"]).transpose(0, 2, 1) for r in res.results], axis=0)
    ys = np.concatenate([np.asarray(r["y_s"]).transpose(0, 2, 1) for r in res.results], axis=0)
    return (np.ascontiguousarray(yp, dtype=np.float32), np.ascontiguousarray(ys, dtype=np.float32))
```

```python
import numpy as np
from contextlib import ExitStack
import concourse.bass as bass
import concourse.mybir as mybir
from concourse.bass_utils import run_bass_kernel_spmd

F32 = mybir.dt.float32
BF16 = mybir.dt.bfloat16
AF = mybir.ActivationFunctionType
ALU = mybir.AluOpType
AX = mybir.AxisListType

D = 1024
KT = 8
NIN = 3088
DFF = 3072
EPS = 1e-6
NLAYER = 2
QSCALE = 128 ** -0.5


class Buf:
    __slots__ = ("w", "r")

    def __init__(self):
        self.w = None
        self.r = {}


def toks(*shape):
    a = np.empty(shape, dtype=object)
    for idx in np.ndindex(*shape):
        a[idx] = Buf()
    return a


def flat(*xs):
    out = []
    for x in xs:
        if isinstance(x, Buf):
            out.append(x)
        elif isinstance(x, np.ndarray):
            out.extend(x.ravel().tolist())
        else:
            for y in x:
                out.extend(flat(y))
    return out


class Key:
    def __init__(self, sem, eng=None, name=""):
        self.sem = sem
        self.eng = eng
        self.val = 0
        self.waited = {}
        self.name = name


class Sched:
    def __init__(self, nc, es, nslots=24, ncslots=8, needed=None):
        self.nc = nc
        self.needed = needed
        self.rec = {}
        self.E = {}
        for nm in ("sync", "scalar", "vector", "gpsimd", "tensor"):
            self.E[nm] = Key(es.enter_context(nc.semaphore("s_" + nm)), getattr(nc, nm), nm)
            self.E[nm].ordn = 0
            self.E[nm].omap = {}
            self.rec[nm] = set()
        self.slots = [Key(es.enter_context(nc.semaphore("d%d" % i)), name="d%d" % i) for i in range(nslots)]
        self.cslots = [Key(es.enter_context(nc.semaphore("c%d" % i)), name="c%d" % i) for i in range(ncslots)]
        self.rr = 0
        self.crr = 0
        self.n_ins = 0

    def _wait(self, E, key, val, raw):
        if val <= 0:
            return
        if key is E and E.name == "tensor":
            return
        if E.waited.get(key, 0) >= val:
            return
        E.eng.wait_ge(key.sem, val)
        E.waited[key] = val
        if key.eng is not None:
            self.rec[key.name].add(key.omap.get(val, -1))

    def _deps(self, E, reads, writes):
        for b in reads:
            if b.w is not None:
                self._wait(E, b.w[0], b.w[1], True)
        for b in writes:
            if b.w is not None:
                self._wait(E, b.w[0], b.w[1], False)
            for k, v in b.r.items():
                self._wait(E, k, v, False)

    def op(self, en, fn, reads=(), writes=(), inc=True):
        E = self.E[en]
        reads = flat(reads)
        writes = flat(writes)
        self._deps(E, reads, writes)
        ins = fn(E.eng)
        self.n_ins += 1
        stamp = E.val + 1
        if inc:
            E.ordn += 1
            E.omap.setdefault(stamp, E.ordn)
            if self.needed is None or E.ordn in self.needed[en]:
                ins.then_inc(E.sem, 1)
                E.val += 1
                E.omap[stamp] = E.ordn
                self.n_inc = getattr(self, "n_inc", 0) + 1
        for b in reads:
            if b.r.get(E, 0) < stamp:
                b.r[E] = stamp
        for b in writes:
            b.w = (E, stamp)
            b.r = {}
        return ins

    def dma(self, out, in_, reads=(), writes=(), q="sync", cast=False, slow=False):
        Q = self.E[q]
        if cast:
            slot = self.cslots[self.crr % len(self.cslots)]
            self.crr += 1
        else:
            slot = self.slots[self.rr % len(self.slots)]
            self.rr += 1
        reads = flat(reads)
        writes = flat(writes)
        self._wait(Q, slot, slot.val, True)
        self._deps(Q, reads, writes)
        ins = Q.eng.dma_start(out=out, in_=in_, allow_slow_non_contiguous=True) if slow else Q.eng.dma_start(out=out, in_=in_)
        self.n_ins += 1
        slot.val += 16
        ins.then_inc(slot.sem, 16)
        for b in reads:
            b.r[slot] = slot.val
        for b in writes:
            b.w = (slot, slot.val)
            b.r = {}
        return ins

    def barrier(self, include_cast=False):
        keys = list(self.E.values()) + self.slots + (self.cslots if include_cast else [])
        for E in self.E.values():
            for k in keys:
                if k is E:
                    continue
                self._wait(E, k, k.val, True)


def seq_tiles(S, maxlen):
    n = -(-S // maxlen)
    base = -(-S // n)
    base = -(-base // 8) * 8
    out = []
    t = 0
    while t < S:
        ln = min(base, S - t)
        out.append((t, ln))
        t += ln
    return out


def _build(groups, debug=False, needed=None):
    nc = bass.Bass("TRN2", target_bir_lowering=False)
    SM = max(S for _, _, S in groups)
    Ssizes = sorted(set(S for _, _, S in groups))
    es = ExitStack()
    K = Sched(nc, es, needed=needed)
    uid = [0]

    def din(name, shape, dt=F32):
        return nc.dram_tensor(name, list(shape), dt, kind="ExternalInput").ap()

    def dscr(name, shape, dt):
        kind = "ExternalOutput" if debug else "Internal"
        return nc.dram_tensor(name, list(shape), dt, kind=kind).ap()

    def sb(stack, name, shape, dt):
        uid[0] += 1
        return stack.enter_context(nc.sbuf_tensor("%s_%d" % (name, uid[0]), list(shape), dt))

    xin = {}
    yout = {}
    for name, n, S in groups:
        xin[name] = din("x_" + name, (n, D, S))
        yout[name] = nc.dram_tensor("y_" + name, [n, D, S], F32, kind="ExternalOutput").ap()
    norm1_g = din("norm1_g", (NLAYER, 128, 8))
    w_in = din("w_in", (NLAYER, D, NIN))
    b_gates = din("b_gates", (NLAYER, 16, 1))
    rg_conv_w = din("rg_conv_w", (NLAYER, 128, 4, 4))
    rg_conv_b = din("rg_conv_b", (NLAYER, 128, 4))
    rg_wa = din("rg_wa", (NLAYER, 2, 8, 64, 64))
    rg_ba = din("rg_ba", (128, 16))
    rg_wx = din("rg_wx", (NLAYER, 2, 8, 64, 64))
    rg_bx = din("rg_bx", (128, 16))
    rg_lambda = din("rg_lambda", (128, 16))
    ml_norm_g = din("ml_norm_g", (NLAYER, 512))
    w_out = din("w_out", (NLAYER, D, D))
    norm2_g = din("norm2_g", (NLAYER, 128, 8))
    w_up = din("w_up", (NLAYER, D, 2 * DFF))
    ffn_conv_w = din("ffn_conv_w", (NLAYER, 128, 3, 48))
    ffn_conv_b = din("ffn_conv_b", (NLAYER, 128, 48))
    w_down = din("w_down", (NLAYER, DFF, D))
    final_g = din("final_g", (128, 8))
    c_ident = din("c_ident", (128, 128))
    c_maskf = din("c_maskf", (128, 128))
    c_maskb = din("c_maskb", (128, 128))

    w_in_bf = dscr("w_in_bf", (NLAYER, D, NIN), BF16)
    w_out_bf = dscr("w_out_bf", (NLAYER, D, D), BF16)
    w_up_bf = dscr("w_up_bf", (NLAYER, D, 2 * DFF), BF16)
    w_down_bf = dscr("w_down_bf", (NLAYER, 8, 128, 24, 128), BF16)
    wrg_bf = dscr("wrg_bf", (NLAYER, 2, 2, 4, 128, 128), BF16)
    tok_wbf = {(w, l): Buf() for w in ("in", "out", "up", "down", "rg") for l in range(NLAYER)}
    seqs = []
    tb = 0
    for name, n, S in groups:
        for si in range(n):
            seqs.append(dict(name=name, si=si, S=S, base=tb, pbase=tb + 3 * len(seqs), idx=len(seqs)))
            tb += S
    TOT = tb
    rgx_all = dscr("rgx_pad", (512, TOT + 3 * len(seqs)), BF16)
    gg_all = dscr("gg_d", (512, TOT), BF16)
    qT_all = dscr("qT_d", (512, TOT), BF16)
    kT_all = dscr("kT_d", (512, TOT), BF16)
    va_all = dscr("va_d", (TOT, 516), BF16)
    og_all = dscr("og_d", (TOT, 512), BF16)
    ktm_all = dscr("ktm_d", (TOT, 512), BF16)
    mixT_all = dscr("mixT_d", (D, TOT), BF16)
    xmid_all = dscr("xmid_d", (D, TOT), F32)
    xc_d = dscr("xc_d", (512, SM), F32)
    hbrg_d = dscr("hbrg_d", (512, SM), F32)
    hbml_d = dscr("hbml_d", (SM, 512), F32)
    for q in seqs:
        b0, S = q["base"], q["S"]
        q["rgx_pad"] = rgx_all[:, q["pbase"]:q["pbase"] + S + 3]
        q["gg_d"] = gg_all[:, b0:b0 + S]
        q["qT_d"] = qT_all[:, b0:b0 + S]
        q["kT_d"] = kT_all[:, b0:b0 + S]
        q["va_d"] = va_all[b0:b0 + S, :]
        q["og_d"] = og_all[b0:b0 + S, :]
        q["ktm_d"] = ktm_all[b0:b0 + S, :]
        q["mixT_d"] = mixT_all[:, b0:b0 + S]
        q["xmid_d"] = xmid_all[:, b0:b0 + S]
        q["grow"] = dscr("grow_%d" % q["idx"], (16, S), F32)
        q["xin"] = xin[q["name"]][q["si"]]
        q["yout"] = yout[q["name"]][q["si"]]

    ps = es.enter_context(nc.psum_tensor("ps", [128, 8, 512], F32))
    bank_tok = toks(8)
    bank_rr = [0]

    def bank():
        b = bank_rr[0] % 8
        bank_rr[0] += 1
        return b

    def bank2():
        if bank_rr[0] % 2:
            bank_rr[0] += 1
        b = bank_rr[0] % 8
        bank_rr[0] += 2
        return [b, b + 1]

    def A(fn, r=(), w=()):
        return K.op("scalar", fn, r, w)

    def V(fn, r=(), w=()):
        return K.op("vector", fn, r, w)

    def G(fn, r=(), w=()):
        return K.op("gpsimd", fn, r, w)

    def PE(fn, r=(), w=(), inc=True):
        return K.op("tensor", fn, r, w, inc)

    def mmgroup(out_ap, pairs, rtoks, wtok):
        n = len(pairs)
        for i, (l_, r_) in enumerate(pairs):
            PE(lambda e, l_=l_, r_=r_, i=i: e.matmul(out_ap, lhsT=l_, rhs=r_, start=(i == 0), stop=(i == n - 1)),
               r=rtoks[i] if isinstance(rtoks, list) else rtoks, w=[wtok], inc=(i == n - 1))

    cst = ExitStack()
    es.enter_context(cst)
    ident_bf = sb(cst, "ident_bf", [128, 128], BF16)
    ident_f = sb(cst, "ident_f", [128, 128], F32)
    ones_bf = sb(cst, "ones_bf", [128, 128], BF16)
    ones_f = sb(cst, "ones_f", [128, 128], F32)
    maskf = sb(cst, "maskf", [128, 128], BF16)
    maskb = sb(cst, "maskb", [128, 128], BF16)
    neghalf = sb(cst, "neghalf", [128, 512], F32)
    zer = sb(cst, "zer", [128, 2048], BF16)
    g1c = sb(cst, "g1c", [128, NLAYER, 8], F32)
    g2c = sb(cst, "g2c", [128, NLAYER, 8], F32)
    gfc = sb(cst, "gfc", [128, 8], F32)
    rgcw = sb(cst, "rgcw", [128, NLAYER, 4, 4], F32)
    rgcb = sb(cst, "rgcb", [128, NLAYER, 4], F32)
    hba = sb(cst, "hba", [128, 16], F32)
    hbx = sb(cst, "hbx", [128, 16], F32)
    lam = sb(cst, "lam", [128, 16], F32)
    Kh = sb(cst, "Kh", [128, 16], F32)
    K2 = sb(cst, "K2", [128, 16], F32)
    ltmp = sb(cst, "ltmp", [128, 16], F32)
    ffcw = sb(cst, "ffcw", [128, NLAYER, 3, 48], F32)
    ffcb = sb(cst, "ffcb", [128, NLAYER, 48], F32)
    bg = sb(cst, "bg", [16, NLAYER], F32)
    ghalf = sb(cst, "ghalf", [128, NLAYER, 512], F32)
    epsc = sb(cst, "epsc", [128, 1], F32)
    tok_c = Buf()

    cw = [tok_c]
    K.dma(ident_f[:], c_ident[:, :], writes=cw)
    K.dma(ident_bf[:], c_ident[:, :], writes=cw, q="gpsimd", cast=True)
    K.dma(maskf[:], c_maskf[:, :], writes=cw, q="gpsimd", cast=True)
    K.dma(maskb[:], c_maskb[:, :], writes=cw, q="gpsimd", cast=True)
    for l in range(NLAYER):
        K.dma(g1c[:, l, :], norm1_g[l], writes=cw)
        K.dma(g2c[:, l, :], norm2_g[l], writes=cw)
        K.dma(rgcw[:, l, :, :], rg_conv_w[l], writes=cw)
        K.dma(rgcb[:, l, :], rg_conv_b[l], writes=cw)
        K.dma(ffcw[:, l, :, :], ffn_conv_w[l], writes=cw)
        K.dma(ffcb[:, l, :], ffn_conv_b[l], writes=cw)
        K.dma(bg[:, l:l + 1], b_gates[l], writes=cw)
        K.dma(ghalf[:, l, :], ml_norm_g[l:l + 1, :].to_broadcast([128, 512]), writes=cw)
    K.dma(hba[:], rg_ba[:, :], writes=cw)
    K.dma(hbx[:], rg_bx[:, :], writes=cw)
    K.dma(lam[:], rg_lambda[:, :], writes=cw)
    K.dma(gfc[:], final_g[:, :], writes=cw)
    V(lambda e: e.memset(ones_bf[:], 1.0), w=cw)
    V(lambda e: e.memset(ones_f[:], 1.0), w=cw)
    V(lambda e: e.memset(neghalf[:], -0.5), w=cw)
    V(lambda e: e.memset(zer[:], 0.0), w=cw)
    V(lambda e: e.memset(epsc[:], EPS), w=cw)
    V(lambda e: e.tensor_scalar_mul(out=hba[:], in0=hba[:], scalar1=0.5), r=cw, w=cw)
    V(lambda e: e.tensor_scalar_mul(out=hbx[:], in0=hbx[:], scalar1=0.5), r=cw, w=cw)
    for l in range(NLAYER):
        V(lambda e, l=l: e.tensor_scalar_mul(out=ghalf[:, l, :], in0=ghalf[:, l, :], scalar1=0.5), r=cw, w=cw)
    A(lambda e: e.activation(out=ltmp[:], in_=lam[:], func=AF.Abs), r=cw, w=cw)
    A(lambda e: e.activation(out=ltmp[:], in_=ltmp[:], func=AF.Exp, scale=-1.0), r=cw, w=cw)
    A(lambda e: e.activation(out=ltmp[:], in_=ltmp[:], func=AF.Ln, bias=1.0), r=cw, w=cw)
    V(lambda e: e.tensor_scalar(out=lam[:], in0=lam[:], scalar1=-1.0, scalar2=0.0, op0=ALU.mult, op1=ALU.max), r=cw, w=cw)
    V(lambda e: e.tensor_add(out=lam[:], in0=lam[:], in1=ltmp[:]), r=cw, w=cw)
    V(lambda e: e.tensor_scalar_mul(out=K2[:], in0=lam[:], scalar1=-8.0), r=cw, w=cw)
    V(lambda e: e.tensor_scalar_mul(out=Kh[:], in0=lam[:], scalar1=-4.0), r=cw, w=cw)

    def cast_weights(l):
        t = tok_wbf[("in", l)]
        for r0 in range(0, D, 256):
            K.dma(w_in_bf[l, r0:r0 + 256, :], w_in[l, r0:r0 + 256, :], writes=[t], q="gpsimd", cast=True)
        t = tok_wbf[("rg", l)]
        K.dma(wrg_bf[l].rearrange("d t c i o -> (d t c i) o")[:, :].rearrange("(a p) o -> p a o", p=128),
              zer[:, 0:2048].rearrange("p (a o) -> p a o", o=128), reads=cw, writes=[t])
        for d in range(2):
            for ty, wsrc in enumerate((rg_wa, rg_wx)):
                for par in range(2):
                    src = wsrc[l, d].rearrange("(c two) i o -> two c i o", two=2)[par]
                    dst = wrg_bf[l, d, ty, :, par * 64:(par + 1) * 64, par * 64:(par + 1) * 64]
                    K.dma(dst, src, writes=[t], q="gpsimd", cast=True)
        t = tok_wbf[("out", l)]
        for r0 in range(0, D, 512):
            K.dma(w_out_bf[l, r0:r0 + 512, :], w_out[l, r0:r0 + 512, :], writes=[t], q="gpsimd", cast=True)
        t = tok_wbf[("up", l)]
        for r0 in range(0, D, 128):
            K.dma(w_up_bf[l, r0:r0 + 128, :], w_up[l, r0:r0 + 128, :], writes=[t], q="gpsimd", cast=True)
        t = tok_wbf[("down", l)]
        for j in range(8):
            for c0 in range(0, 24, 8):
                src = w_down[l, c0 * 128:(c0 + 8) * 128, j * 128:(j + 1) * 128].rearrange("(c p) n -> p c n", p=128)
                K.dma(w_down_bf[l, j, :, c0:c0 + 8, :], src, writes=[t], q="gpsimd", cast=True)

    for l in range(NLAYER):
        cast_weights(l)

    def rmsnorm(xt, xtok, W, gcol, out_t, out_tok, sq, sqtok, rstd, rstd_tok, tmp, tmp_tok):
        for kt in range(KT):
            A(lambda e, kt=kt: e.activation(out=sq[:, kt, 0:W], in_=xt[:, kt, 0:W], func=AF.Square),
              r=[xtok[kt]], w=[sqtok[kt]])
        b = bank()
        mmgroup(ps[:, b, 0:W], [(ones_bf[:, :], sq[:, kt, 0:W]) for kt in range(KT)],
                [[sqtok[kt], tok_c] for kt in range(KT)], bank_tok[b])
        A(lambda e: e.activation(out=tmp[:, 0:W], in_=ps[:, b, 0:W], func=AF.Sqrt, scale=1.0 / D, bias=epsc[:, 0:1]),
          r=[bank_tok[b], tok_c], w=[tmp_tok])
        V(lambda e: e.reciprocal(out=rstd[:, 0:W], in_=tmp[:, 0:W]), r=[tmp_tok], w=[rstd_tok])
        for kt in range(KT):
            V(lambda e, kt=kt: e.scalar_tensor_tensor(out=out_t[:, kt, 0:W], in0=xt[:, kt, 0:W],
                                                      scalar=gcol[:, kt:kt + 1], in1=rstd[:, 0:W],
                                                      op0=ALU.mult, op1=ALU.mult),
              r=[xtok[kt], rstd_tok, tok_c], w=[out_tok[kt]])

    def phase_A(sq_list, l):
        flat_tiles = [(q, i) for q in sq_list for i in range(q["S"] // 512)]
        NT = len(flat_tiles)
        with ExitStack() as st:
            w_sb = sb(st, "w_in_sb", [128, KT, NIN], BF16)
            wtok = toks(KT)
            xT = [sb(st, "xT", [128, KT, 512], F32) for _ in range(2)]
            xtok = [toks(KT) for _ in range(2)]
            sq = sb(st, "sq", [128, KT, 512], BF16)
            sqtok = toks(KT)
            hTs = [sb(st, "hT", [128, KT, 512], BF16) for _ in range(2)]
            htoks = [toks(KT) for _ in range(2)]
            rstd = sb(st, "rstd", [128, 512], F32)
            tmp = sb(st, "tmp", [128, 512], F32)
            rstd_tok, tmp_tok = Buf(), Buf()
            rgx_st = [sb(st, "rgx_st", [128, 4, 512], BF16) for _ in range(2)]
            gg_st = [sb(st, "gg_st", [128, 4, 512], BF16) for _ in range(2)]
            q_st = [sb(st, "q_st", [128, 4, 512], BF16) for _ in range(2)]
            k_st = [sb(st, "k_st", [128, 4, 512], BF16) for _ in range(2)]
            va_st = [sb(st, "va_st", [128, 4, 516], BF16) for _ in range(2)]
            og_st = [sb(st, "og_st", [128, 4, 512], BF16) for _ in range(2)]
            ktm_st = [sb(st, "ktm_st", [128, 4, 512], BF16) for _ in range(2)]
            gr_st = [sb(st, "gr_st", [16, 512], F32) for _ in range(2)]
            th = sb(st, "th", [128, 512], F32)
            th_tok = Buf()
            stok = [{n: toks(4) for n in ("rgx", "gg", "q", "k", "va", "og", "ktm")} for _ in range(2)]
            grtok = [Buf(), Buf()]
            for p in range(2):
                G(lambda e, p=p: e.memset(va_st[p][:], 1.0), w=[stok[p]["va"]])
            for kt in range(KT):
                K.dma(w_sb[:, kt, :], w_in_bf[l, kt * 128:(kt + 1) * 128, :], reads=[tok_wbf[("in", l)]], writes=[wtok[kt]])
            def load_x(i):
                p = i % 2
                q_, ti = flat_tiles[i]
                xsrc = q_["xin"] if l == 0 else q_["xmid_d"]
                for h in range(2):
                    K.dma(xT[p][:, h * 4:(h + 1) * 4, :],
                          xsrc[h * 512:(h + 1) * 512, ti * 512:(ti + 1) * 512].rearrange("(k p) t -> p k t", p=128),
                          writes=[xtok[p][h * 4:(h + 1) * 4]])

            load_x(0)
            if NT > 1:
                load_x(1)
            rmsnorm(xT[0], xtok[0], 512, g1c[:, l, :], hTs[0], htoks[0], sq, sqtok, rstd, rstd_tok, tmp, tmp_tok)
            for i in range(NT):
                p = i % 2
                sv, ti_ = flat_tiles[i]
                t0 = ti_ * 512
                hT, htok = hTs[p], htoks[p]
                if i + 1 < NT:
                    rmsnorm(xT[1 - p], xtok[1 - p], 512, g1c[:, l, :], hTs[1 - p], htoks[1 - p], sq, sqtok, rstd, rstd_tok, tmp, tmp_tok)
                if i + 2 < NT:
                    load_x(i + 2)
                for cg in range(16):
                    b = bank()
                    mmgroup(ps[:, b, :], [(w_sb[:, kt, cg * 128:(cg + 1) * 128], hT[:, kt, :]) for kt in range(KT)],
                            [[wtok[kt], htok[kt]] for kt in range(KT)], bank_tok[b])
                    c = cg % 4
                    if cg < 4:
                        A(lambda e, b=b, c=c: e.copy(out=rgx_st[p][:, c, :], in_=ps[:, b, :]),
                          r=[bank_tok[b]], w=[stok[p]["rgx"][c]])
                    elif cg < 8:
                        A(lambda e, b=b, c=c: e.activation(out=gg_st[p][:, c, :], in_=ps[:, b, :], func=AF.Gelu_apprx_tanh),
                          r=[bank_tok[b]], w=[stok[p]["gg"][c]])
                    elif cg < 12:
                        A(lambda e, b=b, c=c: e.activation(out=q_st[p][:, c, :], in_=ps[:, b, :], func=AF.Copy, scale=QSCALE),
                          r=[bank_tok[b]], w=[stok[p]["q"][c]])
                    else:
                        V(lambda e, b=b, c=c: e.tensor_copy(out=k_st[p][:, c, :], in_=ps[:, b, :]),
                          r=[bank_tok[b]], w=[stok[p]["k"][c]])
                b = bank()
                mmgroup(ps[0:16, b, :], [(w_sb[:, kt, 3072:3088], hT[:, kt, :]) for kt in range(KT)],
                        [[wtok[kt], htok[kt]] for kt in range(KT)], bank_tok[b])
                A(lambda e, b=b: e.activation(out=gr_st[p][:], in_=ps[0:16, b, :], func=AF.Identity, bias=bg[:, l:l + 1]),
                  r=[bank_tok[b], tok_c], w=[grtok[p]])
                for sub in range(4):
                    ts = slice(sub * 128, (sub + 1) * 128)
                    for gi, c0 in enumerate((2048, 2560, 1536)):
                        b = bank()
                        mmgroup(ps[:, b, :], [(hT[:, kt, ts], w_sb[:, kt, c0:c0 + 512]) for kt in range(KT)],
                                [[wtok[kt], htok[kt]] for kt in range(KT)], bank_tok[b])
                        if gi == 0:
                            V(lambda e, b=b, sub=sub: e.tensor_copy(
                                out=va_st[p][:, sub, :].rearrange("p (h c) -> p h c", c=129)[:, :, 0:128],
                                in_=ps[:, b, :].rearrange("p (h c) -> p h c", c=128)),
                              r=[bank_tok[b]], w=[stok[p]["va"][sub]])
                        elif gi == 1:
                            A(lambda e, b=b: e.activation(out=th[:], in_=ps[:, b, :], func=AF.Tanh, scale=0.5),
                              r=[bank_tok[b]], w=[th_tok])
                            V(lambda e, sub=sub: e.scalar_tensor_tensor(out=og_st[p][:, sub, :], in0=th[:], scalar=1.0,
                                                                        in1=ghalf[:, l, :], op0=ALU.add, op1=ALU.mult),
                              r=[th_tok, tok_c], w=[stok[p]["og"][sub]])
                        else:
                            V(lambda e, b=b, sub=sub: e.tensor_copy(out=ktm_st[p][:, sub, :], in_=ps[:, b, :]),
                              r=[bank_tok[b]], w=[stok[p]["ktm"][sub]])
                tsl = slice(t0, t0 + 512)
                K.dma(sv["rgx_pad"][:, 2 + t0:2 + t0 + 512].rearrange("(c p) t -> p c t", p=128), rgx_st[p][:], reads=[stok[p]["rgx"]])
                K.dma(sv["gg_d"][:, tsl].rearrange("(c p) t -> p c t", p=128), gg_st[p][:], reads=[stok[p]["gg"]])
                K.dma(sv["qT_d"][:, tsl].rearrange("(c p) t -> p c t", p=128), q_st[p][:], reads=[stok[p]["q"]])
                K.dma(sv["kT_d"][:, tsl].rearrange("(c p) t -> p c t", p=128), k_st[p][:], reads=[stok[p]["k"]])
                K.dma(sv["va_d"][tsl, :].rearrange("(s p) c -> p s c", p=128), va_st[p][:], reads=[stok[p]["va"]])
                K.dma(sv["og_d"][tsl, :].rearrange("(s p) c -> p s c", p=128), og_st[p][:], reads=[stok[p]["og"]])
                K.dma(sv["ktm_d"][tsl, :].rearrange("(s p) c -> p s c", p=128), ktm_st[p][:], reads=[stok[p]["ktm"]])
                K.dma(sv["grow"][:, tsl], gr_st[p][:], reads=[grtok[p]])
            K.barrier()

    def phase_G(sv, gt_tiles):
        S = sv["S"]
        NC = S // 128
        P = 4 * NC
        WT, STT_, SCB = gt_tiles
        with ExitStack() as st:
            T = [sb(st, "gT", [P, 128], F32) for _ in range(4)]
            tk = Buf()
            for g4 in range(4):
                K.dma(T[g4][:], sv["grow"][g4 * 4:(g4 + 1) * 4, :].rearrange("g (j r) -> (g j) r", r=128), writes=[tk])
            ax = sb(st, "ax", [P, 128], F32)
            lf = sb(st, "lf", [P, 128], F32)
            bb = sb(st, "bb", [P, 128], F32)
            cc = sb(st, "cc", [P, 128], F32)
            ww = sb(st, "ww", [P, 128], F32)
            sst = sb(st, "sst", [P, 128], F32)
            cmax = sb(st, "cmax", [P, 1], F32)
            nmc_col = sb(st, "nmc_col", [P, 1], F32)
            rows = sb(st, "rows", [1, 8, 128], F32)
            for d in range(2):
                I_, F_ = T[2 * d], T[2 * d + 1]
                rev = (d == 1)

                def R(ap2):
                    return ap2[:, ::-1] if rev else ap2
                A(lambda e: e.activation(out=ax[:], in_=F_[:], func=AF.Abs), r=[tk], w=[tk])
                A(lambda e: e.activation(out=ax[:], in_=ax[:], func=AF.Exp, scale=-1.0), r=[tk], w=[tk])
                A(lambda e: e.activation(out=ax[:], in_=ax[:], func=AF.Ln, bias=1.0), r=[tk], w=[tk])
                V(lambda e: e.tensor_scalar_min(out=lf[:], in0=F_[:], scalar1=0.0), r=[tk], w=[tk])
                V(lambda e: e.tensor_sub(out=lf[:], in0=lf[:], in1=ax[:]), r=[tk], w=[tk])
                V(lambda e: e.tensor_tensor_scan(out=R(bb[:]), data0=R(ones_f[0:P, :]), data1=R(lf[:]), initial=0.0,
                                                 op0=ALU.mult, op1=ALU.add), r=[tk, tok_c], w=[tk])
                V(lambda e: e.tensor_sub(out=cc[:], in0=I_[:], in1=bb[:]), r=[tk], w=[tk])
                V(lambda e: e.reduce_max(out=cmax[:], in_=cc[:], axis=AX.X), r=[tk], w=[tk])
                bl = bb[:, 0:1] if rev else bb[:, 127:128]
                b1 = bank()
                PE(lambda e: e.transpose(out=ps[0:1, b1, 0:P], in_=cmax[:], identity=ident_f[0:P, 0:P]), r=[tk, tok_c], w=[bank_tok[b1]])
                V(lambda e: e.tensor_copy(out=rows[:, 0, 0:P], in_=ps[0:1, b1, 0:P]), r=[bank_tok[b1]], w=[tk])
                b2 = bank()
                PE(lambda e: e.transpose(out=ps[0:1, b2, 0:P], in_=bl, identity=ident_f[0:P, 0:P]), r=[tk, tok_c], w=[bank_tok[b2]])
                V(lambda e: e.tensor_copy(out=rows[:, 1, 0:P], in_=ps[0:1, b2, 0:P]), r=[bank_tok[b2]], w=[tk])
                V(lambda e: e.memset(rows[:, 3, :], 0.0), r=[tk], w=[tk])
                for g in range(4):
                    seg = slice(g * NC, (g + 1) * NC)
                    V(lambda e, seg=seg: e.tensor_tensor_scan(out=R(rows[:, 2, seg]), data0=R(rows[:, 0, seg]), data1=R(rows[:, 1, seg]),
                                                              initial=0.0, op0=ALU.max, op1=ALU.add), r=[tk], w=[tk])
                    if NC > 1:
                        if rev:
                            V(lambda e, g=g: e.tensor_copy(out=rows[:, 3, g * NC:(g + 1) * NC - 1], in_=rows[:, 2, g * NC + 1:(g + 1) * NC]), r=[tk], w=[tk])
                        else:
                            V(lambda e, g=g: e.tensor_copy(out=rows[:, 3, g * NC + 1:(g + 1) * NC], in_=rows[:, 2, g * NC:(g + 1) * NC - 1]), r=[tk], w=[tk])
                V(lambda e: e.tensor_max(out=rows[:, 4, 0:P], in0=rows[:, 3, 0:P], in1=rows[:, 0, 0:P]), r=[tk], w=[tk])
                V(lambda e: e.tensor_sub(out=rows[:, 5, 0:P], in0=rows[:, 3, 0:P], in1=rows[:, 4, 0:P]), r=[tk], w=[tk])
                A(lambda e: e.activation(out=rows[:, 5, 0:P], in_=rows[:, 5, 0:P], func=AF.Exp), r=[tk], w=[tk])
                V(lambda e: e.tensor_scalar_mul(out=rows[:, 6, 0:P], in0=rows[:, 4, 0:P], scalar1=-1.0), r=[tk], w=[tk])
                b3 = bank()
                PE(lambda e: e.matmul(ps[0:P, b3, 0:1], lhsT=rows[:, 6, 0:P], rhs=ones_f[0:1, 0:1], start=True, stop=True),
                   r=[tk, tok_c], w=[bank_tok[b3]])
                V(lambda e: e.tensor_copy(out=nmc_col[:], in_=ps[0:P, b3, 0:1]), r=[bank_tok[b3]], w=[tk])
                A(lambda e: e.activation(out=ww[:], in_=cc[:], func=AF.Exp, bias=nmc_col[:]), r=[tk], w=[tk])
                A(lambda e: e.activation(out=sst[:], in_=bb[:], func=AF.Exp, bias=nmc_col[:], scale=-1.0), r=[tk], w=[tk])
                b4 = bank()
                PE(lambda e: e.transpose(out=ps[:, b4, 0:P], in_=ww[:], identity=ident_f[0:P, 0:P]), r=[tk, tok_c], w=[bank_tok[b4]])
                V(lambda e: e.tensor_copy(out=WT[d][:, 0:P], in_=ps[:, b4, 0:P]), r=[bank_tok[b4]], w=[tk])
                b5 = bank()
                PE(lambda e: e.transpose(out=ps[:, b5, 0:P], in_=sst[:], identity=ident_f[0:P, 0:P]), r=[tk, tok_c], w=[bank_tok[b5]])
                V(lambda e: e.tensor_copy(out=STT_[d][:, 0:P], in_=ps[:, b5, 0:P]), r=[bank_tok[b5]], w=[tk])
                b6 = bank()
                PE(lambda e: e.matmul(ps[:, b6, 0:P], lhsT=ones_f[0:1, :], rhs=rows[:, 5, 0:P], start=True, stop=True),
                   r=[tk, tok_c], w=[bank_tok[b6]])
                V(lambda e: e.tensor_copy(out=SCB[d][:, 0:P], in_=ps[:, b6, 0:P]), r=[bank_tok[b6]], w=[tk])
            K.barrier()

    def phase_B(sv, l, gt_tiles):
        S = sv["S"]
        NT = S // 512
        NC = S // 128
        WT, STT_, SCB = gt_tiles
        with ExitStack() as st0:
            wrg = sb(st0, "wrg", [128, 16, 128], BF16)
            wrg_tok = Buf()
            K.dma(wrg[:], wrg_bf[l].rearrange("d t c i o -> i (d t c) o"), reads=[tok_wbf[("rg", l)]], writes=[wrg_tok])
            dgw = sb(st0, "dgw", [128, 16, 128], BF16)
            dgw_tok = Buf()
            for j in range(4):
                for c in range(4):
                    V(lambda e, j=j, c=c: e.tensor_scalar(out=dgw[:, j * 4 + c, :], in0=ident_f[:], scalar1=rgcw[:, l, j, c:c + 1], scalar2=None, op0=ALU.mult),
                      r=[tok_c], w=[dgw_tok])
            for d in (1, 0):
                rev = (d == 1)
                with ExitStack() as st:
                    if rev:
                        rgxw = [sb(st, "rgxw", [128, 4, 516], BF16) for _ in range(2)]
                        xc = [sb(st, "xc", [128, 4, 512], F32)] * 2
                    else:
                        xc = [sb(st, "xc", [128, 4, 512], F32) for _ in range(2)]
                        hbl = sb(st, "hbl", [128, 4, 512], F32)
                        ggl = sb(st, "ggl", [128, 4, 512], BF16)
                        ogl = sb(st, "ogl", [128, 4, 512], BF16)
                        hbm = sb(st, "hbm", [128, 4, 512], F32)
                        mix = sb(st, "mix", [128, 8, 512], BF16)
                        sqv = sb(st, "sqv", [128, 4, 512], F32)
                        yb = sb(st, "yb", [128, 4, 512], BF16)
                        ssum = sb(st, "ssum", [128, 16], F32)
                    qTl = [sb(st, "qTl", [128, 4, 512], BF16) for _ in range(2)]
                    kTl = [sb(st, "kTl", [128, 4, 512], BF16) for _ in range(2)]
                    val = [sb(st, "val", [128, 4, 516], BF16) for _ in range(2)]
                    ktl = [sb(st, "ktl", [128, 4, 512], BF16) for _ in range(2)]
                    xcb = sb(st, "xcb", [128, 4, 512], BF16)
                    ta = sb(st, "ta", [128, 4, 512], F32)
                    tx = sb(st, "tx", [128, 4, 512], F32)
                    aa = sb(st, "aa", [128, 4, 512], F32)
                    a2 = sb(st, "a2", [128, 4, 512], F32)
                    hh = sb(st, "hh", [128, 4, 512], F32)
                    carry = sb(st, "carry", [128, 4], F32)
                    PT = sb(st, "PT", [128, 4, 512], BF16)
                    STb = sb(st, "STb", [128, 4, 512], BF16)
                    vw = sb(st, "vw", [128, 4, 516], BF16)
                    Cst = sb(st, "Cst", [128, 516], F32)
                    Cb = sb(st, "Cb", [128, 4, 516], BF16)
                    hd = sb(st, "hd", [128, 4, 512], F32)
                    dd = sb(st, "dd", [128, 4, 4], F32)
                    names = ("rgxw", "xc", "qT", "kT", "val", "ktl")
                    tkp = [{n: toks(4) for n in names} for _ in range(2)]
                    if rev:
                        tkp[1]["xc"] = tkp[0]["xc"]
                    tk = {n: toks(4) for n in ("xcb", "ta", "tx", "aa", "a2", "hh", "hbl", "ggl", "ogl", "hbm", "hd", "PT", "vw", "Cb", "dd", "STb")}
                    mixtok = toks(8)
                    t_carry, t_C, t_sqv, t_yb, t_ss = (Buf() for _ in range(5))
                    V(lambda e: e.memset(carry[:], 0.0), w=[t_carry])
                    V(lambda e: e.memset(Cst[:], 0.0), w=[t_C])
                    order = list(range(NT - 1, -1, -1) if rev else range(NT))

                    def loads_early(i, p):
                        t0 = i * 512
                        tsl = slice(t0, t0 + 512)
                        if rev:
                            K.dma(rgxw[p][:, :, 0:515], sv["rgx_pad"][:, t0:t0 + 515].rearrange("(c p) t -> p c t", p=128), writes=[tkp[p]["rgxw"]])
                        else:
                            K.dma(xc[p][:], xc_d[:, tsl].rearrange("(c p) t -> p c t", p=128), writes=[tkp[p]["xc"]])
                        K.dma(kTl[p][:], sv["kT_d"][:, tsl].rearrange("(c p) t -> p c t", p=128), writes=[tkp[p]["kT"]])
                        K.dma(qTl[p][:], sv["qT_d"][:, tsl].rearrange("(c p) t -> p c t", p=128), writes=[tkp[p]["qT"]])
                        K.dma(val[p][:], sv["va_d"][tsl, :].rearrange("(s p) c -> p s c", p=128), writes=[tkp[p]["val"]])
                        K.dma(ktl[p][:], sv["ktm_d"][tsl, :].rearrange("(s p) c -> p s c", p=128), writes=[tkp[p]["ktl"]])

                    loads_early(order[0], 0)
                    for it, i in enumerate(order):
                        p = it % 2
                        T = tkp[p]
                        t0 = i * 512
                        tsl = slice(t0, t0 + 512)
                        if it + 1 < NT:
                            loads_early(order[it + 1], 1 - p)
                        if not rev:
                            K.dma(hbl[:], hbrg_d[:, tsl].rearrange("(c p) t -> p c t", p=128), writes=[tk["hbl"]])
                            K.dma(hbm[:], hbml_d[tsl, :].rearrange("(s p) c -> p s c", p=128), writes=[tk["hbm"]])
                            K.dma(ggl[:], sv["gg_d"][:, tsl].rearrange("(c p) t -> p c t", p=128), writes=[tk["ggl"]])
                            K.dma(ogl[:], sv["og_d"][tsl, :].rearrange("(s p) c -> p s c", p=128), writes=[tk["ogl"]])
                        xcp = xc[p]
                        if rev:
                            for c in range(4):
                                bcv = bank()
                                for j in range(4):
                                    PE(lambda e, c=c, j=j, bcv=bcv: e.matmul(ps[:, bcv, :], lhsT=dgw[:, j * 4 + c, :], rhs=rgxw[p][:, c, j:j + 512], start=(j == 0), stop=(j == 3)),
                                       r=[dgw_tok, T["rgxw"][c]], w=[bank_tok[bcv]], inc=(j == 3))
                                A(lambda e, c=c, bcv=bcv: e.activation(out=xcp[:, c, :], in_=ps[:, bcv, :], func=AF.Identity, bias=rgcb[:, l, c:c + 1]),
                                  r=[bank_tok[bcv], tok_c], w=[T["xc"][c]])
                        for c in range(4):
                            V(lambda e, c=c: e.tensor_copy(out=xcb[:, c, :], in_=xcp[:, c, :]), r=[T["xc"][c]], w=[tk["xcb"][c]])
                        mk = maskb if rev else maskf
                        def emit_s12(sub):
                            j = i * 4 + sub
                            cs = slice(sub * 128, (sub + 1) * 128)
                            bst = bank()
                            for h in range(4):
                                PE(lambda e, h=h, bst=bst, cs=cs: e.matmul(ps[:, bst, h * 128:(h + 1) * 128], lhsT=kTl[p][:, h, cs], rhs=qTl[p][:, h, cs], start=True, stop=True),
                                   r=[T["kT"][h], T["qT"][h]], w=[bank_tok[bst]], inc=(h == 3))
                            V(lambda e, bst=bst, sub=sub: e.tensor_tensor(out=PT[:, sub, :].rearrange("p (h t) -> p h t", h=4),
                                                                          in0=ps[:, bst, :].rearrange("p (h t) -> p h t", h=4),
                                                                          in1=mk[:].unsqueeze(1).to_broadcast([128, 4, 128]), op=ALU.mult),
                              r=[bank_tok[bst], tok_c], w=[tk["PT"][sub]])
                            wcols = WT[d][:, j:j + 3 * NC + 1:NC]
                            G(lambda e, sub=sub, wcols=wcols: e.tensor_tensor(out=vw[:, sub, :].rearrange("p (h c) -> p h c", h=4),
                                                                             in0=val[p][:, sub, :].rearrange("p (h c) -> p h c", h=4),
                                                                             in1=wcols.unsqueeze(2).to_broadcast([128, 4, 129]), op=ALU.mult),
                              r=[T["val"][sub]], w=[tk["vw"][sub]])
                            cb_ = bank2()
                            for h in range(4):
                                o_ap = ps[:, cb_[h // 2], (h % 2) * 129:(h % 2) * 129 + 129]
                                PE(lambda e, h=h, o_ap=o_ap, sub=sub: e.matmul(o_ap, lhsT=ktl[p][:, sub, h * 128:(h + 1) * 128], rhs=vw[:, sub, h * 129:(h + 1) * 129], start=True, stop=True),
                                   r=[T["ktl"][sub], tk["vw"][sub]], w=[bank_tok[cb_[h // 2]]], inc=(h % 2 == 1))
                            sccols = SCB[d][:, j:j + 3 * NC + 1:NC]
                            V(lambda e, sccols=sccols: e.tensor_tensor(out=Cst[:].rearrange("p (h c) -> p h c", h=4),
                                                                      in0=Cst[:].rearrange("p (h c) -> p h c", h=4),
                                                                      in1=sccols.unsqueeze(2).to_broadcast([128, 4, 129]), op=ALU.mult),
                              r=[t_C], w=[t_C])
                            V(lambda e, sub=sub: e.tensor_copy(out=Cb[:, sub, :], in_=Cst[:]), r=[t_C], w=[tk["Cb"][sub]])
                            V(lambda e, cb_=cb_: e.tensor_add(out=Cst[:].rearrange("p (b c) -> p b c", b=2), in0=Cst[:].rearrange("p (b c) -> p b c", b=2),
                                                              in1=ps[:, cb_[0]:cb_[0] + 2, 0:258]),
                              r=[t_C, bank_tok[cb_[0]], bank_tok[cb_[1]]], w=[t_C])
                        def emit_rg34(half):
                            bks = {}
                            for c in (2 * half, 2 * half + 1):
                                for ty in range(2):
                                    bb_ = bank()
                                    bks[(c, ty)] = bb_
                                    PE(lambda e, c=c, ty=ty, bb_=bb_: e.matmul(ps[:, bb_, :], lhsT=wrg[:, (d * 2 + ty) * 4 + c, :], rhs=xcb[:, c, :], start=True, stop=True),
                                       r=[wrg_tok, tk["xcb"][c]], w=[bank_tok[bb_]])
                            for c in (2 * half, 2 * half + 1):
                                col = (l * 2 + d) * 4 + c
                                A(lambda e, c=c, col=col, bb_=bks[(c, 0)]: e.activation(out=ta[:, c, :], in_=ps[:, bb_, :], func=AF.Tanh, scale=0.5, bias=hba[:, col:col + 1]),
                                  r=[bank_tok[bks[(c, 0)]], tok_c], w=[tk["ta"][c]])
                                A(lambda e, c=c, col=col, bb_=bks[(c, 1)]: e.activation(out=tx[:, c, :], in_=ps[:, bb_, :], func=AF.Tanh, scale=0.5, bias=hbx[:, col:col + 1]),
                                  r=[bank_tok[bks[(c, 1)]], tok_c], w=[tk["tx"][c]])
                            for c in (2 * half, 2 * half + 1):
                                col = (l * 2 + d) * 4 + c
                                A(lambda e, c=c, col=col: e.activation(out=aa[:, c, :], in_=ta[:, c, :], func=AF.Exp, scale=Kh[:, col:col + 1], bias=Kh[:, col:col + 1]),
                                  r=[tk["ta"][c], tok_c], w=[tk["aa"][c]])
                                A(lambda e, c=c, col=col: e.activation(out=a2[:, c, :], in_=ta[:, c, :], func=AF.Exp, scale=K2[:, col:col + 1], bias=K2[:, col:col + 1]),
                                  r=[tk["ta"][c], tok_c], w=[tk["a2"][c]])
                        subs = list(range(3, -1, -1) if rev else range(4))
                        emit_s12(subs[0])
                        emit_rg34(0)
                        emit_s12(subs[1])
                        emit_rg34(1)
                        emit_s12(subs[2])
                        emit_s12(subs[3])
                        for sub in subs:
                            j = i * 4 + sub
                            cs = slice(sub * 128, (sub + 1) * 128)
                            nb = bank2()
                            for h in range(4):
                                o_ap = ps[:, nb[h // 2], (h % 2) * 129:(h % 2) * 129 + 129]
                                PE(lambda e, h=h, o_ap=o_ap, sub=sub: e.matmul(o_ap, lhsT=PT[:, sub, h * 128:(h + 1) * 128], rhs=vw[:, sub, h * 129:(h + 1) * 129], start=True, stop=False),
                                   r=[tk["PT"][sub], tk["vw"][sub]], w=[bank_tok[nb[h // 2]]], inc=False)
                                PE(lambda e, h=h, o_ap=o_ap, sub=sub, cs=cs: e.matmul(o_ap, lhsT=qTl[p][:, h, cs], rhs=Cb[:, sub, h * 129:(h + 1) * 129], start=False, stop=True),
                                   r=[T["qT"][h], tk["Cb"][sub]], w=[bank_tok[nb[h // 2]]], inc=(h % 2 == 1))
                            stc = STT_[d][:, j:j + 3 * NC + 1:NC]
                            den_ap = ps[:, nb[0]:nb[0] + 2, 128:258:129]
                            nbt = [bank_tok[nb[0]], bank_tok[nb[1]]]
                            V(lambda e, sub=sub, den_ap=den_ap, stc=stc: e.tensor_tensor(out=dd[:, sub, :].rearrange("p (b h) -> p b h", b=2), in0=den_ap,
                                                                                      in1=stc.rearrange("p (b h) -> p b h", b=2), op=ALU.max),
                              r=nbt, w=[tk["dd"][sub]])
                            V(lambda e, sub=sub, den_ap=den_ap: e.scalar_tensor_tensor(out=dd[:, sub, :].rearrange("p (b h) -> p b h", b=2), in0=den_ap, scalar=-1.0,
                                                                                    in1=dd[:, sub, :].rearrange("p (b h) -> p b h", b=2), op0=ALU.mult, op1=ALU.max),
                              r=nbt + [tk["dd"][sub]], w=[tk["dd"][sub]])
                            V(lambda e, sub=sub: e.reciprocal(out=dd[:, sub, :], in_=dd[:, sub, :]), r=[tk["dd"][sub]], w=[tk["dd"][sub]])
                            for bi in range(2):
                                V(lambda e, bi=bi, sub=sub, nb=nb: e.tensor_tensor(
                                    out=hd[:, sub, bi * 256:(bi + 1) * 256].rearrange("p (h c) -> p h c", h=2),
                                    in0=ps[:, nb[bi], 0:258].rearrange("p (h c) -> p h c", h=2)[:, :, 0:128],
                                    in1=dd[:, sub, 2 * bi:2 * bi + 2].unsqueeze(2).to_broadcast([128, 2, 128]), op=ALU.mult),
                                  r=[bank_tok[nb[bi]], tk["dd"][sub]], w=[tk["hd"][sub]])
                        for c in range(4):
                            A(lambda e, c=c: e.activation(out=a2[:, c, :], in_=a2[:, c, :], func=AF.Sqrt, scale=-1.0, bias=1.0 + 1e-6),
                              r=[tk["a2"][c]], w=[tk["a2"][c]])
                        for c in range(4):
                            if rev:
                                G(lambda e, c=c: e.tensor_scalar(out=tx[:, c, :], in0=tx[:, c, :], scalar1=1.0, scalar2=1.0, op0=ALU.add, op1=ALU.mult), r=[tk["tx"][c]], w=[tk["tx"][c]])
                                G(lambda e, c=c: e.tensor_mul(out=tx[:, c, :], in0=tx[:, c, :], in1=xcp[:, c, :]), r=[tk["tx"][c], T["xc"][c]], w=[tk["tx"][c]])
                            else:
                                V(lambda e, c=c: e.scalar_tensor_tensor(out=tx[:, c, :], in0=tx[:, c, :], scalar=1.0, in1=xcp[:, c, :],
                                                                        op0=ALU.add, op1=ALU.mult),
                                  r=[tk["tx"][c], T["xc"][c]], w=[tk["tx"][c]])
                        for c in range(4):
                            V(lambda e, c=c: e.scalar_tensor_tensor(out=tx[:, c, :], in0=a2[:, c, :], scalar=0.5, in1=tx[:, c, :],
                                                                    op0=ALU.mult, op1=ALU.mult),
                              r=[tk["a2"][c], tk["tx"][c]], w=[tk["tx"][c]])
                        for c in range(4):
                            if rev:
                                V(lambda e, c=c: e.tensor_tensor_scan(out=hh[:, c, ::-1], data0=aa[:, c, ::-1], data1=tx[:, c, ::-1],
                                                                      initial=carry[:, c:c + 1], op0=ALU.mult, op1=ALU.add),
                                  r=[tk["aa"][c], tk["tx"][c], t_carry], w=[tk["hh"][c]])
                            else:
                                V(lambda e, c=c: e.tensor_tensor_scan(out=hh[:, c, :], data0=aa[:, c, :], data1=tx[:, c, :],
                                                                      initial=carry[:, c:c + 1], op0=ALU.mult, op1=ALU.add),
                                  r=[tk["aa"][c], tk["tx"][c], t_carry], w=[tk["hh"][c]])
                        last = 0 if rev else 511
                        V(lambda e: e.tensor_copy(out=carry[:, :], in_=hh[:, :, last]), r=[tk["hh"]], w=[t_carry])
                        if rev:
                            K.dma(xc_d[:, tsl].rearrange("(c p) t -> p c t", p=128), xcp[:], reads=[T["xc"]])
                            K.dma(hbrg_d[:, tsl].rearrange("(c p) t -> p c t", p=128), hh[:], reads=[tk["hh"]])
                            K.dma(hbml_d[tsl, :].rearrange("(s p) c -> p s c", p=128), hd[:], reads=[tk["hd"]])
                        else:
                            for c in range(4):
                                G(lambda e, c=c: e.tensor_add(out=hh[:, c, :], in0=hh[:, c, :], in1=hbl[:, c, :]),
                                  r=[tk["hh"][c], tk["hbl"][c]], w=[tk["hh"][c]])
                            for c in range(4):
                                G(lambda e, c=c: e.tensor_mul(out=mix[:, c, :], in0=hh[:, c, :], in1=ggl[:, c, :]),
                                  r=[tk["hh"][c], tk["ggl"][c]], w=[mixtok[c]])
                            G(lambda e: e.tensor_add(out=hd[:], in0=hd[:], in1=hbm[:]), r=[tk["hd"], tk["hbm"]], w=[tk["hd"]])
                            G(lambda e: e.tensor_mul(out=sqv[:], in0=hd[:], in1=hd[:]), r=[tk["hd"]], w=[t_sqv])
                            V(lambda e: e.reduce_sum(out=ssum[:], in_=sqv[:].rearrange("p s (h c) -> p (s h) c", h=4), axis=AX.X),
                              r=[t_sqv], w=[t_ss])
                            A(lambda e: e.activation(out=ssum[:], in_=ssum[:], func=AF.Sqrt, scale=1.0 / 128, bias=epsc[:, 0:1]), r=[t_ss, tok_c], w=[t_ss])
                            V(lambda e: e.reciprocal(out=ssum[:], in_=ssum[:]), r=[t_ss], w=[t_ss])
                            V(lambda e: e.tensor_tensor(out=sqv[:].rearrange("p s (h c) -> p (s h) c", h=4),
                                                        in0=hd[:].rearrange("p s (h c) -> p (s h) c", h=4),
                                                        in1=ssum[:].unsqueeze(2).to_broadcast([128, 16, 128]), op=ALU.mult),
                              r=[tk["hd"], t_ss, t_sqv], w=[t_sqv])
                            G(lambda e: e.tensor_mul(out=yb[:], in0=sqv[:], in1=ogl[:]), r=[t_sqv, tk["ogl"]], w=[t_yb])
                            for half in range(2):
                                bt = bank()
                                pbf = ps[:, bt, :].bitcast(BF16)
                                for q in range(8):
                                    sub, h = half * 2 + q // 4, q % 4
                                    PE(lambda e, q=q, sub=sub, h=h, pbf=pbf: e.transpose(out=pbf[:, q * 128:(q + 1) * 128], in_=yb[:, sub, h * 128:(h + 1) * 128], identity=ident_bf[:]),
                                       r=[t_yb, tok_c], w=[bank_tok[bt]], inc=(q == 7))
                                for s2 in range(2):
                                    sub = half * 2 + s2
                                    A(lambda e, pbf=pbf, sub=sub, s2=s2: e.copy(out=mix[:, 4:8, sub * 128:(sub + 1) * 128],
                                                                            in_=pbf[:, s2 * 512:(s2 + 1) * 512].rearrange("p (h t) -> p h t", h=4)),
                                      r=[bank_tok[bt]], w=[mixtok[4:8]])
                            K.dma(sv["mixT_d"][:, tsl].rearrange("(k p) t -> p k t", p=128), mix[:], reads=[mixtok])
                    K.barrier()

    def phase_C(sq_list, l, final):
        with ExitStack() as st:
            wo = sb(st, "wo", [128, KT, D], BF16)
            wu = sb(st, "wu", [128, KT, 2 * DFF], BF16)
            wd = [sb(st, "wd", [128, 24, 128], BF16) for _ in range(2)]
            wo_tok, wu_tok = toks(KT), toks(KT)
            wd_tok = [Buf() for _ in range(2)]
            for kt in range(KT):
                K.dma(wo[:, kt, :], w_out_bf[l, kt * 128:(kt + 1) * 128, :], reads=[tok_wbf[("out", l)]], writes=[wo_tok[kt]])
            for kt in range(KT):
                K.dma(wu[:, kt, :], w_up_bf[l, kt * 128:(kt + 1) * 128, :], reads=[tok_wbf[("up", l)]], writes=[wu_tok[kt]])
            xt = sb(st, "xt", [128, KT, 512], F32)
            xtok = toks(KT)
            gbuf = sb(st, "gbuf", [128, 24, 512], BF16)
            gtok = toks(24)
            hT = sb(st, "hTc", [128, KT, 512], BF16)
            htok = toks(KT)
            rstd = sb(st, "rstdc", [128, 512], F32)
            tmp = sb(st, "tmpc", [128, 512], F32)
            rstd_tok, tmp_tok = Buf(), Buf()
            accg = [sb(st, "accg", [128, 512], F32) for _ in range(2)]
            accv = [sb(st, "accv", [128, 512], F32) for _ in range(2)]
            acc_tok = [[Buf(), Buf(), Buf()] for _ in range(2)]
            wdc = [0]
            all_tiles = [(q, t0, ln) for q in sq_list for (t0, ln) in seq_tiles(q["S"], 510)]
            for (sv, t0, ln) in all_tiles:
                S = sv["S"]
                xsrc = sv["xin"] if l == 0 else sv["xmid_d"]
                xdst = sv["yout"] if final else sv["xmid_d"]
                mixT_d = sv["mixT_d"]
                W = ln + 2
                lo = t0 - 1
                c0 = 1 if lo < 0 else 0
                c1 = W - 1 if lo + W > S else W
                if c0 == 1:
                    V(lambda e: e.memset(xt[:, :, 0:1], 0.0), w=[xtok])
                    V(lambda e: e.memset(gbuf[:, 0:8, 0:1], 0.0), w=[gtok[0:8]])
                if c1 == W - 1:
                    V(lambda e: e.memset(xt[:, :, W - 1:W], 0.0), w=[xtok])
                    V(lambda e: e.memset(gbuf[:, 0:8, W - 1:W], 0.0), w=[gtok[0:8]])
                for h in range(2):
                    K.dma(gbuf[:, h * 4:(h + 1) * 4, c0:c1],
                          mixT_d[h * 512:(h + 1) * 512, lo + c0:lo + c1].rearrange("(k p) t -> p k t", p=128),
                          writes=[gtok[h * 4:(h + 1) * 4]])
                for kt in range(KT):
                    K.dma(xt[:, kt, c0:c1], xsrc[kt * 128:(kt + 1) * 128, lo + c0:lo + c1], writes=[xtok[kt]])
                for jd in range(KT):
                    b = bank()
                    mmgroup(ps[:, b, 0:W], [(wo[:, kt, jd * 128:(jd + 1) * 128], gbuf[:, kt, 0:W]) for kt in range(KT)],
                            [[wo_tok[kt], gtok[kt]] for kt in range(KT)], bank_tok[b])
                    V(lambda e, b=b, jd=jd: e.tensor_add(out=xt[:, jd, 0:W], in0=xt[:, jd, 0:W], in1=ps[:, b, 0:W]),
                      r=[bank_tok[b], xtok[jd]], w=[xtok[jd]])
                rmsnorm(xt, xtok, W, g2c[:, l, :], hT, htok, gbuf[:, 8:16, :], gtok[8:16], rstd, rstd_tok, tmp, tmp_tok)
                for c in range(24):
                    pp = c % 2
                    bg_ = bank()
                    mmgroup(ps[:, bg_, 0:W], [(wu[:, kt, c * 128:(c + 1) * 128], hT[:, kt, 0:W]) for kt in range(KT)],
                            [[wu_tok[kt], htok[kt]] for kt in range(KT)], bank_tok[bg_])
                    bv_ = bank()
                    mmgroup(ps[:, bv_, 0:W], [(wu[:, kt, DFF + c * 128:DFF + (c + 1) * 128], hT[:, kt, 0:W]) for kt in range(KT)],
                            [[wu_tok[kt], htok[kt]] for kt in range(KT)], bank_tok[bv_])
                    for (bb_, acc, ti, cc_) in ((bg_, accg[pp], 0, c), (bv_, accv[pp], 1, 24 + c)):
                        A(lambda e, bb_=bb_, acc=acc, cc_=cc_: e.activation(out=acc[:, 1:W - 1], in_=ps[:, bb_, 1:W - 1], func=AF.Identity,
                                                                           scale=ffcw[:, l, 1, cc_:cc_ + 1], bias=ffcb[:, l, cc_:cc_ + 1]),
                          r=[bank_tok[bb_], tok_c], w=[acc_tok[pp][ti]])
                        V(lambda e, bb_=bb_, acc=acc, cc_=cc_: e.scalar_tensor_tensor(out=acc[:, 1:W - 1], in0=ps[:, bb_, 0:W - 2],
                                                                                     scalar=ffcw[:, l, 0, cc_:cc_ + 1], in1=acc[:, 1:W - 1],
                                                                                     op0=ALU.mult, op1=ALU.add),
                          r=[bank_tok[bb_], acc_tok[pp][ti], tok_c], w=[acc_tok[pp][ti]])
                        V(lambda e, bb_=bb_, acc=acc, cc_=cc_: e.scalar_tensor_tensor(out=acc[:, 1:W - 1], in0=ps[:, bb_, 2:W],
                                                                                     scalar=ffcw[:, l, 2, cc_:cc_ + 1], in1=acc[:, 1:W - 1],
                                                                                     op0=ALU.mult, op1=ALU.add),
                          r=[bank_tok[bb_], acc_tok[pp][ti], tok_c], w=[acc_tok[pp][ti]])
                    A(lambda e, pp=pp: e.activation(out=accg[pp][:, 1:W - 1], in_=accg[pp][:, 1:W - 1], func=AF.Gelu_apprx_tanh),
                      r=[acc_tok[pp][0]], w=[acc_tok[pp][0]])
                    G(lambda e, pp=pp, c=c: e.tensor_mul(out=gbuf[:, c, 1:W - 1], in0=accg[pp][:, 1:W - 1], in1=accv[pp][:, 1:W - 1]),
                      r=[acc_tok[pp][0], acc_tok[pp][1]], w=[gtok[c]])
                for jd in range(KT):
                    wi = wdc[0] % 2
                    wdc[0] += 1
                    K.dma(wd[wi][:], w_down_bf[l, jd], reads=[tok_wbf[("down", l)]], writes=[wd_tok[wi]])
                    b = bank()
                    mmgroup(ps[:, b, 0:ln], [(wd[wi][:, c, :], gbuf[:, c, 1:W - 1]) for c in range(24)],
                            [[wd_tok[wi], gtok[c]] for c in range(24)], bank_tok[b])
                    V(lambda e, b=b, jd=jd: e.tensor_add(out=xt[:, jd, 1:W - 1], in0=xt[:, jd, 1:W - 1], in1=ps[:, b, 0:ln]),
                      r=[bank_tok[b], xtok[jd]], w=[xtok[jd]])
                    if not final:
                        K.dma(xdst[jd * 128:(jd + 1) * 128, t0:t0 + ln], xt[:, jd, 1:W - 1], reads=[xtok[jd]], q="scalar")
                if final:
                    class _Sh:
                        def __init__(s, t): s.t = t
                        def __getitem__(s, idx):
                            a, b_, c_ = idx
                            return s.t[a, b_, slice(c_.start + 1, c_.stop + 1)]
                    rmsnorm(_Sh(xt), xtok, ln, gfc, _Sh(xt), xtok, gbuf[:, 8:16, :], gtok[8:16], rstd, rstd_tok, tmp, tmp_tok)
                    K.dma(xdst[:, t0:t0 + ln].rearrange("(k p) t -> p k t", p=128), xt[:, :, 1:W - 1], reads=[xtok])
            K.barrier()

    zf = sb(cst, "zf", [128, 4], F32)
    V(lambda e: e.memset(zf[:], 0.0), w=cw)
    for q in seqs:
        for c in range(4):
            K.dma(q["rgx_pad"][c * 128:(c + 1) * 128, 0:2], zer[:, 0:2], reads=cw)
            K.dma(q["rgx_pad"][c * 128:(c + 1) * 128, q["S"] + 2:q["S"] + 3], zer[:, 0:1], reads=cw, slow=True)
    K.barrier()

    for l in range(NLAYER):
        phase_A(seqs, l)
        for q in seqs:
            NC = q["S"] // 128
            with ExitStack() as gs:
                WT = [sb(gs, "WT", [128, 4 * NC], F32) for _ in range(2)]
                STT_ = [sb(gs, "STT", [128, 4 * NC], F32) for _ in range(2)]
                SCB = [sb(gs, "SCB", [128, 4 * NC], F32) for _ in range(2)]
                phase_G(q, (WT, STT_, SCB))
                phase_B(q, l, (WT, STT_, SCB))
        phase_C(seqs, l, final=(l == NLAYER - 1))
    K.barrier(include_cast=True)
    es.close()
    return nc, K


def build(groups, debug=False):
    _, K1 = _build(groups, debug, None)
    nc, K2 = _build(groups, debug, K1.rec)
    build.n_ins = K2.n_ins
    build.n_inc = (getattr(K1, "n_inc", 0), getattr(K2, "n_inc", 0))
    return nc


_GROUPS = [("p", 2, 4096), ("s", 4, 2048)]
_WNAMES = ["norm1_g", "w_in", "b_gates", "rg_conv_w", "rg_conv_b", "rg_wa", "rg_ba", "rg_wx", "rg_bx", "rg_lambda",
           "ml_norm_g", "w_out", "norm2_g", "w_up", "ffn_conv_w", "ffn_conv_b", "w_down", "final_g"]


def host_consts():
    s = np.arange(128)[:, None]
    t = np.arange(128)[None, :]
    return {"c_ident": np.eye(128, dtype=np.float32),
            "c_maskf": (s <= t).astype(np.float32),
            "c_maskb": (s >= t).astype(np.float32)}


def weight_map(inputs):
    m = {k: np.asarray(inputs[k], dtype=np.float32) for k in _WNAMES}
    L = NLAYER
    m["b_gates"] = m["b_gates"].reshape(L, 16, 1)
    m["norm1_g"] = m["norm1_g"].reshape(L, 8, 128).transpose(0, 2, 1)
    m["norm2_g"] = m["norm2_g"].reshape(L, 8, 128).transpose(0, 2, 1)
    m["final_g"] = m["final_g"].reshape(8, 128).transpose(1, 0)
    m["rg_conv_w"] = m["rg_conv_w"].reshape(L, 4, 4, 128).transpose(0, 3, 1, 2)
    m["rg_conv_b"] = m["rg_conv_b"].reshape(L, 4, 128).transpose(0, 2, 1)
    for k in ("rg_ba", "rg_bx", "rg_lambda"):
        m[k] = m[k].reshape(L, 2, 4, 128).transpose(3, 0, 1, 2).reshape(128, 16)
    m["ffn_conv_w"] = m["ffn_conv_w"].reshape(L, 3, 48, 128).transpose(0, 3, 1, 2)
    m["ffn_conv_b"] = m["ffn_conv_b"].reshape(L, 48, 128).transpose(0, 2, 1)
    m = {k: np.ascontiguousarray(v, dtype=np.float32) for k, v in m.items()}
    m.update(host_consts())
    return m


def kernel(**inputs):
    xp = np.asarray(inputs["x_prompt"], dtype=np.float32)
    xs = np.asarray(inputs["x_sample"], dtype=np.float32)
    n = 8
    nc = build(_GROUPS)
    wm = weight_map(inputs)
    in_maps = []
    for c in range(n):
        m = dict(wm)
        m["x_p"] = np.ascontiguousarray(xp[2 * c:2 * c + 2].transpose(0, 2, 1))
        m["x_s"] = np.ascontiguousarray(xs[4 * c:4 * c + 4].transpose(0, 2, 1))
        in_maps.append(m)
    res = run_bass_kernel_spmd(nc, in_maps, core_ids=list(range(n)))
    yp = np.concatenate([np.asarray(r["y_p"]).transpose(0, 2, 1) for r in res.results], axis=0)
    ys = np.concatenate([np.asarray(r["y_s"]).transpose(0, 2, 1) for r in res.results], axis=0)
    return (np.ascontiguousarray(yp, dtype=np.float32), np.ascontiguousarray(ys, dtype=np.float32))
```

```python
import numpy as np
from contextlib import ExitStack
import concourse.bass as bass
import concourse.mybir as mybir
from concourse.bass_utils import run_bass_kernel_spmd

F32 = mybir.dt.float32
BF16 = mybir.dt.bfloat16
AF = mybir.ActivationFunctionType
ALU = mybir.AluOpType
AX = mybir.AxisListType

D = 1024
KT = 8
NIN = 3088
DFF = 3072
EPS = 1e-6
NLAYER = 2
QSCALE = 128 ** -0.5


class Buf:
    __slots__ = ("w", "r")

    def __init__(self):
        self.w = None
        self.r = {}


def toks(*shape):
    a = np.empty(shape, dtype=object)
    for idx in np.ndindex(*shape):
        a[idx] = Buf()
    return a


def flat(*xs):
    out = []
    for x in xs:
        if isinstance(x, Buf):
            out.append(x)
        elif isinstance(x, np.ndarray):
            out.extend(x.ravel().tolist())
        else:
            for y in x:
                out.extend(flat(y))
    return out


class Key:
    def __init__(self, sem, eng=None, name=""):
        self.sem = sem
        self.eng = eng
        self.val = 0
        self.waited = {}
        self.name = name


class Sched:
    def __init__(self, nc, es, nslots=24, ncslots=8, needed=None):
        self.nc = nc
        self.needed = needed
        self.rec = {}
        self.E = {}
        for nm in ("sync", "scalar", "vector", "gpsimd", "tensor"):
            self.E[nm] = Key(es.enter_context(nc.semaphore("s_" + nm)), getattr(nc, nm), nm)
            self.E[nm].ordn = 0
            self.E[nm].omap = {}
            self.rec[nm] = set()
        self.slots = [Key(es.enter_context(nc.semaphore("d%d" % i)), name="d%d" % i) for i in range(nslots)]
        self.cslots = [Key(es.enter_context(nc.semaphore("c%d" % i)), name="c%d" % i) for i in range(ncslots)]
        self.rr = 0
        self.crr = 0
        self.n_ins = 0

    def _wait(self, E, key, val, raw):
        if val <= 0:
            return
        if key is E and E.name == "tensor":
            return
        if E.waited.get(key, 0) >= val:
            return
        E.eng.wait_ge(key.sem, val)
        E.waited[key] = val
        if key.eng is not None:
            self.rec[key.name].add(key.omap.get(val, -1))

    def _deps(self, E, reads, writes):
        for b in reads:
            if b.w is not None:
                self._wait(E, b.w[0], b.w[1], True)
        for b in writes:
            if b.w is not None:
                self._wait(E, b.w[0], b.w[1], False)
            for k, v in b.r.items():
                self._wait(E, k, v, False)

    def op(self, en, fn, reads=(), writes=(), inc=True):
        E = self.E[en]
        reads = flat(reads)
        writes = flat(writes)
        self._deps(E, reads, writes)
        ins = fn(E.eng)
        self.n_ins += 1
        stamp = E.val + 1
        if inc:
            E.ordn += 1
            E.omap.setdefault(stamp, E.ordn)
            if self.needed is None or E.ordn in self.needed[en]:
                ins.then_inc(E.sem, 1)
                E.val += 1
                E.omap[stamp] = E.ordn
                self.n_inc = getattr(self, "n_inc", 0) + 1
        for b in reads:
            if b.r.get(E, 0) < stamp:
                b.r[E] = stamp
        for b in writes:
            b.w = (E, stamp)
            b.r = {}
        return ins

    def dma(self, out, in_, reads=(), writes=(), q="sync", cast=False, slow=False):
        Q = self.E[q]
        if cast:
            slot = self.cslots[self.crr % len(self.cslots)]
            self.crr += 1
        else:
            slot = self.slots[self.rr % len(self.slots)]
            self.rr += 1
        reads = flat(reads)
        writes = flat(writes)
        self._wait(Q, slot, slot.val, True)
        self._deps(Q, reads, writes)
        ins = Q.eng.dma_start(out=out, in_=in_, allow_slow_non_contiguous=True) if slow else Q.eng.dma_start(out=out, in_=in_)
        self.n_ins += 1
        slot.val += 16
        ins.then_inc(slot.sem, 16)
        for b in reads:
            b.r[slot] = slot.val
        for b in writes:
            b.w = (slot, slot.val)
            b.r = {}
        return ins

    def barrier(self, include_cast=False):
        keys = list(self.E.values()) + self.slots + (self.cslots if include_cast else [])
        for E in self.E.values():
            for k in keys:
                if k is E:
                    continue
                self._wait(E, k, k.val, True)


def seq_tiles(S, maxlen):
    n = -(-S // maxlen)
    base = -(-S // n)
    base = -(-base // 8) * 8
    out = []
    t = 0
    while t < S:
        ln = min(base, S - t)
        out.append((t, ln))
        t += ln
    return out


def _build(groups, debug=False, needed=None):
    nc = bass.Bass("TRN2", target_bir_lowering=False)
    SM = max(S for _, _, S in groups)
    Ssizes = sorted(set(S for _, _, S in groups))
    es = ExitStack()
    K = Sched(nc, es, needed=needed)
    uid = [0]

    def din(name, shape, dt=F32):
        return nc.dram_tensor(name, list(shape), dt, kind="ExternalInput").ap()

    def dscr(name, shape, dt):
        kind = "ExternalOutput" if debug else "Internal"
        return nc.dram_tensor(name, list(shape), dt, kind=kind).ap()

    def sb(stack, name, shape, dt):
        uid[0] += 1
        return stack.enter_context(nc.sbuf_tensor("%s_%d" % (name, uid[0]), list(shape), dt))

    xin = {}
    yout = {}
    for name, n, S in groups:
        xin[name] = din("x_" + name, (n, D, S))
        yout[name] = nc.dram_tensor("y_" + name, [n, D, S], F32, kind="ExternalOutput").ap()
    norm1_g = din("norm1_g", (NLAYER, 128, 8))
    w_in = din("w_in", (NLAYER, D, NIN))
    b_gates = din("b_gates", (NLAYER, 16, 1))
    rg_conv_w = din("rg_conv_w", (NLAYER, 128, 4, 4))
    rg_conv_b = din("rg_conv_b", (NLAYER, 128, 4))
    rg_wa = din("rg_wa", (NLAYER, 2, 8, 64, 64))
    rg_ba = din("rg_ba", (128, 16))
    rg_wx = din("rg_wx", (NLAYER, 2, 8, 64, 64))
    rg_bx = din("rg_bx", (128, 16))
    rg_lambda = din("rg_lambda", (128, 16))
    ml_norm_g = din("ml_norm_g", (NLAYER, 512))
    w_out = din("w_out", (NLAYER, D, D))
    norm2_g = din("norm2_g", (NLAYER, 128, 8))
    w_up = din("w_up", (NLAYER, D, 2 * DFF))
    ffn_conv_w = din("ffn_conv_w", (NLAYER, 128, 3, 48))
    ffn_conv_b = din("ffn_conv_b", (NLAYER, 128, 48))
    w_down = din("w_down", (NLAYER, DFF, D))
    final_g = din("final_g", (128, 8))
    c_ident = din("c_ident", (128, 128))
    c_maskf = din("c_maskf", (128, 128))
    c_maskb = din("c_maskb", (128, 128))

    w_in_bf = dscr("w_in_bf", (NLAYER, D, NIN), BF16)
    w_out_bf = dscr("w_out_bf", (NLAYER, D, D), BF16)
    w_up_bf = dscr("w_up_bf", (NLAYER, D, 2 * DFF), BF16)
    w_down_bf = dscr("w_down_bf", (NLAYER, 8, 128, 24, 128), BF16)
    wrg_bf = dscr("wrg_bf", (NLAYER, 2, 2, 4, 128, 128), BF16)
    tok_wbf = {(w, l): Buf() for w in ("in", "out", "up", "down", "rg") for l in range(NLAYER)}
    seqs = []
    tb = 0
    for name, n, S in groups:
        for si in range(n):
            seqs.append(dict(name=name, si=si, S=S, base=tb, pbase=tb + 3 * len(seqs), idx=len(seqs)))
            tb += S
    TOT = tb
    rgx_all = dscr("rgx_pad", (512, TOT + 3 * len(seqs)), BF16)
    gg_all = dscr("gg_d", (512, TOT), BF16)
    qT_all = dscr("qT_d", (512, TOT), BF16)
    kT_all = dscr("kT_d", (512, TOT), BF16)
    va_all = dscr("va_d", (TOT, 516), BF16)
    og_all = dscr("og_d", (TOT, 512), BF16)
    ktm_all = dscr("ktm_d", (TOT, 512), BF16)
    mixT_all = dscr("mixT_d", (D, TOT), BF16)
    xmid_all = dscr("xmid_d", (D, TOT), F32)
    xc_d = dscr("xc_d", (512, SM), F32)
    hbrg_d = dscr("hbrg_d", (512, SM), F32)
    hbml_d = dscr("hbml_d", (SM, 512), F32)
    for q in seqs:
        b0, S = q["base"], q["S"]
        q["rgx_pad"] = rgx_all[:, q["pbase"]:q["pbase"] + S + 3]
        q["gg_d"] = gg_all[:, b0:b0 + S]
        q["qT_d"] = qT_all[:, b0:b0 + S]
        q["kT_d"] = kT_all[:, b0:b0 + S]
        q["va_d"] = va_all[b0:b0 + S, :]
        q["og_d"] = og_all[b0:b0 + S, :]
        q["ktm_d"] = ktm_all[b0:b0 + S, :]
        q["mixT_d"] = mixT_all[:, b0:b0 + S]
        q["xmid_d"] = xmid_all[:, b0:b0 + S]
        q["grow"] = dscr("grow_%d" % q["idx"], (16, S), F32)
        q["xin"] = xin[q["name"]][q["si"]]
        q["yout"] = yout[q["name"]][q["si"]]

    ps = es.enter_context(nc.psum_tensor("ps", [128, 8, 512], F32))
    bank_tok = toks(8)
    bank_rr = [0]

    def bank():
        b = bank_rr[0] % 8
        bank_rr[0] += 1
        return b

    def bank2():
        if bank_rr[0] % 2:
            bank_rr[0] += 1
        b = bank_rr[0] % 8
        bank_rr[0] += 2
        return [b, b + 1]

    def A(fn, r=(), w=()):
        return K.op("scalar", fn, r, w)

    def V(fn, r=(), w=()):
        return K.op("vector", fn, r, w)

    def G(fn, r=(), w=()):
        return K.op("gpsimd", fn, r, w)

    def PE(fn, r=(), w=(), inc=True):
        return K.op("tensor", fn, r, w, inc)

    def mmgroup(out_ap, pairs, rtoks, wtok):
        n = len(pairs)
        for i, (l_, r_) in enumerate(pairs):
            PE(lambda e, l_=l_, r_=r_, i=i: e.matmul(out_ap, lhsT=l_, rhs=r_, start=(i == 0), stop=(i == n - 1)),
               r=rtoks[i] if isinstance(rtoks, list) else rtoks, w=[wtok], inc=(i == n - 1))

    cst = ExitStack()
    es.enter_context(cst)
    ident_bf = sb(cst, "ident_bf", [128, 128], BF16)
    ident_f = sb(cst, "ident_f", [128, 128], F32)
    ones_bf = sb(cst, "ones_bf", [128, 128], BF16)
    ones_f = sb(cst, "ones_f", [128, 128], F32)
    maskf = sb(cst, "maskf", [128, 128], BF16)
    maskb = sb(cst, "maskb", [128, 128], BF16)
    neghalf = sb(cst, "neghalf", [128, 512], F32)
    zer = sb(cst, "zer", [128, 2048], BF16)
    g1c = sb(cst, "g1c", [128, NLAYER, 8], F32)
    g2c = sb(cst, "g2c", [128, NLAYER, 8], F32)
    gfc = sb(cst, "gfc", [128, 8], F32)
    rgcw = sb(cst, "rgcw", [128, NLAYER, 4, 4], F32)
    rgcb = sb(cst, "rgcb", [128, NLAYER, 4], F32)
    hba = sb(cst, "hba", [128, 16], F32)
    hbx = sb(cst, "hbx", [128, 16], F32)
    lam = sb(cst, "lam", [128, 16], F32)
    Kh = sb(cst, "Kh", [128, 16], F32)
    K2 = sb(cst, "K2", [128, 16], F32)
    ltmp = sb(cst, "ltmp", [128, 16], F32)
    ffcw = sb(cst, "ffcw", [128, NLAYER, 3, 48], F32)
    ffcb = sb(cst, "ffcb", [128, NLAYER, 48], F32)
    bg = sb(cst, "bg", [16, NLAYER], F32)
    ghalf = sb(cst, "ghalf", [128, NLAYER, 512], F32)
    epsc = sb(cst, "epsc", [128, 1], F32)
    tok_c = Buf()

    cw = [tok_c]
    K.dma(ident_f[:], c_ident[:, :], writes=cw)
    K.dma(ident_bf[:], c_ident[:, :], writes=cw, q="gpsimd", cast=True)
    K.dma(maskf[:], c_maskf[:, :], writes=cw, q="gpsimd", cast=True)
    K.dma(maskb[:], c_maskb[:, :], writes=cw, q="gpsimd", cast=True)
    for l in range(NLAYER):
        K.dma(g1c[:, l, :], norm1_g[l], writes=cw)
        K.dma(g2c[:, l, :], norm2_g[l], writes=cw)
        K.dma(rgcw[:, l, :, :], rg_conv_w[l], writes=cw)
        K.dma(rgcb[:, l, :], rg_conv_b[l], writes=cw)
        K.dma(ffcw[:, l, :, :], ffn_conv_w[l], writes=cw)
        K.dma(ffcb[:, l, :], ffn_conv_b[l], writes=cw)
        K.dma(bg[:, l:l + 1], b_gates[l], writes=cw)
        K.dma(ghalf[:, l, :], ml_norm_g[l:l + 1, :].to_broadcast([128, 512]), writes=cw)
    K.dma(hba[:], rg_ba[:, :], writes=cw)
    K.dma(hbx[:], rg_bx[:, :], writes=cw)
    K.dma(lam[:], rg_lambda[:, :], writes=cw)
    K.dma(gfc[:], final_g[:, :], writes=cw)
    V(lambda e: e.memset(ones_bf[:], 1.0), w=cw)
    V(lambda e: e.memset(ones_f[:], 1.0), w=cw)
    V(lambda e: e.memset(neghalf[:], -0.5), w=cw)
    V(lambda e: e.memset(zer[:], 0.0), w=cw)
    V(lambda e: e.memset(epsc[:], EPS), w=cw)
    V(lambda e: e.tensor_scalar_mul(out=hba[:], in0=hba[:], scalar1=0.5), r=cw, w=cw)
    V(lambda e: e.tensor_scalar_mul(out=hbx[:], in0=hbx[:], scalar1=0.5), r=cw, w=cw)
    for l in range(NLAYER):
        V(lambda e, l=l: e.tensor_scalar_mul(out=ghalf[:, l, :], in0=ghalf[:, l, :], scalar1=0.5), r=cw, w=cw)
    A(lambda e: e.activation(out=ltmp[:], in_=lam[:], func=AF.Abs), r=cw, w=cw)
    A(lambda e: e.activation(out=ltmp[:], in_=ltmp[:], func=AF.Exp, scale=-1.0), r=cw, w=cw)
    A(lambda e: e.activation(out=ltmp[:], in_=ltmp[:], func=AF.Ln, bias=1.0), r=cw, w=cw)
    V(lambda e: e.tensor_scalar(out=lam[:], in0=lam[:], scalar1=-1.0, scalar2=0.0, op0=ALU.mult, op1=ALU.max), r=cw, w=cw)
    V(lambda e: e.tensor_add(out=lam[:], in0=lam[:], in1=ltmp[:]), r=cw, w=cw)
    V(lambda e: e.tensor_scalar_mul(out=K2[:], in0=lam[:], scalar1=-8.0), r=cw, w=cw)
    V(lambda e: e.tensor_scalar_mul(out=Kh[:], in0=lam[:], scalar1=-4.0), r=cw, w=cw)

    def cast_weights(l):
        t = tok_wbf[("in", l)]
        if l > 0:
            for r0 in range(0, D, 256):
                K.dma(w_in_bf[l, r0:r0 + 256, :], w_in[l, r0:r0 + 256, :], writes=[t], q="gpsimd", cast=True)
        t = tok_wbf[("rg", l)]
        K.dma(wrg_bf[l].rearrange("d t c i o -> (d t c i) o")[:, :].rearrange("(a p) o -> p a o", p=128),
              zer[:, 0:2048].rearrange("p (a o) -> p a o", o=128), reads=cw, writes=[t])
        for d in range(2):
            for ty, wsrc in enumerate((rg_wa, rg_wx)):
                for par in range(2):
                    src = wsrc[l, d].rearrange("(c two) i o -> two c i o", two=2)[par]
                    dst = wrg_bf[l, d, ty, :, par * 64:(par + 1) * 64, par * 64:(par + 1) * 64]
                    K.dma(dst, src, writes=[t], q="gpsimd", cast=True)
        t = tok_wbf[("out", l)]
        for r0 in range(0, D, 512):
            K.dma(w_out_bf[l, r0:r0 + 512, :], w_out[l, r0:r0 + 512, :], writes=[t], q="gpsimd", cast=True)
        t = tok_wbf[("up", l)]
        for r0 in range(0, D, 128):
            K.dma(w_up_bf[l, r0:r0 + 128, :], w_up[l, r0:r0 + 128, :], writes=[t], q="gpsimd", cast=True)
        t = tok_wbf[("down", l)]
        for j in range(8):
            for c0 in range(0, 24, 8):
                src = w_down[l, c0 * 128:(c0 + 8) * 128, j * 128:(j + 1) * 128].rearrange("(c p) n -> p c n", p=128)
                K.dma(w_down_bf[l, j, :, c0:c0 + 8, :], src, writes=[t], q="gpsimd", cast=True)

    for l in range(NLAYER):
        cast_weights(l)

    def rmsnorm(xt, xtok, W, gcol, out_t, out_tok, sq, sqtok, rstd, rstd_tok, tmp, tmp_tok):
        for kt in range(KT):
            A(lambda e, kt=kt: e.activation(out=sq[:, kt, 0:W], in_=xt[:, kt, 0:W], func=AF.Square),
              r=[xtok[kt]], w=[sqtok[kt]])
        b = bank()
        mmgroup(ps[:, b, 0:W], [(ones_bf[:, :], sq[:, kt, 0:W]) for kt in range(KT)],
                [[sqtok[kt], tok_c] for kt in range(KT)], bank_tok[b])
        A(lambda e: e.activation(out=tmp[:, 0:W], in_=ps[:, b, 0:W], func=AF.Sqrt, scale=1.0 / D, bias=epsc[:, 0:1]),
          r=[bank_tok[b], tok_c], w=[tmp_tok])
        V(lambda e: e.reciprocal(out=rstd[:, 0:W], in_=tmp[:, 0:W]), r=[tmp_tok], w=[rstd_tok])
        for kt in range(KT):
            V(lambda e, kt=kt: e.scalar_tensor_tensor(out=out_t[:, kt, 0:W], in0=xt[:, kt, 0:W],
                                                      scalar=gcol[:, kt:kt + 1], in1=rstd[:, 0:W],
                                                      op0=ALU.mult, op1=ALU.mult),
              r=[xtok[kt], rstd_tok, tok_c], w=[out_tok[kt]])

    def phase_A(sq_list, l):
        flat_tiles = [(q, i) for q in sq_list for i in range(q["S"] // 512)]
        NT = len(flat_tiles)
        with ExitStack() as st:
            w_sb = sb(st, "w_in_sb", [128, KT, NIN], BF16)
            wtok = toks(KT)
            if l == 0:
                with ExitStack() as stg:
                    ws = [sb(stg, "wstage", [128, 4, NIN], F32) for _ in range(2)]
                    wstok = toks(KT)
                    for kt in range(KT):
                        K.dma(ws[kt // 4][:, kt % 4, :], w_in[0, kt * 128:(kt + 1) * 128, :], writes=[wstok[kt]])
                    for kt in range(KT):
                        if kt % 3 == 2:
                            A(lambda e, kt=kt: e.copy(out=w_sb[:, kt, :], in_=ws[kt // 4][:, kt % 4, :]), r=[wstok[kt]], w=[wtok[kt]])
                        else:
                            V(lambda e, kt=kt: e.tensor_copy(out=w_sb[:, kt, :], in_=ws[kt // 4][:, kt % 4, :]), r=[wstok[kt]], w=[wtok[kt]])
                    K.barrier()
            xT = [sb(st, "xT", [128, KT, 512], F32) for _ in range(2)]
            xtok = [toks(KT) for _ in range(2)]
            sq = sb(st, "sq", [128, KT, 512], BF16)
            sqtok = toks(KT)
            hTs = [sb(st, "hT", [128, KT, 512], BF16) for _ in range(2)]
            htoks = [toks(KT) for _ in range(2)]
            rstd = sb(st, "rstd", [128, 512], F32)
            tmp = sb(st, "tmp", [128, 512], F32)
            rstd_tok, tmp_tok = Buf(), Buf()
            rgx_st = [sb(st, "rgx_st", [128, 4, 512], BF16) for _ in range(2)]
            gg_st = [sb(st, "gg_st", [128, 4, 512], BF16) for _ in range(2)]
            q_st = [sb(st, "q_st", [128, 4, 512], BF16) for _ in range(2)]
            k_st = [sb(st, "k_st", [128, 4, 512], BF16) for _ in range(2)]
            va_st = [sb(st, "va_st", [128, 4, 516], BF16) for _ in range(2)]
            og_st = [sb(st, "og_st", [128, 4, 512], BF16) for _ in range(2)]
            ktm_st = [sb(st, "ktm_st", [128, 4, 512], BF16) for _ in range(2)]
            gr_st = [sb(st, "gr_st", [16, 512], F32) for _ in range(2)]
            th = sb(st, "th", [128, 512], F32)
            th_tok = Buf()
            stok = [{n: toks(4) for n in ("rgx", "gg", "q", "k", "va", "og", "ktm")} for _ in range(2)]
            grtok = [Buf(), Buf()]
            for p in range(2):
                G(lambda e, p=p: e.memset(va_st[p][:], 1.0), w=[stok[p]["va"]])
            if l > 0:
                for kt in range(KT):
                    K.dma(w_sb[:, kt, :], w_in_bf[l, kt * 128:(kt + 1) * 128, :], reads=[tok_wbf[("in", l)]], writes=[wtok[kt]])
            def load_x(i):
                p = i % 2
                q_, ti = flat_tiles[i]
                xsrc = q_["xin"] if l == 0 else q_["xmid_d"]
                for h in range(2):
                    K.dma(xT[p][:, h * 4:(h + 1) * 4, :],
                          xsrc[h * 512:(h + 1) * 512, ti * 512:(ti + 1) * 512].rearrange("(k p) t -> p k t", p=128),
                          writes=[xtok[p][h * 4:(h + 1) * 4]])

            load_x(0)
            if NT > 1:
                load_x(1)
            rmsnorm(xT[0], xtok[0], 512, g1c[:, l, :], hTs[0], htoks[0], sq, sqtok, rstd, rstd_tok, tmp, tmp_tok)
            for i in range(NT):
                p = i % 2
                sv, ti_ = flat_tiles[i]
                t0 = ti_ * 512
                hT, htok = hTs[p], htoks[p]
                if i + 1 < NT:
                    rmsnorm(xT[1 - p], xtok[1 - p], 512, g1c[:, l, :], hTs[1 - p], htoks[1 - p], sq, sqtok, rstd, rstd_tok, tmp, tmp_tok)
                if i + 2 < NT:
                    load_x(i + 2)
                for cg in range(16):
                    b = bank()
                    mmgroup(ps[:, b, :], [(w_sb[:, kt, cg * 128:(cg + 1) * 128], hT[:, kt, :]) for kt in range(KT)],
                            [[wtok[kt], htok[kt]] for kt in range(KT)], bank_tok[b])
                    c = cg % 4
                    if cg < 4:
                        A(lambda e, b=b, c=c: e.copy(out=rgx_st[p][:, c, :], in_=ps[:, b, :]),
                          r=[bank_tok[b]], w=[stok[p]["rgx"][c]])
                    elif cg < 8:
                        A(lambda e, b=b, c=c: e.activation(out=gg_st[p][:, c, :], in_=ps[:, b, :], func=AF.Gelu_apprx_tanh),
                          r=[bank_tok[b]], w=[stok[p]["gg"][c]])
                    elif cg < 12:
                        A(lambda e, b=b, c=c: e.activation(out=q_st[p][:, c, :], in_=ps[:, b, :], func=AF.Copy, scale=QSCALE),
                          r=[bank_tok[b]], w=[stok[p]["q"][c]])
                    else:
                        V(lambda e, b=b, c=c: e.tensor_copy(out=k_st[p][:, c, :], in_=ps[:, b, :]),
                          r=[bank_tok[b]], w=[stok[p]["k"][c]])
                b = bank()
                mmgroup(ps[0:16, b, :], [(w_sb[:, kt, 3072:3088], hT[:, kt, :]) for kt in range(KT)],
                        [[wtok[kt], htok[kt]] for kt in range(KT)], bank_tok[b])
                A(lambda e, b=b: e.activation(out=gr_st[p][:], in_=ps[0:16, b, :], func=AF.Identity, bias=bg[:, l:l + 1]),
                  r=[bank_tok[b], tok_c], w=[grtok[p]])
                for sub in range(4):
                    ts = slice(sub * 128, (sub + 1) * 128)
                    for gi, c0 in enumerate((2048, 2560, 1536)):
                        b = bank()
                        mmgroup(ps[:, b, :], [(hT[:, kt, ts], w_sb[:, kt, c0:c0 + 512]) for kt in range(KT)],
                                [[wtok[kt], htok[kt]] for kt in range(KT)], bank_tok[b])
                        if gi == 0:
                            V(lambda e, b=b, sub=sub: e.tensor_copy(
                                out=va_st[p][:, sub, :].rearrange("p (h c) -> p h c", c=129)[:, :, 0:128],
                                in_=ps[:, b, :].rearrange("p (h c) -> p h c", c=128)),
                              r=[bank_tok[b]], w=[stok[p]["va"][sub]])
                        elif gi == 1:
                            A(lambda e, b=b: e.activation(out=th[:], in_=ps[:, b, :], func=AF.Tanh, scale=0.5),
                              r=[bank_tok[b]], w=[th_tok])
                            V(lambda e, sub=sub: e.scalar_tensor_tensor(out=og_st[p][:, sub, :], in0=th[:], scalar=1.0,
                                                                        in1=ghalf[:, l, :], op0=ALU.add, op1=ALU.mult),
                              r=[th_tok, tok_c], w=[stok[p]["og"][sub]])
                        else:
                            V(lambda e, b=b, sub=sub: e.tensor_copy(out=ktm_st[p][:, sub, :], in_=ps[:, b, :]),
                              r=[bank_tok[b]], w=[stok[p]["ktm"][sub]])
                tsl = slice(t0, t0 + 512)
                K.dma(sv["rgx_pad"][:, 2 + t0:2 + t0 + 512].rearrange("(c p) t -> p c t", p=128), rgx_st[p][:], reads=[stok[p]["rgx"]])
                K.dma(sv["gg_d"][:, tsl].rearrange("(c p) t -> p c t", p=128), gg_st[p][:], reads=[stok[p]["gg"]])
                K.dma(sv["qT_d"][:, tsl].rearrange("(c p) t -> p c t", p=128), q_st[p][:], reads=[stok[p]["q"]])
                K.dma(sv["kT_d"][:, tsl].rearrange("(c p) t -> p c t", p=128), k_st[p][:], reads=[stok[p]["k"]])
                K.dma(sv["va_d"][tsl, :].rearrange("(s p) c -> p s c", p=128), va_st[p][:], reads=[stok[p]["va"]])
                K.dma(sv["og_d"][tsl, :].rearrange("(s p) c -> p s c", p=128), og_st[p][:], reads=[stok[p]["og"]])
                K.dma(sv["ktm_d"][tsl, :].rearrange("(s p) c -> p s c", p=128), ktm_st[p][:], reads=[stok[p]["ktm"]])
                K.dma(sv["grow"][:, tsl], gr_st[p][:], reads=[grtok[p]])
            K.barrier()

    def phase_G(sv, gt_tiles):
        S = sv["S"]
        NC = S // 128
        P = 4 * NC
        WT, STT_, SCB = gt_tiles
        with ExitStack() as st:
            T = [sb(st, "gT", [P, 128], F32) for _ in range(4)]
            tk = Buf()
            for g4 in range(4):
                K.dma(T[g4][:], sv["grow"][g4 * 4:(g4 + 1) * 4, :].rearrange("g (j r) -> (g j) r", r=128), writes=[tk])
            ax = sb(st, "ax", [P, 128], F32)
            lf = sb(st, "lf", [P, 128], F32)
            bb = sb(st, "bb", [P, 128], F32)
            cc = sb(st, "cc", [P, 128], F32)
            ww = sb(st, "ww", [P, 128], F32)
            sst = sb(st, "sst", [P, 128], F32)
            cmax = sb(st, "cmax", [P, 1], F32)
            nmc_col = sb(st, "nmc_col", [P, 1], F32)
            rows = sb(st, "rows", [1, 8, 128], F32)
            for d in range(2):
                I_, F_ = T[2 * d], T[2 * d + 1]
                rev = (d == 1)

                def R(ap2):
                    return ap2[:, ::-1] if rev else ap2
                A(lambda e: e.activation(out=ax[:], in_=F_[:], func=AF.Abs), r=[tk], w=[tk])
                A(lambda e: e.activation(out=ax[:], in_=ax[:], func=AF.Exp, scale=-1.0), r=[tk], w=[tk])
                A(lambda e: e.activation(out=ax[:], in_=ax[:], func=AF.Ln, bias=1.0), r=[tk], w=[tk])
                V(lambda e: e.tensor_scalar_min(out=lf[:], in0=F_[:], scalar1=0.0), r=[tk], w=[tk])
                V(lambda e: e.tensor_sub(out=lf[:], in0=lf[:], in1=ax[:]), r=[tk], w=[tk])
                V(lambda e: e.tensor_tensor_scan(out=R(bb[:]), data0=R(ones_f[0:P, :]), data1=R(lf[:]), initial=0.0,
                                                 op0=ALU.mult, op1=ALU.add), r=[tk, tok_c], w=[tk])
                V(lambda e: e.tensor_sub(out=cc[:], in0=I_[:], in1=bb[:]), r=[tk], w=[tk])
                V(lambda e: e.reduce_max(out=cmax[:], in_=cc[:], axis=AX.X), r=[tk], w=[tk])
                bl = bb[:, 0:1] if rev else bb[:, 127:128]
                b1 = bank()
                PE(lambda e: e.transpose(out=ps[0:1, b1, 0:P], in_=cmax[:], identity=ident_f[0:P, 0:P]), r=[tk, tok_c], w=[bank_tok[b1]])
                V(lambda e: e.tensor_copy(out=rows[:, 0, 0:P], in_=ps[0:1, b1, 0:P]), r=[bank_tok[b1]], w=[tk])
                b2 = bank()
                PE(lambda e: e.transpose(out=ps[0:1, b2, 0:P], in_=bl, identity=ident_f[0:P, 0:P]), r=[tk, tok_c], w=[bank_tok[b2]])
                V(lambda e: e.tensor_copy(out=rows[:, 1, 0:P], in_=ps[0:1, b2, 0:P]), r=[bank_tok[b2]], w=[tk])
                V(lambda e: e.memset(rows[:, 3, :], 0.0), r=[tk], w=[tk])
                for g in range(4):
                    seg = slice(g * NC, (g + 1) * NC)
                    V(lambda e, seg=seg: e.tensor_tensor_scan(out=R(rows[:, 2, seg]), data0=R(rows[:, 0, seg]), data1=R(rows[:, 1, seg]),
                                                              initial=0.0, op0=ALU.max, op1=ALU.add), r=[tk], w=[tk])
                    if NC > 1:
                        if rev:
                            V(lambda e, g=g: e.tensor_copy(out=rows[:, 3, g * NC:(g + 1) * NC - 1], in_=rows[:, 2, g * NC + 1:(g + 1) * NC]), r=[tk], w=[tk])
                        else:
                            V(lambda e, g=g: e.tensor_copy(out=rows[:, 3, g * NC + 1:(g + 1) * NC], in_=rows[:, 2, g * NC:(g + 1) * NC - 1]), r=[tk], w=[tk])
                V(lambda e: e.tensor_max(out=rows[:, 4, 0:P], in0=rows[:, 3, 0:P], in1=rows[:, 0, 0:P]), r=[tk], w=[tk])
                V(lambda e: e.tensor_sub(out=rows[:, 5, 0:P], in0=rows[:, 3, 0:P], in1=rows[:, 4, 0:P]), r=[tk], w=[tk])
                A(lambda e: e.activation(out=rows[:, 5, 0:P], in_=rows[:, 5, 0:P], func=AF.Exp), r=[tk], w=[tk])
                V(lambda e: e.tensor_scalar_mul(out=rows[:, 6, 0:P], in0=rows[:, 4, 0:P], scalar1=-1.0), r=[tk], w=[tk])
                b3 = bank()
                PE(lambda e: e.matmul(ps[0:P, b3, 0:1], lhsT=rows[:, 6, 0:P], rhs=ones_f[0:1, 0:1], start=True, stop=True),
                   r=[tk, tok_c], w=[bank_tok[b3]])
                V(lambda e: e.tensor_copy(out=nmc_col[:], in_=ps[0:P, b3, 0:1]), r=[bank_tok[b3]], w=[tk])
                A(lambda e: e.activation(out=ww[:], in_=cc[:], func=AF.Exp, bias=nmc_col[:]), r=[tk], w=[tk])
                A(lambda e: e.activation(out=sst[:], in_=bb[:], func=AF.Exp, bias=nmc_col[:], scale=-1.0), r=[tk], w=[tk])
                b4 = bank()
                PE(lambda e: e.transpose(out=ps[:, b4, 0:P], in_=ww[:], identity=ident_f[0:P, 0:P]), r=[tk, tok_c], w=[bank_tok[b4]])
                V(lambda e: e.tensor_copy(out=WT[d][:, 0:P], in_=ps[:, b4, 0:P]), r=[bank_tok[b4]], w=[tk])
                b5 = bank()
                PE(lambda e: e.transpose(out=ps[:, b5, 0:P], in_=sst[:], identity=ident_f[0:P, 0:P]), r=[tk, tok_c], w=[bank_tok[b5]])
                V(lambda e: e.tensor_copy(out=STT_[d][:, 0:P], in_=ps[:, b5, 0:P]), r=[bank_tok[b5]], w=[tk])
                b6 = bank()
                PE(lambda e: e.matmul(ps[:, b6, 0:P], lhsT=ones_f[0:1, :], rhs=rows[:, 5, 0:P], start=True, stop=True),
                   r=[tk, tok_c], w=[bank_tok[b6]])
                V(lambda e: e.tensor_copy(out=SCB[d][:, 0:P], in_=ps[:, b6, 0:P]), r=[bank_tok[b6]], w=[tk])
            K.barrier()

    def phase_B(sv, l, gt_tiles):
        S = sv["S"]
        NT = S // 512
        NC = S // 128
        WT, STT_, SCB = gt_tiles
        with ExitStack() as st0:
            wrg = sb(st0, "wrg", [128, 16, 128], BF16)
            wrg_tok = Buf()
            K.dma(wrg[:], wrg_bf[l].rearrange("d t c i o -> i (d t c) o"), reads=[tok_wbf[("rg", l)]], writes=[wrg_tok])
            dgw = sb(st0, "dgw", [128, 16, 128], BF16)
            dgw_tok = Buf()
            for j in range(4):
                for c in range(4):
                    V(lambda e, j=j, c=c: e.tensor_scalar(out=dgw[:, j * 4 + c, :], in0=ident_f[:], scalar1=rgcw[:, l, j, c:c + 1], scalar2=None, op0=ALU.mult),
                      r=[tok_c], w=[dgw_tok])
            for d in (1, 0):
                rev = (d == 1)
                with ExitStack() as st:
                    if rev:
                        rgxw = [sb(st, "rgxw", [128, 4, 516], BF16) for _ in range(2)]
                        xc = [sb(st, "xc", [128, 4, 512], F32)] * 2
                    else:
                        xc = [sb(st, "xc", [128, 4, 512], F32) for _ in range(2)]
                        hbl = sb(st, "hbl", [128, 4, 512], F32)
                        ggl = sb(st, "ggl", [128, 4, 512], BF16)
                        ogl = sb(st, "ogl", [128, 4, 512], BF16)
                        hbm = sb(st, "hbm", [128, 4, 512], F32)
                        mix = sb(st, "mix", [128, 8, 512], BF16)
                        sqv = sb(st, "sqv", [128, 4, 512], F32)
                        yb = sb(st, "yb", [128, 4, 512], BF16)
                        ssum = sb(st, "ssum", [128, 16], F32)
                    qTl = [sb(st, "qTl", [128, 4, 512], BF16) for _ in range(2)]
                    kTl = [sb(st, "kTl", [128, 4, 512], BF16) for _ in range(2)]
                    val = [sb(st, "val", [128, 4, 516], BF16) for _ in range(2)]
                    ktl = [sb(st, "ktl", [128, 4, 512], BF16) for _ in range(2)]
                    xcb = sb(st, "xcb", [128, 4, 512], BF16)
                    ta = sb(st, "ta", [128, 4, 512], F32)
                    tx = sb(st, "tx", [128, 4, 512], F32)
                    aa = sb(st, "aa", [128, 4, 512], F32)
                    a2 = sb(st, "a2", [128, 4, 512], F32)
                    hh = sb(st, "hh", [128, 4, 512], F32)
                    carry = sb(st, "carry", [128, 4], F32)
                    PT = sb(st, "PT", [128, 4, 512], BF16)
                    STb = sb(st, "STb", [128, 4, 512], BF16)
                    vw = sb(st, "vw", [128, 4, 516], BF16)
                    Cst = sb(st, "Cst", [128, 516], F32)
                    Cb = sb(st, "Cb", [128, 4, 516], BF16)
                    hd = sb(st, "hd", [128, 4, 512], F32)
                    dd = sb(st, "dd", [128, 4, 4], F32)
                    names = ("rgxw", "xc", "qT", "kT", "val", "ktl")
                    tkp = [{n: toks(4) for n in names} for _ in range(2)]
                    if rev:
                        tkp[1]["xc"] = tkp[0]["xc"]
                    tk = {n: toks(4) for n in ("xcb", "ta", "tx", "aa", "a2", "hh", "hbl", "ggl", "ogl", "hbm", "hd", "PT", "vw", "Cb", "dd", "STb")}
                    mixtok = toks(8)
                    t_carry, t_C, t_sqv, t_yb, t_ss = (Buf() for _ in range(5))
                    V(lambda e: e.memset(carry[:], 0.0), w=[t_carry])
                    V(lambda e: e.memset(Cst[:], 0.0), w=[t_C])
                    order = list(range(NT - 1, -1, -1) if rev else range(NT))

                    def loads_early(i, p):
                        t0 = i * 512
                        tsl = slice(t0, t0 + 512)
                        if rev:
                            K.dma(rgxw[p][:, :, 0:515], sv["rgx_pad"][:, t0:t0 + 515].rearrange("(c p) t -> p c t", p=128), writes=[tkp[p]["rgxw"]])
                        else:
                            K.dma(xc[p][:], xc_d[:, tsl].rearrange("(c p) t -> p c t", p=128), writes=[tkp[p]["xc"]])
                        K.dma(kTl[p][:], sv["kT_d"][:, tsl].rearrange("(c p) t -> p c t", p=128), writes=[tkp[p]["kT"]])
                        K.dma(qTl[p][:], sv["qT_d"][:, tsl].rearrange("(c p) t -> p c t", p=128), writes=[tkp[p]["qT"]])
                        K.dma(val[p][:], sv["va_d"][tsl, :].rearrange("(s p) c -> p s c", p=128), writes=[tkp[p]["val"]])
                        K.dma(ktl[p][:], sv["ktm_d"][tsl, :].rearrange("(s p) c -> p s c", p=128), writes=[tkp[p]["ktl"]])

                    loads_early(order[0], 0)
                    for it, i in enumerate(order):
                        p = it % 2
                        T = tkp[p]
                        t0 = i * 512
                        tsl = slice(t0, t0 + 512)
                        if it + 1 < NT:
                            loads_early(order[it + 1], 1 - p)
                        if not rev:
                            K.dma(hbl[:], hbrg_d[:, tsl].rearrange("(c p) t -> p c t", p=128), writes=[tk["hbl"]])
                            K.dma(hbm[:], hbml_d[tsl, :].rearrange("(s p) c -> p s c", p=128), writes=[tk["hbm"]])
                            K.dma(ggl[:], sv["gg_d"][:, tsl].rearrange("(c p) t -> p c t", p=128), writes=[tk["ggl"]])
                            K.dma(ogl[:], sv["og_d"][tsl, :].rearrange("(s p) c -> p s c", p=128), writes=[tk["ogl"]])
                        xcp = xc[p]
                        if rev:
                            for c in range(4):
                                bcv = bank()
                                for j in range(4):
                                    PE(lambda e, c=c, j=j, bcv=bcv: e.matmul(ps[:, bcv, :], lhsT=dgw[:, j * 4 + c, :], rhs=rgxw[p][:, c, j:j + 512], start=(j == 0), stop=(j == 3)),
                                       r=[dgw_tok, T["rgxw"][c]], w=[bank_tok[bcv]], inc=(j == 3))
                                A(lambda e, c=c, bcv=bcv: e.activation(out=xcp[:, c, :], in_=ps[:, bcv, :], func=AF.Identity, bias=rgcb[:, l, c:c + 1]),
                                  r=[bank_tok[bcv], tok_c], w=[T["xc"][c]])
                        for c in range(4):
                            V(lambda e, c=c: e.tensor_copy(out=xcb[:, c, :], in_=xcp[:, c, :]), r=[T["xc"][c]], w=[tk["xcb"][c]])
                        mk = maskb if rev else maskf
                        def emit_s12(sub):
                            j = i * 4 + sub
                            cs = slice(sub * 128, (sub + 1) * 128)
                            bst = bank()
                            for h in range(4):
                                PE(lambda e, h=h, bst=bst, cs=cs: e.matmul(ps[:, bst, h * 128:(h + 1) * 128], lhsT=kTl[p][:, h, cs], rhs=qTl[p][:, h, cs], start=True, stop=True),
                                   r=[T["kT"][h], T["qT"][h]], w=[bank_tok[bst]], inc=(h == 3))
                            V(lambda e, bst=bst, sub=sub: e.tensor_tensor(out=PT[:, sub, :].rearrange("p (h t) -> p h t", h=4),
                                                                          in0=ps[:, bst, :].rearrange("p (h t) -> p h t", h=4),
                                                                          in1=mk[:].unsqueeze(1).to_broadcast([128, 4, 128]), op=ALU.mult),
                              r=[bank_tok[bst], tok_c], w=[tk["PT"][sub]])
                            wcols = WT[d][:, j:j + 3 * NC + 1:NC]
                            G(lambda e, sub=sub, wcols=wcols: e.tensor_tensor(out=vw[:, sub, :].rearrange("p (h c) -> p h c", h=4),
                                                                             in0=val[p][:, sub, :].rearrange("p (h c) -> p h c", h=4),
                                                                             in1=wcols.unsqueeze(2).to_broadcast([128, 4, 129]), op=ALU.mult),
                              r=[T["val"][sub]], w=[tk["vw"][sub]])
                            cb_ = bank2()
                            for h in range(4):
                                o_ap = ps[:, cb_[h // 2], (h % 2) * 129:(h % 2) * 129 + 129]
                                PE(lambda e, h=h, o_ap=o_ap, sub=sub: e.matmul(o_ap, lhsT=ktl[p][:, sub, h * 128:(h + 1) * 128], rhs=vw[:, sub, h * 129:(h + 1) * 129], start=True, stop=True),
                                   r=[T["ktl"][sub], tk["vw"][sub]], w=[bank_tok[cb_[h // 2]]], inc=(h % 2 == 1))
                            sccols = SCB[d][:, j:j + 3 * NC + 1:NC]
                            V(lambda e, sccols=sccols: e.tensor_tensor(out=Cst[:].rearrange("p (h c) -> p h c", h=4),
                                                                      in0=Cst[:].rearrange("p (h c) -> p h c", h=4),
                                                                      in1=sccols.unsqueeze(2).to_broadcast([128, 4, 129]), op=ALU.mult),
                              r=[t_C], w=[t_C])
                            V(lambda e, sub=sub: e.tensor_copy(out=Cb[:, sub, :], in_=Cst[:]), r=[t_C], w=[tk["Cb"][sub]])
                            V(lambda e, cb_=cb_: e.tensor_add(out=Cst[:].rearrange("p (b c) -> p b c", b=2), in0=Cst[:].rearrange("p (b c) -> p b c", b=2),
                                                              in1=ps[:, cb_[0]:cb_[0] + 2, 0:258]),
                              r=[t_C, bank_tok[cb_[0]], bank_tok[cb_[1]]], w=[t_C])
                        def emit_rg34(half):
                            bks = {}
                            for c in (2 * half, 2 * half + 1):
                                for ty in range(2):
                                    bb_ = bank()
                                    bks[(c, ty)] = bb_
                                    PE(lambda e, c=c, ty=ty, bb_=bb_: e.matmul(ps[:, bb_, :], lhsT=wrg[:, (d * 2 + ty) * 4 + c, :], rhs=xcb[:, c, :], start=True, stop=True),
                                       r=[wrg_tok, tk["xcb"][c]], w=[bank_tok[bb_]])
                            for c in (2 * half, 2 * half + 1):
                                col = (l * 2 + d) * 4 + c
                                A(lambda e, c=c, col=col, bb_=bks[(c, 0)]: e.activation(out=ta[:, c, :], in_=ps[:, bb_, :], func=AF.Tanh, scale=0.5, bias=hba[:, col:col + 1]),
                                  r=[bank_tok[bks[(c, 0)]], tok_c], w=[tk["ta"][c]])
                                A(lambda e, c=c, col=col, bb_=bks[(c, 1)]: e.activation(out=tx[:, c, :], in_=ps[:, bb_, :], func=AF.Tanh, scale=0.5, bias=hbx[:, col:col + 1]),
                                  r=[bank_tok[bks[(c, 1)]], tok_c], w=[tk["tx"][c]])
                            for c in (2 * half, 2 * half + 1):
                                col = (l * 2 + d) * 4 + c
                                A(lambda e, c=c, col=col: e.activation(out=aa[:, c, :], in_=ta[:, c, :], func=AF.Exp, scale=Kh[:, col:col + 1], bias=Kh[:, col:col + 1]),
                                  r=[tk["ta"][c], tok_c], w=[tk["aa"][c]])
                                A(lambda e, c=c, col=col: e.activation(out=a2[:, c, :], in_=ta[:, c, :], func=AF.Exp, scale=K2[:, col:col + 1], bias=K2[:, col:col + 1]),
                                  r=[tk["ta"][c], tok_c], w=[tk["a2"][c]])
                        subs = list(range(3, -1, -1) if rev else range(4))
                        emit_s12(subs[0])
                        emit_rg34(0)
                        emit_s12(subs[1])
                        emit_rg34(1)
                        emit_s12(subs[2])
                        emit_s12(subs[3])
                        for sub in subs:
                            j = i * 4 + sub
                            cs = slice(sub * 128, (sub + 1) * 128)
                            nb = bank2()
                            for h in range(4):
                                o_ap = ps[:, nb[h // 2], (h % 2) * 129:(h % 2) * 129 + 129]
                                PE(lambda e, h=h, o_ap=o_ap, sub=sub: e.matmul(o_ap, lhsT=PT[:, sub, h * 128:(h + 1) * 128], rhs=vw[:, sub, h * 129:(h + 1) * 129], start=True, stop=False),
                                   r=[tk["PT"][sub], tk["vw"][sub]], w=[bank_tok[nb[h // 2]]], inc=False)
                                PE(lambda e, h=h, o_ap=o_ap, sub=sub, cs=cs: e.matmul(o_ap, lhsT=qTl[p][:, h, cs], rhs=Cb[:, sub, h * 129:(h + 1) * 129], start=False, stop=True),
                                   r=[T["qT"][h], tk["Cb"][sub]], w=[bank_tok[nb[h // 2]]], inc=(h % 2 == 1))
                            stc = STT_[d][:, j:j + 3 * NC + 1:NC]
                            den_ap = ps[:, nb[0]:nb[0] + 2, 128:258:129]
                            nbt = [bank_tok[nb[0]], bank_tok[nb[1]]]
                            V(lambda e, sub=sub, den_ap=den_ap, stc=stc: e.tensor_tensor(out=dd[:, sub, :].rearrange("p (b h) -> p b h", b=2), in0=den_ap,
                                                                                      in1=stc.rearrange("p (b h) -> p b h", b=2), op=ALU.max),
                              r=nbt, w=[tk["dd"][sub]])
                            V(lambda e, sub=sub, den_ap=den_ap: e.scalar_tensor_tensor(out=dd[:, sub, :].rearrange("p (b h) -> p b h", b=2), in0=den_ap, scalar=-1.0,
                                                                                    in1=dd[:, sub, :].rearrange("p (b h) -> p b h", b=2), op0=ALU.mult, op1=ALU.max),
                              r=nbt + [tk["dd"][sub]], w=[tk["dd"][sub]])
                            V(lambda e, sub=sub: e.reciprocal(out=dd[:, sub, :], in_=dd[:, sub, :]), r=[tk["dd"][sub]], w=[tk["dd"][sub]])
                            for bi in range(2):
                                V(lambda e, bi=bi, sub=sub, nb=nb: e.tensor_tensor(
                                    out=hd[:, sub, bi * 256:(bi + 1) * 256].rearrange("p (h c) -> p h c", h=2),
                                    in0=ps[:, nb[bi], 0:258].rearrange("p (h c) -> p h c", h=2)[:, :, 0:128],
                                    in1=dd[:, sub, 2 * bi:2 * bi + 2].unsqueeze(2).to_broadcast([128, 2, 128]), op=ALU.mult),
                                  r=[bank_tok[nb[bi]], tk["dd"][sub]], w=[tk["hd"][sub]])
                        for c in range(4):
                            A(lambda e, c=c: e.activation(out=a2[:, c, :], in_=a2[:, c, :], func=AF.Sqrt, scale=-1.0, bias=1.0 + 1e-6),
                              r=[tk["a2"][c]], w=[tk["a2"][c]])
                        for c in range(4):
                            if rev:
                                G(lambda e, c=c: e.tensor_scalar(out=tx[:, c, :], in0=tx[:, c, :], scalar1=1.0, scalar2=1.0, op0=ALU.add, op1=ALU.mult), r=[tk["tx"][c]], w=[tk["tx"][c]])
                                G(lambda e, c=c: e.tensor_mul(out=tx[:, c, :], in0=tx[:, c, :], in1=xcp[:, c, :]), r=[tk["tx"][c], T["xc"][c]], w=[tk["tx"][c]])
                            else:
                                V(lambda e, c=c: e.scalar_tensor_tensor(out=tx[:, c, :], in0=tx[:, c, :], scalar=1.0, in1=xcp[:, c, :],
                                                                        op0=ALU.add, op1=ALU.mult),
                                  r=[tk["tx"][c], T["xc"][c]], w=[tk["tx"][c]])
                        for c in range(4):
                            V(lambda e, c=c: e.scalar_tensor_tensor(out=tx[:, c, :], in0=a2[:, c, :], scalar=0.5, in1=tx[:, c, :],
                                                                    op0=ALU.mult, op1=ALU.mult),
                              r=[tk["a2"][c], tk["tx"][c]], w=[tk["tx"][c]])
                        for c in range(4):
                            if rev:
                                V(lambda e, c=c: e.tensor_tensor_scan(out=hh[:, c, ::-1], data0=aa[:, c, ::-1], data1=tx[:, c, ::-1],
                                                                      initial=carry[:, c:c + 1], op0=ALU.mult, op1=ALU.add),
                                  r=[tk["aa"][c], tk["tx"][c], t_carry], w=[tk["hh"][c]])
                            else:
                                V(lambda e, c=c: e.tensor_tensor_scan(out=hh[:, c, :], data0=aa[:, c, :], data1=tx[:, c, :],
                                                                      initial=carry[:, c:c + 1], op0=ALU.mult, op1=ALU.add),
                                  r=[tk["aa"][c], tk["tx"][c], t_carry], w=[tk["hh"][c]])
                        last = 0 if rev else 511
                        V(lambda e: e.tensor_copy(out=carry[:, :], in_=hh[:, :, last]), r=[tk["hh"]], w=[t_carry])
                        if rev:
                            K.dma(xc_d[:, tsl].rearrange("(c p) t -> p c t", p=128), xcp[:], reads=[T["xc"]])
                            K.dma(hbrg_d[:, tsl].rearrange("(c p) t -> p c t", p=128), hh[:], reads=[tk["hh"]])
                            K.dma(hbml_d[tsl, :].rearrange("(s p) c -> p s c", p=128), hd[:], reads=[tk["hd"]])
                        else:
                            for c in range(4):
                                G(lambda e, c=c: e.tensor_add(out=hh[:, c, :], in0=hh[:, c, :], in1=hbl[:, c, :]),
                                  r=[tk["hh"][c], tk["hbl"][c]], w=[tk["hh"][c]])
                            for c in range(4):
                                G(lambda e, c=c: e.tensor_mul(out=mix[:, c, :], in0=hh[:, c, :], in1=ggl[:, c, :]),
                                  r=[tk["hh"][c], tk["ggl"][c]], w=[mixtok[c]])
                            G(lambda e: e.tensor_add(out=hd[:], in0=hd[:], in1=hbm[:]), r=[tk["hd"], tk["hbm"]], w=[tk["hd"]])
                            G(lambda e: e.tensor_mul(out=sqv[:], in0=hd[:], in1=hd[:]), r=[tk["hd"]], w=[t_sqv])
                            V(lambda e: e.reduce_sum(out=ssum[:], in_=sqv[:].rearrange("p s (h c) -> p (s h) c", h=4), axis=AX.X),
                              r=[t_sqv], w=[t_ss])
                            A(lambda e: e.activation(out=ssum[:], in_=ssum[:], func=AF.Sqrt, scale=1.0 / 128, bias=epsc[:, 0:1]), r=[t_ss, tok_c], w=[t_ss])
                            V(lambda e: e.reciprocal(out=ssum[:], in_=ssum[:]), r=[t_ss], w=[t_ss])
                            V(lambda e: e.tensor_tensor(out=sqv[:].rearrange("p s (h c) -> p (s h) c", h=4),
                                                        in0=hd[:].rearrange("p s (h c) -> p (s h) c", h=4),
                                                        in1=ssum[:].unsqueeze(2).to_broadcast([128, 16, 128]), op=ALU.mult),
                              r=[tk["hd"], t_ss, t_sqv], w=[t_sqv])
                            G(lambda e: e.tensor_mul(out=yb[:], in0=sqv[:], in1=ogl[:]), r=[t_sqv, tk["ogl"]], w=[t_yb])
                            for half in range(2):
                                bt = bank()
                                pbf = ps[:, bt, :].bitcast(BF16)
                                for q in range(8):
                                    sub, h = half * 2 + q // 4, q % 4
                                    PE(lambda e, q=q, sub=sub, h=h, pbf=pbf: e.transpose(out=pbf[:, q * 128:(q + 1) * 128], in_=yb[:, sub, h * 128:(h + 1) * 128], identity=ident_bf[:]),
                                       r=[t_yb, tok_c], w=[bank_tok[bt]], inc=(q == 7))
                                for s2 in range(2):
                                    sub = half * 2 + s2
                                    A(lambda e, pbf=pbf, sub=sub, s2=s2: e.copy(out=mix[:, 4:8, sub * 128:(sub + 1) * 128],
                                                                            in_=pbf[:, s2 * 512:(s2 + 1) * 512].rearrange("p (h t) -> p h t", h=4)),
                                      r=[bank_tok[bt]], w=[mixtok[4:8]])
                            K.dma(sv["mixT_d"][:, tsl].rearrange("(k p) t -> p k t", p=128), mix[:], reads=[mixtok])
                    K.barrier()

    def phase_C(sq_list, l, final):
        with ExitStack() as st:
            wo = sb(st, "wo", [128, KT, D], BF16)
            wu = sb(st, "wu", [128, KT, 2 * DFF], BF16)
            wd = [sb(st, "wd", [128, 24, 128], BF16) for _ in range(2)]
            wo_tok, wu_tok = toks(KT), toks(KT)
            wd_tok = [Buf() for _ in range(2)]
            for kt in range(KT):
                K.dma(wo[:, kt, :], w_out_bf[l, kt * 128:(kt + 1) * 128, :], reads=[tok_wbf[("out", l)]], writes=[wo_tok[kt]])
            for kt in range(KT):
                K.dma(wu[:, kt, :], w_up_bf[l, kt * 128:(kt + 1) * 128, :], reads=[tok_wbf[("up", l)]], writes=[wu_tok[kt]])
            xt = sb(st, "xt", [128, KT, 512], F32)
            xtok = toks(KT)
            gbuf = sb(st, "gbuf", [128, 24, 512], BF16)
            gtok = toks(24)
            hT = sb(st, "hTc", [128, KT, 512], BF16)
            htok = toks(KT)
            rstd = sb(st, "rstdc", [128, 512], F32)
            tmp = sb(st, "tmpc", [128, 512], F32)
            rstd_tok, tmp_tok = Buf(), Buf()
            accg = [sb(st, "accg", [128, 512], F32) for _ in range(2)]
            accv = [sb(st, "accv", [128, 512], F32) for _ in range(2)]
            acc_tok = [[Buf(), Buf(), Buf()] for _ in range(2)]
            wdc = [0]
            all_tiles = [(q, t0, ln) for q in sq_list for (t0, ln) in seq_tiles(q["S"], 510)]
            for (sv, t0, ln) in all_tiles:
                S = sv["S"]
                xsrc = sv["xin"] if l == 0 else sv["xmid_d"]
                xdst = sv["yout"] if final else sv["xmid_d"]
                mixT_d = sv["mixT_d"]
                W = ln + 2
                lo = t0 - 1
                c0 = 1 if lo < 0 else 0
                c1 = W - 1 if lo + W > S else W
                if c0 == 1:
                    V(lambda e: e.memset(xt[:, :, 0:1], 0.0), w=[xtok])
                    V(lambda e: e.memset(gbuf[:, 0:8, 0:1], 0.0), w=[gtok[0:8]])
                if c1 == W - 1:
                    V(lambda e: e.memset(xt[:, :, W - 1:W], 0.0), w=[xtok])
                    V(lambda e: e.memset(gbuf[:, 0:8, W - 1:W], 0.0), w=[gtok[0:8]])
                for h in range(2):
                    K.dma(gbuf[:, h * 4:(h + 1) * 4, c0:c1],
                          mixT_d[h * 512:(h + 1) * 512, lo + c0:lo + c1].rearrange("(k p) t -> p k t", p=128),
                          writes=[gtok[h * 4:(h + 1) * 4]])
                for kt in range(KT):
                    K.dma(xt[:, kt, c0:c1], xsrc[kt * 128:(kt + 1) * 128, lo + c0:lo + c1], writes=[xtok[kt]])
                for jd in range(KT):
                    b = bank()
                    mmgroup(ps[:, b, 0:W], [(wo[:, kt, jd * 128:(jd + 1) * 128], gbuf[:, kt, 0:W]) for kt in range(KT)],
                            [[wo_tok[kt], gtok[kt]] for kt in range(KT)], bank_tok[b])
                    V(lambda e, b=b, jd=jd: e.tensor_add(out=xt[:, jd, 0:W], in0=xt[:, jd, 0:W], in1=ps[:, b, 0:W]),
                      r=[bank_tok[b], xtok[jd]], w=[xtok[jd]])
                rmsnorm(xt, xtok, W, g2c[:, l, :], hT, htok, gbuf[:, 8:16, :], gtok[8:16], rstd, rstd_tok, tmp, tmp_tok)
                for c in range(24):
                    pp = c % 2
                    bg_ = bank()
                    mmgroup(ps[:, bg_, 0:W], [(wu[:, kt, c * 128:(c + 1) * 128], hT[:, kt, 0:W]) for kt in range(KT)],
                            [[wu_tok[kt], htok[kt]] for kt in range(KT)], bank_tok[bg_])
                    bv_ = bank()
                    mmgroup(ps[:, bv_, 0:W], [(wu[:, kt, DFF + c * 128:DFF + (c + 1) * 128], hT[:, kt, 0:W]) for kt in range(KT)],
                            [[wu_tok[kt], htok[kt]] for kt in range(KT)], bank_tok[bv_])
                    for (bb_, acc, ti, cc_) in ((bg_, accg[pp], 0, c), (bv_, accv[pp], 1, 24 + c)):
                        A(lambda e, bb_=bb_, acc=acc, cc_=cc_: e.activation(out=acc[:, 1:W - 1], in_=ps[:, bb_, 1:W - 1], func=AF.Identity,
                                                                           scale=ffcw[:, l, 1, cc_:cc_ + 1], bias=ffcb[:, l, cc_:cc_ + 1]),
                          r=[bank_tok[bb_], tok_c], w=[acc_tok[pp][ti]])
                        V(lambda e, bb_=bb_, acc=acc, cc_=cc_: e.scalar_tensor_tensor(out=acc[:, 1:W - 1], in0=ps[:, bb_, 0:W - 2],
                                                                                     scalar=ffcw[:, l, 0, cc_:cc_ + 1], in1=acc[:, 1:W - 1],
                                                                                     op0=ALU.mult, op1=ALU.add),
                          r=[bank_tok[bb_], acc_tok[pp][ti], tok_c], w=[acc_tok[pp][ti]])
                        V(lambda e, bb_=bb_, acc=acc, cc_=cc_: e.scalar_tensor_tensor(out=acc[:, 1:W - 1], in0=ps[:, bb_, 2:W],
                                                                                     scalar=ffcw[:, l, 2, cc_:cc_ + 1], in1=acc[:, 1:W - 1],
                                                                                     op0=ALU.mult, op1=ALU.add),
                          r=[bank_tok[bb_], acc_tok[pp][ti], tok_c], w=[acc_tok[pp][ti]])
                    A(lambda e, pp=pp: e.activation(out=accg[pp][:, 1:W - 1], in_=accg[pp][:, 1:W - 1], func=AF.Gelu_apprx_tanh),
                      r=[acc_tok[pp][0]], w=[acc_tok[pp][0]])
                    G(lambda e, pp=pp, c=c: e.tensor_mul(out=gbuf[:, c, 1:W - 1], in0=accg[pp][:, 1:W - 1], in1=accv[pp][:, 1:W - 1]),
                      r=[acc_tok[pp][0], acc_tok[pp][1]], w=[gtok[c]])
                for jd in range(KT):
                    wi = wdc[0] % 2
                    wdc[0] += 1
                    K.dma(wd[wi][:], w_down_bf[l, jd], reads=[tok_wbf[("down", l)]], writes=[wd_tok[wi]])
                    b = bank()
                    mmgroup(ps[:, b, 0:ln], [(wd[wi][:, c, :], gbuf[:, c, 1:W - 1]) for c in range(24)],
                            [[wd_tok[wi], gtok[c]] for c in range(24)], bank_tok[b])
                    V(lambda e, b=b, jd=jd: e.tensor_add(out=xt[:, jd, 1:W - 1], in0=xt[:, jd, 1:W - 1], in1=ps[:, b, 0:ln]),
                      r=[bank_tok[b], xtok[jd]], w=[xtok[jd]])
                    if not final:
                        K.dma(xdst[jd * 128:(jd + 1) * 128, t0:t0 + ln], xt[:, jd, 1:W - 1], reads=[xtok[jd]], q="scalar")
                if final:
                    class _Sh:
                        def __init__(s, t): s.t = t
                        def __getitem__(s, idx):
                            a, b_, c_ = idx
                            return s.t[a, b_, slice(c_.start + 1, c_.stop + 1)]
                    rmsnorm(_Sh(xt), xtok, ln, gfc, _Sh(xt), xtok, gbuf[:, 8:16, :], gtok[8:16], rstd, rstd_tok, tmp, tmp_tok)
                    K.dma(xdst[:, t0:t0 + ln].rearrange("(k p) t -> p k t", p=128), xt[:, :, 1:W - 1], reads=[xtok])
            K.barrier()

    zf = sb(cst, "zf", [128, 4], F32)
    V(lambda e: e.memset(zf[:], 0.0), w=cw)
    for q in seqs:
        for c in range(4):
            K.dma(q["rgx_pad"][c * 128:(c + 1) * 128, 0:2], zer[:, 0:2], reads=cw)
            K.dma(q["rgx_pad"][c * 128:(c + 1) * 128, q["S"] + 2:q["S"] + 3], zer[:, 0:1], reads=cw, slow=True)
    K.barrier()

    for l in range(NLAYER):
        phase_A(seqs, l)
        for q in seqs:
            NC = q["S"] // 128
            with ExitStack() as gs:
                WT = [sb(gs, "WT", [128, 4 * NC], F32) for _ in range(2)]
                STT_ = [sb(gs, "STT", [128, 4 * NC], F32) for _ in range(2)]
                SCB = [sb(gs, "SCB", [128, 4 * NC], F32) for _ in range(2)]
                phase_G(q, (WT, STT_, SCB))
                phase_B(q, l, (WT, STT_, SCB))
        phase_C(seqs, l, final=(l == NLAYER - 1))
    K.barrier(include_cast=True)
    es.close()
    return nc, K


def build(groups, debug=False):
    _, K1 = _build(groups, debug, None)
    nc, K2 = _build(groups, debug, K1.rec)
    build.n_ins = K2.n_ins
    build.n_inc = (getattr(K1, "n_inc", 0), getattr(K2, "n_inc", 0))
    return nc


_GROUPS = [("p", 2, 4096), ("s", 4, 2048)]
_WNAMES = ["norm1_g", "w_in", "b_gates", "rg_conv_w", "rg_conv_b", "rg_wa", "rg_ba", "rg_wx", "rg_bx", "rg_lambda",
           "ml_norm_g", "w_out", "norm2_g", "w_up", "ffn_conv_w", "ffn_conv_b", "w_down", "final_g"]


def host_consts():
    s = np.arange(128)[:, None]
    t = np.arange(128)[None, :]
    return {"c_ident": np.eye(128, dtype=np.float32),
            "c_maskf": (s <= t).astype(np.float32),
            "c_maskb": (s >= t).astype(np.float32)}


def weight_map(inputs):
    m = {k: np.asarray(inputs[k], dtype=np.float32) for k in _WNAMES}
    L = NLAYER
    m["b_gates"] = m["b_gates"].reshape(L, 16, 1)
    m["norm1_g"] = m["norm1_g"].reshape(L, 8, 128).transpose(0, 2, 1)
    m["norm2_g"] = m["norm2_g"].reshape(L, 8, 128).transpose(0, 2, 1)
    m["final_g"] = m["final_g"].reshape(8, 128).transpose(1, 0)
    m["rg_conv_w"] = m["rg_conv_w"].reshape(L, 4, 4, 128).transpose(0, 3, 1, 2)
    m["rg_conv_b"] = m["rg_conv_b"].reshape(L, 4, 128).transpose(0, 2, 1)
    for k in ("rg_ba", "rg_bx", "rg_lambda"):
        m[k] = m[k].reshape(L, 2, 4, 128).transpose(3, 0, 1, 2).reshape(128, 16)
    m["ffn_conv_w"] = m["ffn_conv_w"].reshape(L, 3, 48, 128).transpose(0, 3, 1, 2)
    m["ffn_conv_b"] = m["ffn_conv_b"].reshape(L, 48, 128).transpose(0, 2, 1)
    m = {k: np.ascontiguousarray(v, dtype=np.float32) for k, v in m.items()}
    m.update(host_consts())
    return m


def kernel(**inputs):
    xp = np.asarray(inputs["x_prompt"], dtype=np.float32)
    xs = np.asarray(inputs["x_sample"], dtype=np.float32)
    n = 8
    nc = build(_GROUPS)
    wm = weight_map(inputs)
    in_maps = []
    for c in range(n):
        m = dict(wm)
        m["x_p"] = np.ascontiguousarray(xp[2 * c:2 * c + 2].transpose(0, 2, 1))
        m["x_s"] = np.ascontiguousarray(xs[4 * c:4 * c + 4].transpose(0, 2, 1))
        in_maps.append(m)
    res = run_bass_kernel_spmd(nc, in_maps, core_ids=list(range(n)))
    yp = np.concatenate([np.asarray(r["y_p"]).transpose(0, 2, 1) for r in res.results], axis=0)
    ys = np.concatenate([np.asarray(r["y_s"]).transpose(0, 2, 1) for r in res.results], axis=0)
    return (np.ascontiguousarray(yp, dtype=np.float32), np.ascontiguousarray(ys, dtype=np.float32))
```

```python
import numpy as np
from contextlib import ExitStack
import concourse.bass as bass
import concourse.mybir as mybir
from concourse.bass_utils import run_bass_kernel_spmd

F32 = mybir.dt.float32
BF16 = mybir.dt.bfloat16
AF = mybir.ActivationFunctionType
ALU = mybir.AluOpType
AX = mybir.AxisListType

D = 1024
KT = 8
NIN = 3088
DFF = 3072
EPS = 1e-6
NLAYER = 2
QSCALE = 128 ** -0.5


class Buf:
    __slots__ = ("w", "r")

    def __init__(self):
        self.w = None
        self.r = {}


def toks(*shape):
    a = np.empty(shape, dtype=object)
    for idx in np.ndindex(*shape):
        a[idx] = Buf()
    return a


def flat(*xs):
    out = []
    for x in xs:
        if isinstance(x, Buf):
            out.append(x)
        elif isinstance(x, np.ndarray):
            out.extend(x.ravel().tolist())
        else:
            for y in x:
                out.extend(flat(y))
    return out


class Key:
    def __init__(self, sem, eng=None, name=""):
        self.sem = sem
        self.eng = eng
        self.val = 0
        self.waited = {}
        self.name = name


class Sched:
    def __init__(self, nc, es, nslots=24, ncslots=8, needed=None):
        self.nc = nc
        self.needed = needed
        self.rec = {}
        self.E = {}
        for nm in ("sync", "scalar", "vector", "gpsimd", "tensor"):
            self.E[nm] = Key(es.enter_context(nc.semaphore("s_" + nm)), getattr(nc, nm), nm)
            self.E[nm].ordn = 0
            self.E[nm].omap = {}
            self.rec[nm] = set()
        self.slots = [Key(es.enter_context(nc.semaphore("d%d" % i)), name="d%d" % i) for i in range(nslots)]
        self.cslots = [Key(es.enter_context(nc.semaphore("c%d" % i)), name="c%d" % i) for i in range(ncslots)]
        self.rr = 0
        self.crr = 0
        self.n_ins = 0

    def _wait(self, E, key, val, raw):
        if val <= 0:
            return
        if key is E and E.name == "tensor":
            return
        if E.waited.get(key, 0) >= val:
            return
        E.eng.wait_ge(key.sem, val)
        E.waited[key] = val
        if key.eng is not None:
            self.rec[key.name].add(key.omap.get(val, -1))

    def _deps(self, E, reads, writes):
        for b in reads:
            if b.w is not None:
                self._wait(E, b.w[0], b.w[1], True)
        for b in writes:
            if b.w is not None:
                self._wait(E, b.w[0], b.w[1], False)
            for k, v in b.r.items():
                self._wait(E, k, v, False)

    def op(self, en, fn, reads=(), writes=(), inc=True):
        E = self.E[en]
        reads = flat(reads)
        writes = flat(writes)
        self._deps(E, reads, writes)
        ins = fn(E.eng)
        self.n_ins += 1
        stamp = E.val + 1
        if inc:
            E.ordn += 1
            E.omap.setdefault(stamp, E.ordn)
            if self.needed is None or E.ordn in self.needed[en]:
                ins.then_inc(E.sem, 1)
                E.val += 1
                E.omap[stamp] = E.ordn
                self.n_inc = getattr(self, "n_inc", 0) + 1
        for b in reads:
            if b.r.get(E, 0) < stamp:
                b.r[E] = stamp
        for b in writes:
            b.w = (E, stamp)
            b.r = {}
        return ins

    def dma(self, out, in_, reads=(), writes=(), q="sync", cast=False, slow=False):
        Q = self.E[q]
        if cast:
            slot = self.cslots[self.crr % len(self.cslots)]
            self.crr += 1
        else:
            slot = self.slots[self.rr % len(self.slots)]
            self.rr += 1
        reads = flat(reads)
        writes = flat(writes)
        self._wait(Q, slot, slot.val, True)
        self._deps(Q, reads, writes)
        ins = Q.eng.dma_start(out=out, in_=in_, allow_slow_non_contiguous=True) if slow else Q.eng.dma_start(out=out, in_=in_)
        self.n_ins += 1
        slot.val += 16
        ins.then_inc(slot.sem, 16)
        for b in reads:
            b.r[slot] = slot.val
        for b in writes:
            b.w = (slot, slot.val)
            b.r = {}
        return ins

    def barrier(self, include_cast=False):
        keys = list(self.E.values()) + self.slots + (self.cslots if include_cast else [])
        for E in self.E.values():
            for k in keys:
                if k is E:
                    continue
                self._wait(E, k, k.val, True)


def seq_tiles(S, maxlen):
    n = -(-S // maxlen)
    base = -(-S // n)
    base = -(-base // 8) * 8
    out = []
    t = 0
    while t < S:
        ln = min(base, S - t)
        out.append((t, ln))
        t += ln
    return out


def _build(groups, debug=False, needed=None):
    nc = bass.Bass("TRN2", target_bir_lowering=False)
    SM = max(S for _, _, S in groups)
    Ssizes = sorted(set(S for _, _, S in groups))
    es = ExitStack()
    K = Sched(nc, es, needed=needed)
    uid = [0]

    def din(name, shape, dt=F32):
        return nc.dram_tensor(name, list(shape), dt, kind="ExternalInput").ap()

    def dscr(name, shape, dt):
        kind = "ExternalOutput" if debug else "Internal"
        return nc.dram_tensor(name, list(shape), dt, kind=kind).ap()

    def sb(stack, name, shape, dt):
        uid[0] += 1
        return stack.enter_context(nc.sbuf_tensor("%s_%d" % (name, uid[0]), list(shape), dt))

    xin = {}
    yout = {}
    for name, n, S in groups:
        xin[name] = din("x_" + name, (n, D, S))
        yout[name] = nc.dram_tensor("y_" + name, [n, D, S], F32, kind="ExternalOutput").ap()
    norm1_g = din("norm1_g", (NLAYER, 128, 8))
    w_in = din("w_in", (NLAYER, D, NIN))
    b_gates = din("b_gates", (NLAYER, 16, 1))
    rg_conv_w = din("rg_conv_w", (NLAYER, 128, 4, 4))
    rg_conv_b = din("rg_conv_b", (NLAYER, 128, 4))
    rg_wa = din("rg_wa", (NLAYER, 2, 8, 64, 64))
    rg_ba = din("rg_ba", (128, 16))
    rg_wx = din("rg_wx", (NLAYER, 2, 8, 64, 64))
    rg_bx = din("rg_bx", (128, 16))
    rg_lambda = din("rg_lambda", (128, 16))
    ml_norm_g = din("ml_norm_g", (NLAYER, 512))
    w_out = din("w_out", (NLAYER, D, D))
    norm2_g = din("norm2_g", (NLAYER, 128, 8))
    w_up = din("w_up", (NLAYER, D, 2 * DFF))
    ffn_conv_w = din("ffn_conv_w", (NLAYER, 128, 3, 48))
    ffn_conv_b = din("ffn_conv_b", (NLAYER, 128, 48))
    w_down = din("w_down", (NLAYER, DFF, D))
    final_g = din("final_g", (128, 8))
    c_ident = din("c_ident", (128, 128))
    c_maskf = din("c_maskf", (128, 128))
    c_maskb = din("c_maskb", (128, 128))

    w_in_bf = dscr("w_in_bf", (NLAYER, D, NIN), BF16)
    w_out_bf = dscr("w_out_bf", (NLAYER, D, D), BF16)
    w_up_bf = dscr("w_up_bf", (NLAYER, D, 2 * DFF), BF16)
    w_down_bf = dscr("w_down_bf", (NLAYER, 8, 128, 24, 128), BF16)
    wrg_bf = dscr("wrg_bf", (NLAYER, 2, 2, 4, 128, 128), BF16)
    tok_wbf = {(w, l): Buf() for w in ("in", "out", "up", "down", "rg") for l in range(NLAYER)}
    seqs = []
    tb = 0
    for name, n, S in groups:
        for si in range(n):
            seqs.append(dict(name=name, si=si, S=S, base=tb, pbase=tb + 3 * len(seqs), idx=len(seqs)))
            tb += S
    TOT = tb
    rgx_all = dscr("rgx_pad", (512, TOT + 3 * len(seqs)), BF16)
    gg_all = dscr("gg_d", (512, TOT), BF16)
    qT_all = dscr("qT_d", (512, TOT), BF16)
    kT_all = dscr("kT_d", (512, TOT), BF16)
    va_all = dscr("va_d", (TOT, 516), BF16)
    og_all = dscr("og_d", (TOT, 512), BF16)
    ktm_all = dscr("ktm_d", (TOT, 512), BF16)
    mixT_all = dscr("mixT_d", (D, TOT), BF16)
    xmid_all = dscr("xmid_d", (D, TOT), F32)
    xc_d = dscr("xc_d", (512, SM), F32)
    hbrg_d = dscr("hbrg_d", (512, SM), F32)
    hbml_d = dscr("hbml_d", (SM, 512), F32)
    for q in seqs:
        b0, S = q["base"], q["S"]
        q["rgx_pad"] = rgx_all[:, q["pbase"]:q["pbase"] + S + 3]
        q["gg_d"] = gg_all[:, b0:b0 + S]
        q["qT_d"] = qT_all[:, b0:b0 + S]
        q["kT_d"] = kT_all[:, b0:b0 + S]
        q["va_d"] = va_all[b0:b0 + S, :]
        q["og_d"] = og_all[b0:b0 + S, :]
        q["ktm_d"] = ktm_all[b0:b0 + S, :]
        q["mixT_d"] = mixT_all[:, b0:b0 + S]
        q["xmid_d"] = xmid_all[:, b0:b0 + S]
        q["grow"] = dscr("grow_%d" % q["idx"], (16, S), F32)
        q["xin"] = xin[q["name"]][q["si"]]
        q["yout"] = yout[q["name"]][q["si"]]

    ps = es.enter_context(nc.psum_tensor("ps", [128, 8, 512], F32))
    bank_tok = toks(8)
    bank_rr = [0]

    def bank():
        b = bank_rr[0] % 8
        bank_rr[0] += 1
        return b

    def bank2():
        if bank_rr[0] % 2:
            bank_rr[0] += 1
        b = bank_rr[0] % 8
        bank_rr[0] += 2
        return [b, b + 1]

    def A(fn, r=(), w=()):
        return K.op("scalar", fn, r, w)

    def V(fn, r=(), w=()):
        return K.op("vector", fn, r, w)

    def G(fn, r=(), w=()):
        return K.op("gpsimd", fn, r, w)

    def PE(fn, r=(), w=(), inc=True):
        return K.op("tensor", fn, r, w, inc)

    def mmgroup(out_ap, pairs, rtoks, wtok):
        n = len(pairs)
        for i, (l_, r_) in enumerate(pairs):
            PE(lambda e, l_=l_, r_=r_, i=i: e.matmul(out_ap, lhsT=l_, rhs=r_, start=(i == 0), stop=(i == n - 1)),
               r=rtoks[i] if isinstance(rtoks, list) else rtoks, w=[wtok], inc=(i == n - 1))

    cst = ExitStack()
    es.enter_context(cst)
    ident_bf = sb(cst, "ident_bf", [128, 128], BF16)
    ident_f = sb(cst, "ident_f", [128, 128], F32)
    ones_bf = sb(cst, "ones_bf", [128, 128], BF16)
    ones_f = sb(cst, "ones_f", [128, 128], F32)
    maskf = sb(cst, "maskf", [128, 128], BF16)
    maskb = sb(cst, "maskb", [128, 128], BF16)
    neghalf = sb(cst, "neghalf", [128, 512], F32)
    zer = sb(cst, "zer", [128, 2048], BF16)
    g1c = sb(cst, "g1c", [128, NLAYER, 8], F32)
    g2c = sb(cst, "g2c", [128, NLAYER, 8], F32)
    gfc = sb(cst, "gfc", [128, 8], F32)
    rgcw = sb(cst, "rgcw", [128, NLAYER, 4, 4], F32)
    rgcb = sb(cst, "rgcb", [128, NLAYER, 4], F32)
    hba = sb(cst, "hba", [128, 16], F32)
    hbx = sb(cst, "hbx", [128, 16], F32)
    lam = sb(cst, "lam", [128, 16], F32)
    Kh = sb(cst, "Kh", [128, 16], F32)
    K2 = sb(cst, "K2", [128, 16], F32)
    ltmp = sb(cst, "ltmp", [128, 16], F32)
    ffcw = sb(cst, "ffcw", [128, NLAYER, 3, 48], F32)
    ffcb = sb(cst, "ffcb", [128, NLAYER, 48], F32)
    bg = sb(cst, "bg", [16, NLAYER], F32)
    ghalf = sb(cst, "ghalf", [128, NLAYER, 512], F32)
    epsc = sb(cst, "epsc", [128, 1], F32)
    tok_c = Buf()

    cw = [tok_c]
    K.dma(ident_f[:], c_ident[:, :], writes=cw)
    K.dma(ident_bf[:], c_ident[:, :], writes=cw, q="gpsimd", cast=True)
    K.dma(maskf[:], c_maskf[:, :], writes=cw, q="gpsimd", cast=True)
    K.dma(maskb[:], c_maskb[:, :], writes=cw, q="gpsimd", cast=True)
    for l in range(NLAYER):
        K.dma(g1c[:, l, :], norm1_g[l], writes=cw)
        K.dma(g2c[:, l, :], norm2_g[l], writes=cw)
        K.dma(rgcw[:, l, :, :], rg_conv_w[l], writes=cw)
        K.dma(rgcb[:, l, :], rg_conv_b[l], writes=cw)
        K.dma(ffcw[:, l, :, :], ffn_conv_w[l], writes=cw)
        K.dma(ffcb[:, l, :], ffn_conv_b[l], writes=cw)
        K.dma(bg[:, l:l + 1], b_gates[l], writes=cw)
        K.dma(ghalf[:, l, :], ml_norm_g[l:l + 1, :].to_broadcast([128, 512]), writes=cw)
    K.dma(hba[:], rg_ba[:, :], writes=cw)
    K.dma(hbx[:], rg_bx[:, :], writes=cw)
    K.dma(lam[:], rg_lambda[:, :], writes=cw)
    K.dma(gfc[:], final_g[:, :], writes=cw)
    V(lambda e: e.memset(ones_bf[:], 1.0), w=cw)
    V(lambda e: e.memset(ones_f[:], 1.0), w=cw)
    V(lambda e: e.memset(neghalf[:], -0.5), w=cw)
    V(lambda e: e.memset(zer[:], 0.0), w=cw)
    V(lambda e: e.memset(epsc[:], EPS), w=cw)
    V(lambda e: e.tensor_scalar_mul(out=hba[:], in0=hba[:], scalar1=0.5), r=cw, w=cw)
    V(lambda e: e.tensor_scalar_mul(out=hbx[:], in0=hbx[:], scalar1=0.5), r=cw, w=cw)
    for l in range(NLAYER):
        V(lambda e, l=l: e.tensor_scalar_mul(out=ghalf[:, l, :], in0=ghalf[:, l, :], scalar1=0.5), r=cw, w=cw)
    A(lambda e: e.activation(out=ltmp[:], in_=lam[:], func=AF.Abs), r=cw, w=cw)
    A(lambda e: e.activation(out=ltmp[:], in_=ltmp[:], func=AF.Exp, scale=-1.0), r=cw, w=cw)
    A(lambda e: e.activation(out=ltmp[:], in_=ltmp[:], func=AF.Ln, bias=1.0), r=cw, w=cw)
    V(lambda e: e.tensor_scalar(out=lam[:], in0=lam[:], scalar1=-1.0, scalar2=0.0, op0=ALU.mult, op1=ALU.max), r=cw, w=cw)
    V(lambda e: e.tensor_add(out=lam[:], in0=lam[:], in1=ltmp[:]), r=cw, w=cw)
    V(lambda e: e.tensor_scalar_mul(out=K2[:], in0=lam[:], scalar1=-8.0), r=cw, w=cw)
    V(lambda e: e.tensor_scalar_mul(out=Kh[:], in0=lam[:], scalar1=-4.0), r=cw, w=cw)

    def cast_weights(l):
        t = tok_wbf[("in", l)]
        if l > 0:
            for r0 in range(0, D, 256):
                K.dma(w_in_bf[l, r0:r0 + 256, :], w_in[l, r0:r0 + 256, :], writes=[t], q="gpsimd", cast=True)
        t = tok_wbf[("rg", l)]
        K.dma(wrg_bf[l].rearrange("d t c i o -> (d t c i) o")[:, :].rearrange("(a p) o -> p a o", p=128),
              zer[:, 0:2048].rearrange("p (a o) -> p a o", o=128), reads=cw, writes=[t])
        for d in range(2):
            for ty, wsrc in enumerate((rg_wa, rg_wx)):
                for par in range(2):
                    src = wsrc[l, d].rearrange("(c two) i o -> two c i o", two=2)[par]
                    dst = wrg_bf[l, d, ty, :, par * 64:(par + 1) * 64, par * 64:(par + 1) * 64]
                    K.dma(dst, src, writes=[t], q="gpsimd", cast=True)
        t = tok_wbf[("out", l)]
        for r0 in range(0, D, 512):
            K.dma(w_out_bf[l, r0:r0 + 512, :], w_out[l, r0:r0 + 512, :], writes=[t], q="gpsimd", cast=True)
        t = tok_wbf[("up", l)]
        for r0 in range(0, D, 128):
            K.dma(w_up_bf[l, r0:r0 + 128, :], w_up[l, r0:r0 + 128, :], writes=[t], q="gpsimd", cast=True)
        t = tok_wbf[("down", l)]
        for j in range(8):
            for c0 in range(0, 24, 8):
                src = w_down[l, c0 * 128:(c0 + 8) * 128, j * 128:(j + 1) * 128].rearrange("(c p) n -> p c n", p=128)
                K.dma(w_down_bf[l, j, :, c0:c0 + 8, :], src, writes=[t], q="gpsimd", cast=True)

    for l in range(NLAYER):
        cast_weights(l)

    def rmsnorm(xt, xtok, W, gcol, out_t, out_tok, sq, sqtok, rstd, rstd_tok, tmp, tmp_tok):
        for kt in range(KT):
            A(lambda e, kt=kt: e.activation(out=sq[:, kt, 0:W], in_=xt[:, kt, 0:W], func=AF.Square),
              r=[xtok[kt]], w=[sqtok[kt]])
        b = bank()
        mmgroup(ps[:, b, 0:W], [(ones_bf[:, :], sq[:, kt, 0:W]) for kt in range(KT)],
                [[sqtok[kt], tok_c] for kt in range(KT)], bank_tok[b])
        A(lambda e: e.activation(out=tmp[:, 0:W], in_=ps[:, b, 0:W], func=AF.Sqrt, scale=1.0 / D, bias=epsc[:, 0:1]),
          r=[bank_tok[b], tok_c], w=[tmp_tok])
        V(lambda e: e.reciprocal(out=rstd[:, 0:W], in_=tmp[:, 0:W]), r=[tmp_tok], w=[rstd_tok])
        for kt in range(KT):
            V(lambda e, kt=kt: e.scalar_tensor_tensor(out=out_t[:, kt, 0:W], in0=xt[:, kt, 0:W],
                                                      scalar=gcol[:, kt:kt + 1], in1=rstd[:, 0:W],
                                                      op0=ALU.mult, op1=ALU.mult),
              r=[xtok[kt], rstd_tok, tok_c], w=[out_tok[kt]])

    def phase_A(sq_list, l):
        flat_tiles = [(q, i) for q in sq_list for i in range(q["S"] // 512)]
        NT = len(flat_tiles)
        with ExitStack() as st:
            w_sb = sb(st, "w_in_sb", [128, KT, NIN], BF16)
            wtok = toks(KT)
            if l == 0:
                with ExitStack() as stg:
                    ws = [sb(stg, "wstage", [128, 4, NIN], F32) for _ in range(2)]
                    wstok = toks(KT)
                    for kt in range(KT):
                        K.dma(ws[kt // 4][:, kt % 4, :], w_in[0, kt * 128:(kt + 1) * 128, :], writes=[wstok[kt]])
                    for kt in range(KT):
                        if kt % 3 == 2:
                            A(lambda e, kt=kt: e.copy(out=w_sb[:, kt, :], in_=ws[kt // 4][:, kt % 4, :]), r=[wstok[kt]], w=[wtok[kt]])
                        else:
                            V(lambda e, kt=kt: e.tensor_copy(out=w_sb[:, kt, :], in_=ws[kt // 4][:, kt % 4, :]), r=[wstok[kt]], w=[wtok[kt]])
                    K.barrier()
            xT = [sb(st, "xT", [128, KT, 512], F32) for _ in range(2)]
            xtok = [toks(KT) for _ in range(2)]
            sq = sb(st, "sq", [128, KT, 512], BF16)
            sqtok = toks(KT)
            hTs = [sb(st, "hT", [128, KT, 512], BF16) for _ in range(2)]
            htoks = [toks(KT) for _ in range(2)]
            rstd = sb(st, "rstd", [128, 512], F32)
            tmp = sb(st, "tmp", [128, 512], F32)
            rstd_tok, tmp_tok = Buf(), Buf()
            rgx_st = [sb(st, "rgx_st", [128, 4, 512], BF16) for _ in range(2)]
            gg_st = [sb(st, "gg_st", [128, 4, 512], BF16) for _ in range(2)]
            q_st = [sb(st, "q_st", [128, 4, 512], BF16) for _ in range(2)]
            k_st = [sb(st, "k_st", [128, 4, 512], BF16) for _ in range(2)]
            va_st = [sb(st, "va_st", [128, 4, 516], BF16) for _ in range(2)]
            og_st = [sb(st, "og_st", [128, 4, 512], BF16) for _ in range(2)]
            ktm_st = [sb(st, "ktm_st", [128, 4, 512], BF16) for _ in range(2)]
            gr_st = [sb(st, "gr_st", [16, 512], F32) for _ in range(2)]
            th = sb(st, "th", [128, 512], F32)
            th_tok = Buf()
            stok = [{n: toks(4) for n in ("rgx", "gg", "q", "k", "va", "og", "ktm")} for _ in range(2)]
            grtok = [Buf(), Buf()]
            for p in range(2):
                G(lambda e, p=p: e.memset(va_st[p][:], 1.0), w=[stok[p]["va"]])
            if l > 0:
                for kt in range(KT):
                    K.dma(w_sb[:, kt, :], w_in_bf[l, kt * 128:(kt + 1) * 128, :], reads=[tok_wbf[("in", l)]], writes=[wtok[kt]])
            def load_x(i):
                p = i % 2
                q_, ti = flat_tiles[i]
                xsrc = q_["xin"] if l == 0 else q_["xmid_d"]
                for h in range(2):
                    K.dma(xT[p][:, h * 4:(h + 1) * 4, :],
                          xsrc[h * 512:(h + 1) * 512, ti * 512:(ti + 1) * 512].rearrange("(k p) t -> p k t", p=128),
                          writes=[xtok[p][h * 4:(h + 1) * 4]])

            load_x(0)
            if NT > 1:
                load_x(1)
            rmsnorm(xT[0], xtok[0], 512, g1c[:, l, :], hTs[0], htoks[0], sq, sqtok, rstd, rstd_tok, tmp, tmp_tok)
            for i in range(NT):
                p = i % 2
                sv, ti_ = flat_tiles[i]
                t0 = ti_ * 512
                hT, htok = hTs[p], htoks[p]
                if i + 1 < NT:
                    rmsnorm(xT[1 - p], xtok[1 - p], 512, g1c[:, l, :], hTs[1 - p], htoks[1 - p], sq, sqtok, rstd, rstd_tok, tmp, tmp_tok)
                if i + 2 < NT:
                    load_x(i + 2)
                for cg in range(16):
                    b = bank()
                    mmgroup(ps[:, b, :], [(w_sb[:, kt, cg * 128:(cg + 1) * 128], hT[:, kt, :]) for kt in range(KT)],
                            [[wtok[kt], htok[kt]] for kt in range(KT)], bank_tok[b])
                    c = cg % 4
                    if cg < 4:
                        A(lambda e, b=b, c=c: e.copy(out=rgx_st[p][:, c, :], in_=ps[:, b, :]),
                          r=[bank_tok[b]], w=[stok[p]["rgx"][c]])
                    elif cg < 8:
                        A(lambda e, b=b, c=c: e.activation(out=gg_st[p][:, c, :], in_=ps[:, b, :], func=AF.Gelu_apprx_tanh),
                          r=[bank_tok[b]], w=[stok[p]["gg"][c]])
                    elif cg < 12:
                        A(lambda e, b=b, c=c: e.activation(out=q_st[p][:, c, :], in_=ps[:, b, :], func=AF.Copy, scale=QSCALE),
                          r=[bank_tok[b]], w=[stok[p]["q"][c]])
                    else:
                        V(lambda e, b=b, c=c: e.tensor_copy(out=k_st[p][:, c, :], in_=ps[:, b, :]),
                          r=[bank_tok[b]], w=[stok[p]["k"][c]])
                b = bank()
                mmgroup(ps[0:16, b, :], [(w_sb[:, kt, 3072:3088], hT[:, kt, :]) for kt in range(KT)],
                        [[wtok[kt], htok[kt]] for kt in range(KT)], bank_tok[b])
                A(lambda e, b=b: e.activation(out=gr_st[p][:], in_=ps[0:16, b, :], func=AF.Identity, bias=bg[:, l:l + 1]),
                  r=[bank_tok[b], tok_c], w=[grtok[p]])
                for sub in range(4):
                    ts = slice(sub * 128, (sub + 1) * 128)
                    for gi, c0 in enumerate((2048, 2560, 1536)):
                        b = bank()
                        mmgroup(ps[:, b, :], [(hT[:, kt, ts], w_sb[:, kt, c0:c0 + 512]) for kt in range(KT)],
                                [[wtok[kt], htok[kt]] for kt in range(KT)], bank_tok[b])
                        if gi == 0:
                            V(lambda e, b=b, sub=sub: e.tensor_copy(
                                out=va_st[p][:, sub, :].rearrange("p (h c) -> p h c", c=129)[:, :, 0:128],
                                in_=ps[:, b, :].rearrange("p (h c) -> p h c", c=128)),
                              r=[bank_tok[b]], w=[stok[p]["va"][sub]])
                        elif gi == 1:
                            A(lambda e, b=b: e.activation(out=th[:], in_=ps[:, b, :], func=AF.Tanh, scale=0.5),
                              r=[bank_tok[b]], w=[th_tok])
                            V(lambda e, sub=sub: e.scalar_tensor_tensor(out=og_st[p][:, sub, :], in0=th[:], scalar=1.0,
                                                                        in1=ghalf[:, l, :], op0=ALU.add, op1=ALU.mult),
                              r=[th_tok, tok_c], w=[stok[p]["og"][sub]])
                        else:
                            V(lambda e, b=b, sub=sub: e.tensor_copy(out=ktm_st[p][:, sub, :], in_=ps[:, b, :]),
                              r=[bank_tok[b]], w=[stok[p]["ktm"][sub]])
                tsl = slice(t0, t0 + 512)
                K.dma(sv["rgx_pad"][:, 2 + t0:2 + t0 + 512].rearrange("(c p) t -> p c t", p=128), rgx_st[p][:], reads=[stok[p]["rgx"]])
                K.dma(sv["gg_d"][:, tsl].rearrange("(c p) t -> p c t", p=128), gg_st[p][:], reads=[stok[p]["gg"]])
                K.dma(sv["qT_d"][:, tsl].rearrange("(c p) t -> p c t", p=128), q_st[p][:], reads=[stok[p]["q"]])
                K.dma(sv["kT_d"][:, tsl].rearrange("(c p) t -> p c t", p=128), k_st[p][:], reads=[stok[p]["k"]])
                K.dma(sv["va_d"][tsl, :].rearrange("(s p) c -> p s c", p=128), va_st[p][:], reads=[stok[p]["va"]])
                K.dma(sv["og_d"][tsl, :].rearrange("(s p) c -> p s c", p=128), og_st[p][:], reads=[stok[p]["og"]])
                K.dma(sv["ktm_d"][tsl, :].rearrange("(s p) c -> p s c", p=128), ktm_st[p][:], reads=[stok[p]["ktm"]])
                K.dma(sv["grow"][:, tsl], gr_st[p][:], reads=[grtok[p]])
            K.barrier()

    def phase_G(sv, gt_tiles):
        S = sv["S"]
        NC = S // 128
        P = 4 * NC
        WT, STT_, SCB = gt_tiles
        with ExitStack() as st:
            T = [sb(st, "gT", [P, 128], F32) for _ in range(4)]
            tk = Buf()
            for g4 in range(4):
                K.dma(T[g4][:], sv["grow"][g4 * 4:(g4 + 1) * 4, :].rearrange("g (j r) -> (g j) r", r=128), writes=[tk])
            ax = sb(st, "ax", [P, 128], F32)
            lf = sb(st, "lf", [P, 128], F32)
            bb = sb(st, "bb", [P, 128], F32)
            cc = sb(st, "cc", [P, 128], F32)
            ww = sb(st, "ww", [P, 128], F32)
            sst = sb(st, "sst", [P, 128], F32)
            cmax = sb(st, "cmax", [P, 1], F32)
            nmc_col = sb(st, "nmc_col", [P, 1], F32)
            rows = sb(st, "rows", [1, 8, 128], F32)
            for d in range(2):
                I_, F_ = T[2 * d], T[2 * d + 1]
                rev = (d == 1)

                def R(ap2):
                    return ap2[:, ::-1] if rev else ap2
                A(lambda e: e.activation(out=ax[:], in_=F_[:], func=AF.Abs), r=[tk], w=[tk])
                A(lambda e: e.activation(out=ax[:], in_=ax[:], func=AF.Exp, scale=-1.0), r=[tk], w=[tk])
                A(lambda e: e.activation(out=ax[:], in_=ax[:], func=AF.Ln, bias=1.0), r=[tk], w=[tk])
                V(lambda e: e.tensor_scalar_min(out=lf[:], in0=F_[:], scalar1=0.0), r=[tk], w=[tk])
                V(lambda e: e.tensor_sub(out=lf[:], in0=lf[:], in1=ax[:]), r=[tk], w=[tk])
                V(lambda e: e.tensor_tensor_scan(out=R(bb[:]), data0=R(ones_f[0:P, :]), data1=R(lf[:]), initial=0.0,
                                                 op0=ALU.mult, op1=ALU.add), r=[tk, tok_c], w=[tk])
                V(lambda e: e.tensor_sub(out=cc[:], in0=I_[:], in1=bb[:]), r=[tk], w=[tk])
                V(lambda e: e.reduce_max(out=cmax[:], in_=cc[:], axis=AX.X), r=[tk], w=[tk])
                bl = bb[:, 0:1] if rev else bb[:, 127:128]
                b1 = bank()
                PE(lambda e: e.transpose(out=ps[0:1, b1, 0:P], in_=cmax[:], identity=ident_f[0:P, 0:P]), r=[tk, tok_c], w=[bank_tok[b1]])
                V(lambda e: e.tensor_copy(out=rows[:, 0, 0:P], in_=ps[0:1, b1, 0:P]), r=[bank_tok[b1]], w=[tk])
                b2 = bank()
                PE(lambda e: e.transpose(out=ps[0:1, b2, 0:P], in_=bl, identity=ident_f[0:P, 0:P]), r=[tk, tok_c], w=[bank_tok[b2]])
                V(lambda e: e.tensor_copy(out=rows[:, 1, 0:P], in_=ps[0:1, b2, 0:P]), r=[bank_tok[b2]], w=[tk])
                V(lambda e: e.memset(rows[:, 3, :], 0.0), r=[tk], w=[tk])
                for g in range(4):
                    seg = slice(g * NC, (g + 1) * NC)
                    V(lambda e, seg=seg: e.tensor_tensor_scan(out=R(rows[:, 2, seg]), data0=R(rows[:, 0, seg]), data1=R(rows[:, 1, seg]),
                                                              initial=0.0, op0=ALU.max, op1=ALU.add), r=[tk], w=[tk])
                    if NC > 1:
                        if rev:
                            V(lambda e, g=g: e.tensor_copy(out=rows[:, 3, g * NC:(g + 1) * NC - 1], in_=rows[:, 2, g * NC + 1:(g + 1) * NC]), r=[tk], w=[tk])
                        else:
                            V(lambda e, g=g: e.tensor_copy(out=rows[:, 3, g * NC + 1:(g + 1) * NC], in_=rows[:, 2, g * NC:(g + 1) * NC - 1]), r=[tk], w=[tk])
                V(lambda e: e.tensor_max(out=rows[:, 4, 0:P], in0=rows[:, 3, 0:P], in1=rows[:, 0, 0:P]), r=[tk], w=[tk])
                V(lambda e: e.tensor_sub(out=rows[:, 5, 0:P], in0=rows[:, 3, 0:P], in1=rows[:, 4, 0:P]), r=[tk], w=[tk])
                A(lambda e: e.activation(out=rows[:, 5, 0:P], in_=rows[:, 5, 0:P], func=AF.Exp), r=[tk], w=[tk])
                V(lambda e: e.tensor_scalar_mul(out=rows[:, 6, 0:P], in0=rows[:, 4, 0:P], scalar1=-1.0), r=[tk], w=[tk])
                b3 = bank()
                PE(lambda e: e.matmul(ps[0:P, b3, 0:1], lhsT=rows[:, 6, 0:P], rhs=ones_f[0:1, 0:1], start=True, stop=True),
                   r=[tk, tok_c], w=[bank_tok[b3]])
                V(lambda e: e.tensor_copy(out=nmc_col[:], in_=ps[0:P, b3, 0:1]), r=[bank_tok[b3]], w=[tk])
                A(lambda e: e.activation(out=ww[:], in_=cc[:], func=AF.Exp, bias=nmc_col[:]), r=[tk], w=[tk])
                A(lambda e: e.activation(out=sst[:], in_=bb[:], func=AF.Exp, bias=nmc_col[:], scale=-1.0), r=[tk], w=[tk])
                b4 = bank()
                PE(lambda e: e.transpose(out=ps[:, b4, 0:P], in_=ww[:], identity=ident_f[0:P, 0:P]), r=[tk, tok_c], w=[bank_tok[b4]])
                V(lambda e: e.tensor_copy(out=WT[d][:, 0:P], in_=ps[:, b4, 0:P]), r=[bank_tok[b4]], w=[tk])
                b5 = bank()
                PE(lambda e: e.transpose(out=ps[:, b5, 0:P], in_=sst[:], identity=ident_f[0:P, 0:P]), r=[tk, tok_c], w=[bank_tok[b5]])
                V(lambda e: e.tensor_copy(out=STT_[d][:, 0:P], in_=ps[:, b5, 0:P]), r=[bank_tok[b5]], w=[tk])
                b6 = bank()
                PE(lambda e: e.matmul(ps[:, b6, 0:P], lhsT=ones_f[0:1, :], rhs=rows[:, 5, 0:P], start=True, stop=True),
                   r=[tk, tok_c], w=[bank_tok[b6]])
                V(lambda e: e.tensor_copy(out=SCB[d][:, 0:P], in_=ps[:, b6, 0:P]), r=[bank_tok[b6]], w=[tk])
            K.barrier()

    def phase_B(sv, l, gt_tiles):
        S = sv["S"]
        NT = S // 512
        NC = S // 128
        WT, STT_, SCB = gt_tiles
        with ExitStack() as st0:
            wrg = sb(st0, "wrg", [128, 16, 128], BF16)
            wrg_tok = Buf()
            K.dma(wrg[:], wrg_bf[l].rearrange("d t c i o -> i (d t c) o"), reads=[tok_wbf[("rg", l)]], writes=[wrg_tok])
            dgw = sb(st0, "dgw", [128, 16, 128], BF16)
            dgw_tok = Buf()
            for j in range(4):
                for c in range(4):
                    V(lambda e, j=j, c=c: e.tensor_scalar(out=dgw[:, j * 4 + c, :], in0=ident_f[:], scalar1=rgcw[:, l, j, c:c + 1], scalar2=None, op0=ALU.mult),
                      r=[tok_c], w=[dgw_tok])
            for d in (1, 0):
                rev = (d == 1)
                with ExitStack() as st:
                    if rev:
                        rgxw = [sb(st, "rgxw", [128, 4, 516], BF16) for _ in range(2)]
                        xc = [sb(st, "xc", [128, 4, 512], F32)] * 2
                    else:
                        xc = [sb(st, "xc", [128, 4, 512], F32) for _ in range(2)]
                        hbl = sb(st, "hbl", [128, 4, 512], F32)
                        ggl = sb(st, "ggl", [128, 4, 512], BF16)
                        ogl = sb(st, "ogl", [128, 4, 512], BF16)
                        hbm = sb(st, "hbm", [128, 4, 512], F32)
                        mix = sb(st, "mix", [128, 8, 512], BF16)
                        sqv = sb(st, "sqv", [128, 4, 512], F32)
                        yb = sb(st, "yb", [128, 4, 512], BF16)
                        ssum = sb(st, "ssum", [128, 16], F32)
                    qTl = [sb(st, "qTl", [128, 4, 512], BF16) for _ in range(2)]
                    kTl = [sb(st, "kTl", [128, 4, 512], BF16) for _ in range(2)]
                    val = [sb(st, "val", [128, 4, 516], BF16) for _ in range(2)]
                    ktl = [sb(st, "ktl", [128, 4, 512], BF16) for _ in range(2)]
                    xcb = sb(st, "xcb", [128, 4, 512], BF16)
                    ta = sb(st, "ta", [128, 4, 512], F32)
                    tx = sb(st, "tx", [128, 4, 512], F32)
                    aa = sb(st, "aa", [128, 4, 512], F32)
                    a2 = sb(st, "a2", [128, 4, 512], F32)
                    hh = sb(st, "hh", [128, 4, 512], F32)
                    carry = sb(st, "carry", [128, 4], F32)
                    PT = sb(st, "PT", [128, 4, 512], BF16)
                    STb = sb(st, "STb", [128, 4, 512], BF16)
                    vw = sb(st, "vw", [128, 4, 516], BF16)
                    Cst = sb(st, "Cst", [128, 516], F32)
                    Cb = sb(st, "Cb", [128, 4, 516], BF16)
                    hd = sb(st, "hd", [128, 4, 512], F32)
                    dd = sb(st, "dd", [128, 4, 4], F32)
                    names = ("rgxw", "xc", "qT", "kT", "val", "ktl")
                    tkp = [{n: toks(4) for n in names} for _ in range(2)]
                    if rev:
                        tkp[1]["xc"] = tkp[0]["xc"]
                    tk = {n: toks(4) for n in ("xcb", "ta", "tx", "aa", "a2", "hh", "hbl", "ggl", "ogl", "hbm", "hd", "PT", "vw", "Cb", "dd", "STb")}
                    mixtok = toks(8)
                    t_carry, t_C, t_sqv, t_yb, t_ss = (Buf() for _ in range(5))
                    V(lambda e: e.memset(carry[:], 0.0), w=[t_carry])
                    V(lambda e: e.memset(Cst[:], 0.0), w=[t_C])
                    order = list(range(NT - 1, -1, -1) if rev else range(NT))

                    def loads_early(i, p):
                        t0 = i * 512
                        tsl = slice(t0, t0 + 512)
                        if rev:
                            lo_c = 2 if i == 0 else 0
                            hi_c = 514 if i == NT - 1 else 515
                            if i == 0:
                                V(lambda e, p=p: e.memset(rgxw[p][:, :, 0:2], 0.0), w=[tkp[p]["rgxw"]])
                            if i == NT - 1:
                                V(lambda e, p=p: e.memset(rgxw[p][:, :, 514:515], 0.0), w=[tkp[p]["rgxw"]])
                            K.dma(rgxw[p][:, :, lo_c:hi_c], sv["rgx_pad"][:, t0 + lo_c:t0 + hi_c].rearrange("(c p) t -> p c t", p=128), writes=[tkp[p]["rgxw"]])
                        else:
                            K.dma(xc[p][:], xc_d[:, tsl].rearrange("(c p) t -> p c t", p=128), writes=[tkp[p]["xc"]])
                        K.dma(kTl[p][:], sv["kT_d"][:, tsl].rearrange("(c p) t -> p c t", p=128), writes=[tkp[p]["kT"]])
                        K.dma(qTl[p][:], sv["qT_d"][:, tsl].rearrange("(c p) t -> p c t", p=128), writes=[tkp[p]["qT"]])
                        K.dma(val[p][:], sv["va_d"][tsl, :].rearrange("(s p) c -> p s c", p=128), writes=[tkp[p]["val"]])
                        K.dma(ktl[p][:], sv["ktm_d"][tsl, :].rearrange("(s p) c -> p s c", p=128), writes=[tkp[p]["ktl"]])

                    loads_early(order[0], 0)
                    for it, i in enumerate(order):
                        p = it % 2
                        T = tkp[p]
                        t0 = i * 512
                        tsl = slice(t0, t0 + 512)
                        if it + 1 < NT:
                            loads_early(order[it + 1], 1 - p)
                        if not rev:
                            K.dma(hbl[:], hbrg_d[:, tsl].rearrange("(c p) t -> p c t", p=128), writes=[tk["hbl"]])
                            K.dma(hbm[:], hbml_d[tsl, :].rearrange("(s p) c -> p s c", p=128), writes=[tk["hbm"]])
                            K.dma(ggl[:], sv["gg_d"][:, tsl].rearrange("(c p) t -> p c t", p=128), writes=[tk["ggl"]])
                            K.dma(ogl[:], sv["og_d"][tsl, :].rearrange("(s p) c -> p s c", p=128), writes=[tk["ogl"]])
                        xcp = xc[p]
                        if rev:
                            for c in range(4):
                                bcv = bank()
                                for j in range(4):
                                    PE(lambda e, c=c, j=j, bcv=bcv: e.matmul(ps[:, bcv, :], lhsT=dgw[:, j * 4 + c, :], rhs=rgxw[p][:, c, j:j + 512], start=(j == 0), stop=(j == 3)),
                                       r=[dgw_tok, T["rgxw"][c]], w=[bank_tok[bcv]], inc=(j == 3))
                                A(lambda e, c=c, bcv=bcv: e.activation(out=xcp[:, c, :], in_=ps[:, bcv, :], func=AF.Identity, bias=rgcb[:, l, c:c + 1]),
                                  r=[bank_tok[bcv], tok_c], w=[T["xc"][c]])
                        for c in range(4):
                            V(lambda e, c=c: e.tensor_copy(out=xcb[:, c, :], in_=xcp[:, c, :]), r=[T["xc"][c]], w=[tk["xcb"][c]])
                        mk = maskb if rev else maskf
                        def emit_s12(sub):
                            j = i * 4 + sub
                            cs = slice(sub * 128, (sub + 1) * 128)
                            bst = bank()
                            for h in range(4):
                                PE(lambda e, h=h, bst=bst, cs=cs: e.matmul(ps[:, bst, h * 128:(h + 1) * 128], lhsT=kTl[p][:, h, cs], rhs=qTl[p][:, h, cs], start=True, stop=True),
                                   r=[T["kT"][h], T["qT"][h]], w=[bank_tok[bst]], inc=(h == 3))
                            V(lambda e, bst=bst, sub=sub: e.tensor_tensor(out=PT[:, sub, :].rearrange("p (h t) -> p h t", h=4),
                                                                          in0=ps[:, bst, :].rearrange("p (h t) -> p h t", h=4),
                                                                          in1=mk[:].unsqueeze(1).to_broadcast([128, 4, 128]), op=ALU.mult),
                              r=[bank_tok[bst], tok_c], w=[tk["PT"][sub]])
                            wcols = WT[d][:, j:j + 3 * NC + 1:NC]
                            G(lambda e, sub=sub, wcols=wcols: e.tensor_tensor(out=vw[:, sub, :].rearrange("p (h c) -> p h c", h=4),
                                                                             in0=val[p][:, sub, :].rearrange("p (h c) -> p h c", h=4),
                                                                             in1=wcols.unsqueeze(2).to_broadcast([128, 4, 129]), op=ALU.mult),
                              r=[T["val"][sub]], w=[tk["vw"][sub]])
                            cb_ = bank2()
                            for h in range(4):
                                o_ap = ps[:, cb_[h // 2], (h % 2) * 129:(h % 2) * 129 + 129]
                                PE(lambda e, h=h, o_ap=o_ap, sub=sub: e.matmul(o_ap, lhsT=ktl[p][:, sub, h * 128:(h + 1) * 128], rhs=vw[:, sub, h * 129:(h + 1) * 129], start=True, stop=True),
                                   r=[T["ktl"][sub], tk["vw"][sub]], w=[bank_tok[cb_[h // 2]]], inc=(h % 2 == 1))
                            sccols = SCB[d][:, j:j + 3 * NC + 1:NC]
                            V(lambda e, sccols=sccols: e.tensor_tensor(out=Cst[:].rearrange("p (h c) -> p h c", h=4),
                                                                      in0=Cst[:].rearrange("p (h c) -> p h c", h=4),
                                                                      in1=sccols.unsqueeze(2).to_broadcast([128, 4, 129]), op=ALU.mult),
                              r=[t_C], w=[t_C])
                            V(lambda e, sub=sub: e.tensor_copy(out=Cb[:, sub, :], in_=Cst[:]), r=[t_C], w=[tk["Cb"][sub]])
                            V(lambda e, cb_=cb_: e.tensor_add(out=Cst[:].rearrange("p (b c) -> p b c", b=2), in0=Cst[:].rearrange("p (b c) -> p b c", b=2),
                                                              in1=ps[:, cb_[0]:cb_[0] + 2, 0:258]),
                              r=[t_C, bank_tok[cb_[0]], bank_tok[cb_[1]]], w=[t_C])
                        def emit_rg34(half):
                            bks = {}
                            for c in (2 * half, 2 * half + 1):
                                for ty in range(2):
                                    bb_ = bank()
                                    bks[(c, ty)] = bb_
                                    PE(lambda e, c=c, ty=ty, bb_=bb_: e.matmul(ps[:, bb_, :], lhsT=wrg[:, (d * 2 + ty) * 4 + c, :], rhs=xcb[:, c, :], start=True, stop=True),
                                       r=[wrg_tok, tk["xcb"][c]], w=[bank_tok[bb_]])
                            for c in (2 * half, 2 * half + 1):
                                col = (l * 2 + d) * 4 + c
                                A(lambda e, c=c, col=col, bb_=bks[(c, 0)]: e.activation(out=ta[:, c, :], in_=ps[:, bb_, :], func=AF.Tanh, scale=0.5, bias=hba[:, col:col + 1]),
                                  r=[bank_tok[bks[(c, 0)]], tok_c], w=[tk["ta"][c]])
                                A(lambda e, c=c, col=col, bb_=bks[(c, 1)]: e.activation(out=tx[:, c, :], in_=ps[:, bb_, :], func=AF.Tanh, scale=0.5, bias=hbx[:, col:col + 1]),
                                  r=[bank_tok[bks[(c, 1)]], tok_c], w=[tk["tx"][c]])
                            for c in (2 * half, 2 * half + 1):
                                col = (l * 2 + d) * 4 + c
                                A(lambda e, c=c, col=col: e.activation(out=aa[:, c, :], in_=ta[:, c, :], func=AF.Exp, scale=Kh[:, col:col + 1], bias=Kh[:, col:col + 1]),
                                  r=[tk["ta"][c], tok_c], w=[tk["aa"][c]])
                                A(lambda e, c=c, col=col: e.activation(out=a2[:, c, :], in_=ta[:, c, :], func=AF.Exp, scale=K2[:, col:col + 1], bias=K2[:, col:col + 1]),
                                  r=[tk["ta"][c], tok_c], w=[tk["a2"][c]])
                        subs = list(range(3, -1, -1) if rev else range(4))
                        emit_s12(subs[0])
                        emit_rg34(0)
                        emit_s12(subs[1])
                        emit_rg34(1)
                        emit_s12(subs[2])
                        emit_s12(subs[3])
                        for sub in subs:
                            j = i * 4 + sub
                            cs = slice(sub * 128, (sub + 1) * 128)
                            nb = bank2()
                            for h in range(4):
                                o_ap = ps[:, nb[h // 2], (h % 2) * 129:(h % 2) * 129 + 129]
                                PE(lambda e, h=h, o_ap=o_ap, sub=sub: e.matmul(o_ap, lhsT=PT[:, sub, h * 128:(h + 1) * 128], rhs=vw[:, sub, h * 129:(h + 1) * 129], start=True, stop=False),
                                   r=[tk["PT"][sub], tk["vw"][sub]], w=[bank_tok[nb[h // 2]]], inc=False)
                                PE(lambda e, h=h, o_ap=o_ap, sub=sub, cs=cs: e.matmul(o_ap, lhsT=qTl[p][:, h, cs], rhs=Cb[:, sub, h * 129:(h + 1) * 129], start=False, stop=True),
                                   r=[T["qT"][h], tk["Cb"][sub]], w=[bank_tok[nb[h // 2]]], inc=(h % 2 == 1))
                            stc = STT_[d][:, j:j + 3 * NC + 1:NC]
                            den_ap = ps[:, nb[0]:nb[0] + 2, 128:258:129]
                            nbt = [bank_tok[nb[0]], bank_tok[nb[1]]]
                            V(lambda e, sub=sub, den_ap=den_ap, stc=stc: e.tensor_tensor(out=dd[:, sub, :].rearrange("p (b h) -> p b h", b=2), in0=den_ap,
                                                                                      in1=stc.rearrange("p (b h) -> p b h", b=2), op=ALU.max),
                              r=nbt, w=[tk["dd"][sub]])
                            V(lambda e, sub=sub, den_ap=den_ap: e.scalar_tensor_tensor(out=dd[:, sub, :].rearrange("p (b h) -> p b h", b=2), in0=den_ap, scalar=-1.0,
                                                                                    in1=dd[:, sub, :].rearrange("p (b h) -> p b h", b=2), op0=ALU.mult, op1=ALU.max),
                              r=nbt + [tk["dd"][sub]], w=[tk["dd"][sub]])
                            V(lambda e, sub=sub: e.reciprocal(out=dd[:, sub, :], in_=dd[:, sub, :]), r=[tk["dd"][sub]], w=[tk["dd"][sub]])
                            for bi in range(2):
                                V(lambda e, bi=bi, sub=sub, nb=nb: e.tensor_tensor(
                                    out=hd[:, sub, bi * 256:(bi + 1) * 256].rearrange("p (h c) -> p h c", h=2),
                                    in0=ps[:, nb[bi], 0:258].rearrange("p (h c) -> p h c", h=2)[:, :, 0:128],
                                    in1=dd[:, sub, 2 * bi:2 * bi + 2].unsqueeze(2).to_broadcast([128, 2, 128]), op=ALU.mult),
                                  r=[bank_tok[nb[bi]], tk["dd"][sub]], w=[tk["hd"][sub]])
                        for c in range(4):
                            A(lambda e, c=c: e.activation(out=a2[:, c, :], in_=a2[:, c, :], func=AF.Sqrt, scale=-1.0, bias=1.0 + 1e-6),
                              r=[tk["a2"][c]], w=[tk["a2"][c]])
                        for c in range(4):
                            if rev:
                                G(lambda e, c=c: e.tensor_scalar(out=tx[:, c, :], in0=tx[:, c, :], scalar1=1.0, scalar2=1.0, op0=ALU.add, op1=ALU.mult), r=[tk["tx"][c]], w=[tk["tx"][c]])
                                G(lambda e, c=c: e.tensor_mul(out=tx[:, c, :], in0=tx[:, c, :], in1=xcp[:, c, :]), r=[tk["tx"][c], T["xc"][c]], w=[tk["tx"][c]])
                            else:
                                V(lambda e, c=c: e.scalar_tensor_tensor(out=tx[:, c, :], in0=tx[:, c, :], scalar=1.0, in1=xcp[:, c, :],
                                                                        op0=ALU.add, op1=ALU.mult),
                                  r=[tk["tx"][c], T["xc"][c]], w=[tk["tx"][c]])
                        for c in range(4):
                            V(lambda e, c=c: e.scalar_tensor_tensor(out=tx[:, c, :], in0=a2[:, c, :], scalar=0.5, in1=tx[:, c, :],
                                                                    op0=ALU.mult, op1=ALU.mult),
                              r=[tk["a2"][c], tk["tx"][c]], w=[tk["tx"][c]])
                        for c in range(4):
                            if rev:
                                V(lambda e, c=c: e.tensor_tensor_scan(out=hh[:, c, ::-1], data0=aa[:, c, ::-1], data1=tx[:, c, ::-1],
                                                                      initial=carry[:, c:c + 1], op0=ALU.mult, op1=ALU.add),
                                  r=[tk["aa"][c], tk["tx"][c], t_carry], w=[tk["hh"][c]])
                            else:
                                V(lambda e, c=c: e.tensor_tensor_scan(out=hh[:, c, :], data0=aa[:, c, :], data1=tx[:, c, :],
                                                                      initial=carry[:, c:c + 1], op0=ALU.mult, op1=ALU.add),
                                  r=[tk["aa"][c], tk["tx"][c], t_carry], w=[tk["hh"][c]])
                        last = 0 if rev else 511
                        V(lambda e: e.tensor_copy(out=carry[:, :], in_=hh[:, :, last]), r=[tk["hh"]], w=[t_carry])
                        if rev:
                            K.dma(xc_d[:, tsl].rearrange("(c p) t -> p c t", p=128), xcp[:], reads=[T["xc"]])
                            K.dma(hbrg_d[:, tsl].rearrange("(c p) t -> p c t", p=128), hh[:], reads=[tk["hh"]])
                            K.dma(hbml_d[tsl, :].rearrange("(s p) c -> p s c", p=128), hd[:], reads=[tk["hd"]])
                        else:
                            for c in range(4):
                                G(lambda e, c=c: e.tensor_add(out=hh[:, c, :], in0=hh[:, c, :], in1=hbl[:, c, :]),
                                  r=[tk["hh"][c], tk["hbl"][c]], w=[tk["hh"][c]])
                            for c in range(4):
                                G(lambda e, c=c: e.tensor_mul(out=mix[:, c, :], in0=hh[:, c, :], in1=ggl[:, c, :]),
                                  r=[tk["hh"][c], tk["ggl"][c]], w=[mixtok[c]])
                            G(lambda e: e.tensor_add(out=hd[:], in0=hd[:], in1=hbm[:]), r=[tk["hd"], tk["hbm"]], w=[tk["hd"]])
                            G(lambda e: e.tensor_mul(out=sqv[:], in0=hd[:], in1=hd[:]), r=[tk["hd"]], w=[t_sqv])
                            V(lambda e: e.reduce_sum(out=ssum[:], in_=sqv[:].rearrange("p s (h c) -> p (s h) c", h=4), axis=AX.X),
                              r=[t_sqv], w=[t_ss])
                            A(lambda e: e.activation(out=ssum[:], in_=ssum[:], func=AF.Sqrt, scale=1.0 / 128, bias=epsc[:, 0:1]), r=[t_ss, tok_c], w=[t_ss])
                            V(lambda e: e.reciprocal(out=ssum[:], in_=ssum[:]), r=[t_ss], w=[t_ss])
                            V(lambda e: e.tensor_tensor(out=sqv[:].rearrange("p s (h c) -> p (s h) c", h=4),
                                                        in0=hd[:].rearrange("p s (h c) -> p (s h) c", h=4),
                                                        in1=ssum[:].unsqueeze(2).to_broadcast([128, 16, 128]), op=ALU.mult),
                              r=[tk["hd"], t_ss, t_sqv], w=[t_sqv])
                            G(lambda e: e.tensor_mul(out=yb[:], in0=sqv[:], in1=ogl[:]), r=[t_sqv, tk["ogl"]], w=[t_yb])
                            for half in range(2):
                                bt = bank()
                                pbf = ps[:, bt, :].bitcast(BF16)
                                for q in range(8):
                                    sub, h = half * 2 + q // 4, q % 4
                                    PE(lambda e, q=q, sub=sub, h=h, pbf=pbf: e.transpose(out=pbf[:, q * 128:(q + 1) * 128], in_=yb[:, sub, h * 128:(h + 1) * 128], identity=ident_bf[:]),
                                       r=[t_yb, tok_c], w=[bank_tok[bt]], inc=(q == 7))
                                for s2 in range(2):
                                    sub = half * 2 + s2
                                    A(lambda e, pbf=pbf, sub=sub, s2=s2: e.copy(out=mix[:, 4:8, sub * 128:(sub + 1) * 128],
                                                                            in_=pbf[:, s2 * 512:(s2 + 1) * 512].rearrange("p (h t) -> p h t", h=4)),
                                      r=[bank_tok[bt]], w=[mixtok[4:8]])
                            K.dma(sv["mixT_d"][:, tsl].rearrange("(k p) t -> p k t", p=128), mix[:], reads=[mixtok])
                    K.barrier()

    def phase_C(sq_list, l, final):
        with ExitStack() as st:
            wo = sb(st, "wo", [128, KT, D], BF16)
            wu = sb(st, "wu", [128, KT, 2 * DFF], BF16)
            wd = [sb(st, "wd", [128, 24, 128], BF16) for _ in range(2)]
            wo_tok, wu_tok = toks(KT), toks(KT)
            wd_tok = [Buf() for _ in range(2)]
            for kt in range(KT):
                K.dma(wo[:, kt, :], w_out_bf[l, kt * 128:(kt + 1) * 128, :], reads=[tok_wbf[("out", l)]], writes=[wo_tok[kt]])
            for kt in range(KT):
                K.dma(wu[:, kt, :], w_up_bf[l, kt * 128:(kt + 1) * 128, :], reads=[tok_wbf[("up", l)]], writes=[wu_tok[kt]])
            xt = sb(st, "xt", [128, KT, 512], F32)
            xtok = toks(KT)
            gbuf = sb(st, "gbuf", [128, 24, 512], BF16)
            gtok = toks(24)
            hT = sb(st, "hTc", [128, KT, 512], BF16)
            htok = toks(KT)
            rstd = sb(st, "rstdc", [128, 512], F32)
            tmp = sb(st, "tmpc", [128, 512], F32)
            rstd_tok, tmp_tok = Buf(), Buf()
            accg = [sb(st, "accg", [128, 512], F32) for _ in range(2)]
            accv = [sb(st, "accv", [128, 512], F32) for _ in range(2)]
            acc_tok = [[Buf(), Buf(), Buf()] for _ in range(2)]
            wdc = [0]
            all_tiles = [(q, t0, ln) for q in sq_list for (t0, ln) in seq_tiles(q["S"], 510)]
            for (sv, t0, ln) in all_tiles:
                S = sv["S"]
                xsrc = sv["xin"] if l == 0 else sv["xmid_d"]
                xdst = sv["yout"] if final else sv["xmid_d"]
                mixT_d = sv["mixT_d"]
                W = ln + 2
                lo = t0 - 1
                c0 = 1 if lo < 0 else 0
                c1 = W - 1 if lo + W > S else W
                if c0 == 1:
                    V(lambda e: e.memset(xt[:, :, 0:1], 0.0), w=[xtok])
                    V(lambda e: e.memset(gbuf[:, 0:8, 0:1], 0.0), w=[gtok[0:8]])
                if c1 == W - 1:
                    V(lambda e: e.memset(xt[:, :, W - 1:W], 0.0), w=[xtok])
                    V(lambda e: e.memset(gbuf[:, 0:8, W - 1:W], 0.0), w=[gtok[0:8]])
                for h in range(2):
                    K.dma(gbuf[:, h * 4:(h + 1) * 4, c0:c1],
                          mixT_d[h * 512:(h + 1) * 512, lo + c0:lo + c1].rearrange("(k p) t -> p k t", p=128),
                          writes=[gtok[h * 4:(h + 1) * 4]])
                for kt in range(KT):
                    K.dma(xt[:, kt, c0:c1], xsrc[kt * 128:(kt + 1) * 128, lo + c0:lo + c1], writes=[xtok[kt]])
                for jd in range(KT):
                    b = bank()
                    mmgroup(ps[:, b, 0:W], [(wo[:, kt, jd * 128:(jd + 1) * 128], gbuf[:, kt, 0:W]) for kt in range(KT)],
                            [[wo_tok[kt], gtok[kt]] for kt in range(KT)], bank_tok[b])
                    V(lambda e, b=b, jd=jd: e.tensor_add(out=xt[:, jd, 0:W], in0=xt[:, jd, 0:W], in1=ps[:, b, 0:W]),
                      r=[bank_tok[b], xtok[jd]], w=[xtok[jd]])
                rmsnorm(xt, xtok, W, g2c[:, l, :], hT, htok, gbuf[:, 8:16, :], gtok[8:16], rstd, rstd_tok, tmp, tmp_tok)
                for c in range(24):
                    pp = c % 2
                    bg_ = bank()
                    mmgroup(ps[:, bg_, 0:W], [(wu[:, kt, c * 128:(c + 1) * 128], hT[:, kt, 0:W]) for kt in range(KT)],
                            [[wu_tok[kt], htok[kt]] for kt in range(KT)], bank_tok[bg_])
                    bv_ = bank()
                    mmgroup(ps[:, bv_, 0:W], [(wu[:, kt, DFF + c * 128:DFF + (c + 1) * 128], hT[:, kt, 0:W]) for kt in range(KT)],
                            [[wu_tok[kt], htok[kt]] for kt in range(KT)], bank_tok[bv_])
                    for (bb_, acc, ti, cc_) in ((bg_, accg[pp], 0, c), (bv_, accv[pp], 1, 24 + c)):
                        A(lambda e, bb_=bb_, acc=acc, cc_=cc_: e.activation(out=acc[:, 1:W - 1], in_=ps[:, bb_, 1:W - 1], func=AF.Identity,
                                                                           scale=ffcw[:, l, 1, cc_:cc_ + 1], bias=ffcb[:, l, cc_:cc_ + 1]),
                          r=[bank_tok[bb_], tok_c], w=[acc_tok[pp][ti]])
                        V(lambda e, bb_=bb_, acc=acc, cc_=cc_: e.scalar_tensor_tensor(out=acc[:, 1:W - 1], in0=ps[:, bb_, 0:W - 2],
                                                                                     scalar=ffcw[:, l, 0, cc_:cc_ + 1], in1=acc[:, 1:W - 1],
                                                                                     op0=ALU.mult, op1=ALU.add),
                          r=[bank_tok[bb_], acc_tok[pp][ti], tok_c], w=[acc_tok[pp][ti]])
                        V(lambda e, bb_=bb_, acc=acc, cc_=cc_: e.scalar_tensor_tensor(out=acc[:, 1:W - 1], in0=ps[:, bb_, 2:W],
                                                                                     scalar=ffcw[:, l, 2, cc_:cc_ + 1], in1=acc[:, 1:W - 1],
                                                                                     op0=ALU.mult, op1=ALU.add),
                          r=[bank_tok[bb_], acc_tok[pp][ti], tok_c], w=[acc_tok[pp][ti]])
                    A(lambda e, pp=pp: e.activation(out=accg[pp][:, 1:W - 1], in_=accg[pp][:, 1:W - 1], func=AF.Gelu_apprx_tanh),
                      r=[acc_tok[pp][0]], w=[acc_tok[pp][0]])
                    G(lambda e, pp=pp, c=c: e.tensor_mul(out=gbuf[:, c, 1:W - 1], in0=accg[pp][:, 1:W - 1], in1=accv[pp][:, 1:W - 1]),
                      r=[acc_tok[pp][0], acc_tok[pp][1]], w=[gtok[c]])
                for jd in range(KT):
                    wi = wdc[0] % 2
                    wdc[0] += 1
                    K.dma(wd[wi][:], w_down_bf[l, jd], reads=[tok_wbf[("down", l)]], writes=[wd_tok[wi]])
                    b = bank()
                    mmgroup(ps[:, b, 0:ln], [(wd[wi][:, c, :], gbuf[:, c, 1:W - 1]) for c in range(24)],
                            [[wd_tok[wi], gtok[c]] for c in range(24)], bank_tok[b])
                    V(lambda e, b=b, jd=jd: e.tensor_add(out=xt[:, jd, 1:W - 1], in0=xt[:, jd, 1:W - 1], in1=ps[:, b, 0:ln]),
                      r=[bank_tok[b], xtok[jd]], w=[xtok[jd]])
                    if not final:
                        K.dma(xdst[jd * 128:(jd + 1) * 128, t0:t0 + ln], xt[:, jd, 1:W - 1], reads=[xtok[jd]], q="scalar")
                if final:
                    class _Sh:
                        def __init__(s, t): s.t = t
                        def __getitem__(s, idx):
                            a, b_, c_ = idx
                            return s.t[a, b_, slice(c_.start + 1, c_.stop + 1)]
                    rmsnorm(_Sh(xt), xtok, ln, gfc, _Sh(xt), xtok, gbuf[:, 8:16, :], gtok[8:16], rstd, rstd_tok, tmp, tmp_tok)
                    K.dma(xdst[:, t0:t0 + ln].rearrange("(k p) t -> p k t", p=128), xt[:, :, 1:W - 1], reads=[xtok])
            K.barrier()

    zf = sb(cst, "zf", [128, 4], F32)
    V(lambda e: e.memset(zf[:], 0.0), w=cw)
    K.barrier()

    for l in range(NLAYER):
        phase_A(seqs, l)
        for q in seqs:
            NC = q["S"] // 128
            with ExitStack() as gs:
                WT = [sb(gs, "WT", [128, 4 * NC], F32) for _ in range(2)]
                STT_ = [sb(gs, "STT", [128, 4 * NC], F32) for _ in range(2)]
                SCB = [sb(gs, "SCB", [128, 4 * NC], F32) for _ in range(2)]
                phase_G(q, (WT, STT_, SCB))
                phase_B(q, l, (WT, STT_, SCB))
        phase_C(seqs, l, final=(l == NLAYER - 1))
    K.barrier(include_cast=True)
    es.close()
    return nc, K


def build(groups, debug=False):
    _, K1 = _build(groups, debug, None)
    nc, K2 = _build(groups, debug, K1.rec)
    build.n_ins = K2.n_ins
    build.n_inc = (getattr(K1, "n_inc", 0), getattr(K2, "n_inc", 0))
    return nc


_GROUPS = [("p", 2, 4096), ("s", 4, 2048)]
_WNAMES = ["norm1_g", "w_in", "b_gates", "rg_conv_w", "rg_conv_b", "rg_wa", "rg_ba", "rg_wx", "rg_bx", "rg_lambda",
           "ml_norm_g", "w_out", "norm2_g", "w_up", "ffn_conv_w", "ffn_conv_b", "w_down", "final_g"]


def host_consts():
    s = np.arange(128)[:, None]
    t = np.arange(128)[None, :]
    return {"c_ident": np.eye(128, dtype=np.float32),
            "c_maskf": (s <= t).astype(np.float32),
            "c_maskb": (s >= t).astype(np.float32)}


def weight_map(inputs):
    m = {k: np.asarray(inputs[k], dtype=np.float32) for k in _WNAMES}
    L = NLAYER
    m["b_gates"] = m["b_gates"].reshape(L, 16, 1)
    m["norm1_g"] = m["norm1_g"].reshape(L, 8, 128).transpose(0, 2, 1)
    m["norm2_g"] = m["norm2_g"].reshape(L, 8, 128).transpose(0, 2, 1)
    m["final_g"] = m["final_g"].reshape(8, 128).transpose(1, 0)
    m["rg_conv_w"] = m["rg_conv_w"].reshape(L, 4, 4, 128).transpose(0, 3, 1, 2)
    m["rg_conv_b"] = m["rg_conv_b"].reshape(L, 4, 128).transpose(0, 2, 1)
    for k in ("rg_ba", "rg_bx", "rg_lambda"):
        m[k] = m[k].reshape(L, 2, 4, 128).transpose(3, 0, 1, 2).reshape(128, 16)
    m["ffn_conv_w"] = m["ffn_conv_w"].reshape(L, 3, 48, 128).transpose(0, 3, 1, 2)
    m["ffn_conv_b"] = m["ffn_conv_b"].reshape(L, 48, 128).transpose(0, 2, 1)
    m = {k: np.ascontiguousarray(v, dtype=np.float32) for k, v in m.items()}
    m.update(host_consts())
    return m


def kernel(**inputs):
    xp = np.asarray(inputs["x_prompt"], dtype=np.float32)
    xs = np.asarray(inputs["x_sample"], dtype=np.float32)
    n = 8
    nc = build(_GROUPS)
    wm = weight_map(inputs)
    in_maps = []
    for c in range(n):
        m = dict(wm)
        m["x_p"] = np.ascontiguousarray(xp[2 * c:2 * c + 2].transpose(0, 2, 1))
        m["x_s"] = np.ascontiguousarray(xs[4 * c:4 * c + 4].transpose(0, 2, 1))
        in_maps.append(m)
    res = run_bass_kernel_spmd(nc, in_maps, core_ids=list(range(n)))
    yp = np.concatenate([np.asarray(r["y_p"]).transpose(0, 2, 1) for r in res.results], axis=0)
    ys = np.concatenate([np.asarray(r["y_s"]).transpose(0, 2, 1) for r in res.results], axis=0)
    return (np.ascontiguousarray(yp, dtype=np.float32), np.ascontiguousarray(ys, dtype=np.float32))
```
